# Optimizing a Trainium2 kernel written in Bass

```python
import jax, jax.numpy as jnp
from jax import lax
import numpy as np

D_MODEL = 1024
BATCH = 2
SEQ = 8192
DEPTH = 4
DEC_BATCH = 128
DEC_SEQ = 1
PAST_LEN = 8192
PAGE_SIZE = 128

EPS = 1e-6
N_MIXERS = 4
N_RET = len(range(0, DEPTH, N_MIXERS))
N_SWA = len(range(1, DEPTH, N_MIXERS))
N_CONV = len(range(2, DEPTH, N_MIXERS))
N_POOL = len(range(3, DEPTH, N_MIXERS))
RET_HEADS = 4
RET_DK = D_MODEL // RET_HEADS
RET_DV = 2 * D_MODEL // RET_HEADS
RET_CHUNK = 128
RET_THETA = 10000.0
SWA_HEADS = 16
SWA_KV_HEADS = 4
SWA_HEAD_DIM = D_MODEL // SWA_HEADS
SWA_GROUP = SWA_HEADS // SWA_KV_HEADS
WINDOW = 128
SWA_BLOCK = WINDOW
ROPE_THETA = 500000.0
ROPE_DIM = SWA_HEAD_DIM // 4
CONV_WIDTH = 3
POOL_WINDOWS = (2, 4, 8, 16)
POOL_GROUPS = len(POOL_WINDOWS)
POOL_GROUP_DIM = D_MODEL // POOL_GROUPS
POOL_BUF = max(POOL_WINDOWS) - 1
N_MEM = 256
XATTN_HEADS = 4
XATTN_HEAD_DIM = D_MODEL // XATTN_HEADS
D_FF = ((8 * D_MODEL + 3 * 256 - 1) // (3 * 256)) * 256
WINDOW_BUF = min(WINDOW, PAST_LEN)

kernel_name = "hybrid_retention_swa_conv_pool_step"


def rmsnorm(x, g):
    xf = x.astype(jnp.float32)
    y = xf * lax.rsqrt(jnp.mean(xf * xf, axis=-1, keepdims=True) + EPS)
    return (y * g.astype(jnp.float32)).astype(x.dtype)


def rope(x, pos, rot_dim, theta):
    half = rot_dim // 2
    inv = 1.0 / (theta ** (jnp.arange(half, dtype=jnp.float32) / half))
    ang = pos.astype(jnp.float32)[:, None] * inv[None, :]
    cos = jnp.cos(ang)[None, :, None, :]
    sin = jnp.sin(ang)[None, :, None, :]
    xf = x.astype(jnp.float32)
    x1 = xf[..., :half]
    x2 = xf[..., half:rot_dim]
    out = jnp.concatenate([x1 * cos - x2 * sin, x2 * cos + x1 * sin, xf[..., rot_dim:]], axis=-1)
    return out.astype(x.dtype)


def retention_block(q, k, v, S, log_g):
    C = q.shape[2]
    idx = jnp.arange(C, dtype=jnp.float32)
    diff = idx[:, None] - idx[None, :]
    decay = jnp.where(diff >= 0, jnp.exp(log_g[:, None, None] * jnp.where(diff >= 0, diff, 0.0)), 0.0)
    inner = jnp.einsum('bhld,bhmd->bhlm', q, k) * decay[None]
    q_dec = q * jnp.exp(log_g[:, None] * (idx + 1.0)[None, :])[None, :, :, None]
    o = jnp.einsum('bhlm,bhmv->bhlv', inner, v) + jnp.einsum('bhld,bhdv->bhlv', q_dec, S)
    k_dec = k * jnp.exp(log_g[:, None] * (C - 1.0 - idx)[None, :])[None, :, :, None]
    S_new = jnp.exp(log_g * C)[None, :, None, None] * S + jnp.einsum('bhld,bhlv->bhdv', k_dec, v)
    return o, S_new


def retention_mix(h, pos, S0, w_in, w_out, chunk):
    B, L, _ = h.shape
    q, k, v, g = jnp.split(h @ w_in, [D_MODEL, 2 * D_MODEL, 4 * D_MODEL], axis=-1)
    q = rope(q.reshape(B, L, RET_HEADS, RET_DK), pos, RET_DK, RET_THETA)
    k = rope(k.reshape(B, L, RET_HEADS, RET_DK), pos, RET_DK, RET_THETA) * (RET_DK ** -0.5)
    v = v.reshape(B, L, RET_HEADS, RET_DV)
    log_g = jnp.log1p(-jnp.exp2(-5.0 - jnp.arange(RET_HEADS, dtype=jnp.float32)))
    nc = L // chunk

    def to_chunks(t):
        return t.astype(jnp.float32).reshape(B, nc, chunk, RET_HEADS, -1).transpose(1, 0, 3, 2, 4)

    def step(S, qkv):
        o, S = retention_block(qkv[0], qkv[1], qkv[2], S, log_g)
        return S, o

    S_fin, o = lax.scan(step, S0.astype(jnp.float32), (to_chunks(q), to_chunks(k), to_chunks(v)))
    o = o.transpose(1, 0, 3, 2, 4).reshape(B, L, RET_HEADS, RET_DV)
    mu = jnp.mean(o, axis=-1, keepdims=True)
    var = jnp.mean(jnp.square(o - mu), axis=-1, keepdims=True)
    o = ((o - mu) * lax.rsqrt(var + EPS)).reshape(B, L, 2 * D_MODEL).astype(h.dtype)
    y = (jax.nn.silu(g) * o) @ w_out
    return y, S_fin.astype(h.dtype)


def sink_attend(q, k, v, valid, sinks):
    Z, T = q.shape[:2]
    qg = q.reshape(Z, T, SWA_KV_HEADS, SWA_GROUP, SWA_HEAD_DIM)
    s = jnp.einsum('ztkgd,zskd->zkgts', qg, k, preferred_element_type=jnp.float32) * (SWA_HEAD_DIM ** -0.5)
    s = jnp.where(valid[:, None, None], s, -jnp.inf)
    sink = jnp.broadcast_to(sinks.astype(jnp.float32).reshape(1, SWA_KV_HEADS, SWA_GROUP, 1, 1), s.shape[:-1] + (1,))
    p = jax.nn.softmax(jnp.concatenate([s, sink], axis=-1), axis=-1)[..., :-1]
    o = jnp.einsum('zkgts,zskd->ztkgd', p.astype(v.dtype), v)
    return o.reshape(Z, T, SWA_HEADS * SWA_HEAD_DIM)


def swa_project(h, pos, w_qkv):
    B, L, _ = h.shape
    q, k, v = jnp.split(h @ w_qkv, [SWA_HEADS * SWA_HEAD_DIM, (SWA_HEADS + SWA_KV_HEADS) * SWA_HEAD_DIM], axis=-1)
    q = rope(q.reshape(B, L, SWA_HEADS, SWA_HEAD_DIM), pos, ROPE_DIM, ROPE_THETA)
    k = rope(k.reshape(B, L, SWA_KV_HEADS, SWA_HEAD_DIM), pos, ROPE_DIM, ROPE_THETA)
    v = v.reshape(B, L, SWA_KV_HEADS, SWA_HEAD_DIM)
    return q, k, v


def swa_prompt(h, w_qkv, w_o, sinks):
    B, L, _ = h.shape
    q, k, v = swa_project(h, jnp.arange(L), w_qkv)
    nb = L // SWA_BLOCK

    def band(t):
        tb = t.reshape(B, nb, SWA_BLOCK, SWA_KV_HEADS, SWA_HEAD_DIM)
        prev = jnp.concatenate([jnp.zeros_like(tb[:, :1]), tb[:, :-1]], axis=1)
        return jnp.concatenate([prev, tb], axis=2).reshape(B * nb, 2 * SWA_BLOCK, SWA_KV_HEADS, SWA_HEAD_DIM)

    blk = jnp.arange(nb)[:, None] * SWA_BLOCK
    qpos = blk + jnp.arange(SWA_BLOCK)[None, :]
    kpos = blk - SWA_BLOCK + jnp.arange(2 * SWA_BLOCK)[None, :]
    diff = qpos[:, :, None] - kpos[:, None, :]
    valid = (diff >= 0) & (diff <= WINDOW) & (kpos[:, None, :] >= 0)
    valid = jnp.tile(valid, (B, 1, 1))
    o = sink_attend(q.reshape(B * nb, SWA_BLOCK, SWA_HEADS, SWA_HEAD_DIM), band(k), band(v), valid, sinks)
    y = o.reshape(B, L, SWA_HEADS * SWA_HEAD_DIM) @ w_o
    return y, k[:, L - WINDOW_BUF:], v[:, L - WINDOW_BUF:]


def swa_sample(h, kbuf, vbuf, w_qkv, w_o, sinks):
    T = h.shape[1]
    pos = PAST_LEN + jnp.arange(T)
    q, k, v = swa_project(h, pos, w_qkv)
    kk = jnp.concatenate([kbuf.astype(k.dtype), k], axis=1)
    vv = jnp.concatenate([vbuf.astype(v.dtype), v], axis=1)
    kpos = PAST_LEN - WINDOW_BUF + jnp.arange(WINDOW_BUF + T)
    diff = pos[:, None] - kpos[None, :]
    valid = ((diff >= 0) & (diff <= WINDOW))[None]
    y = sink_attend(q, kk, vv, valid, sinks) @ w_o
    return y, kk[:, -WINDOW_BUF:], vv[:, -WINDOW_BUF:]


def conv_mix(h, buf, w_in, w_conv, w_out):
    L = h.shape[1]
    bg, cg, z = jnp.split(h @ w_in, 3, axis=-1)
    c_in = cg * z
    full = jnp.concatenate([buf.astype(c_in.dtype), c_in], axis=1)
    acc = w_conv[0] * full[:, 0:L]
    for j in range(1, CONV_WIDTH):
        acc = acc + w_conv[j] * full[:, j:j + L]
    y = (bg * acc) @ w_out
    return y, full[:, -(CONV_WIDTH - 1):]


def pool_mix(h, buf, pos0, w_grp, scale):
    B, L, _ = h.shape
    hf = h.astype(jnp.float32)
    full = jnp.concatenate([buf.astype(jnp.float32), hf], axis=1)
    cs = jnp.concatenate([jnp.zeros((B, 1, D_MODEL), jnp.float32), jnp.cumsum(full, axis=1)], axis=1)
    end = cs[:, POOL_BUF + 1:POOL_BUF + 1 + L]
    pos = pos0 + jnp.arange(L)
    pooled = []
    for gi, w in enumerate(POOL_WINDOWS):
        sl = slice(gi * POOL_GROUP_DIM, (gi + 1) * POOL_GROUP_DIM)
        start = cs[:, POOL_BUF + 1 - w:POOL_BUF + 1 - w + L, sl]
        cnt = jnp.minimum(pos + 1, w).astype(jnp.float32)[None, :, None]
        pooled.append((end[..., sl] - start) / cnt)
    d = (jnp.concatenate(pooled, axis=-1) - hf).reshape(B, L, POOL_GROUPS, POOL_GROUP_DIM).astype(h.dtype)
    y = jnp.einsum('blgc,gcd->blgd', d, w_grp).reshape(B, L, D_MODEL) * scale
    return y, full[:, -POOL_BUF:].astype(h.dtype)


def mem_kv(mem, g_mem, w_kv):
    B = mem.shape[0]
    k, v = jnp.split(rmsnorm(mem, g_mem) @ w_kv, 2, axis=-1)
    return (k.reshape(B, N_MEM, XATTN_HEADS, XATTN_HEAD_DIM), v.reshape(B, N_MEM, XATTN_HEADS, XATTN_HEAD_DIM))


def mem_attend(h, k, v, w_q, w_o):
    B, L, _ = h.shape
    q = (h @ w_q).reshape(B, L, XATTN_HEADS, XATTN_HEAD_DIM)
    s = jnp.einsum('blhd,bmhd->bhlm', q, k, preferred_element_type=jnp.float32) * (XATTN_HEAD_DIM ** -0.5)
    p = jax.nn.softmax(s, axis=-1)
    o = jnp.einsum('bhlm,bmhd->blhd', p.astype(v.dtype), v).reshape(B, L, D_MODEL)
    return o @ w_o


def swiglu(h, w_in, w_out):
    a, b = jnp.split(h @ w_in, 2, axis=-1)
    return (jax.nn.silu(a) * b) @ w_out


def setup_inputs(seed: int = 0) -> dict:
    key = jax.random.key(seed)
    ks = iter(jax.random.split(key, 32))

    def nrm(shape, scale=1.0):
        return jax.random.normal(next(ks), shape, jnp.float32) * scale

    D = D_MODEL
    return {
        "x_prompt": nrm((BATCH, SEQ, D)),
        "x_sample": nrm((DEC_BATCH, DEC_SEQ, D)),
        "state_ret": nrm((N_RET, DEC_BATCH, RET_HEADS, RET_DK, RET_DV), 0.5),
        "cache_swa_k": nrm((N_SWA, DEC_BATCH, WINDOW_BUF, SWA_KV_HEADS, SWA_HEAD_DIM)),
        "cache_swa_v": nrm((N_SWA, DEC_BATCH, WINDOW_BUF, SWA_KV_HEADS, SWA_HEAD_DIM)),
        "state_conv": nrm((N_CONV, DEC_BATCH, CONV_WIDTH - 1, D)),
        "state_pool": nrm((N_POOL, DEC_BATCH, POOL_BUF, D)),
        "cache_mem_k": nrm((DEPTH, DEC_BATCH, N_MEM, XATTN_HEADS, XATTN_HEAD_DIM)),
        "cache_mem_v": nrm((DEPTH, DEC_BATCH, N_MEM, XATTN_HEADS, XATTN_HEAD_DIM)),
        "mem_prompt": nrm((BATCH, N_MEM, D)),
        "norm_g": 1.0 + nrm((DEPTH, 3, D), 0.1),
        "mem_norm_g": 1.0 + nrm((DEPTH, D), 0.1),
        "final_norm_g": 1.0 + nrm((D,), 0.1),
        "ret_w_in": nrm((N_RET, D, 6 * D), D ** -0.5),
        "ret_w_out": nrm((N_RET, 2 * D, D), (2 * D) ** -0.5),
        "swa_w_qkv": nrm((N_SWA, D, (SWA_HEADS + 2 * SWA_KV_HEADS) * SWA_HEAD_DIM), D ** -0.5),
        "swa_w_o": nrm((N_SWA, SWA_HEADS * SWA_HEAD_DIM, D), (SWA_HEADS * SWA_HEAD_DIM) ** -0.5),
        "swa_sinks": nrm((N_SWA, SWA_HEADS)),
        "conv_w_in": nrm((N_CONV, D, 3 * D), D ** -0.5),
        "conv_w": nrm((N_CONV, CONV_WIDTH, D), CONV_WIDTH ** -0.5),
        "conv_w_out": nrm((N_CONV, D, D), D ** -0.5),
        "pool_w": nrm((N_POOL, POOL_GROUPS, POOL_GROUP_DIM, POOL_GROUP_DIM), POOL_GROUP_DIM ** -0.5),
        "pool_scale": 1.0 + nrm((N_POOL, D), 0.1),
        "xattn_w_q": nrm((DEPTH, D, D), D ** -0.5),
        "xattn_w_kv": nrm((DEPTH, D, 2 * D), D ** -0.5),
        "xattn_w_o": nrm((DEPTH, D, D), D ** -0.5),
        "ffn_w_in": nrm((DEPTH, D, 2 * D_FF), D ** -0.5),
        "ffn_w_out": nrm((DEPTH, D_FF, D), D_FF ** -0.5),
    }


def reference(x_prompt, x_sample, state_ret, cache_swa_k, cache_swa_v, state_conv, state_pool,
              cache_mem_k, cache_mem_v, mem_prompt, norm_g, mem_norm_g, final_norm_g,
              ret_w_in, ret_w_out, swa_w_qkv, swa_w_o, swa_sinks, conv_w_in, conv_w, conv_w_out,
              pool_w, pool_scale, xattn_w_q, xattn_w_kv, xattn_w_o, ffn_w_in, ffn_w_out):
    xp, xs = x_prompt, x_sample
    Bp, Lp, _ = xp.shape
    Bs, Ls, _ = xs.shape
    pos_p = jnp.arange(Lp)
    pos_s = PAST_LEN + jnp.arange(Ls)
    ret_p, ret_s, swk_p, swv_p, swk_s, swv_s = [], [], [], [], [], []
    conv_p, conv_s, pool_p, pool_s, memk_p, memv_p = [], [], [], [], [], []
    for i in range(DEPTH):
        j = i // N_MIXERS
        kind = i % N_MIXERS
        hp = rmsnorm(xp, norm_g[i, 0])
        hs = rmsnorm(xs, norm_g[i, 0])
        if kind == 0:
            S0 = jnp.zeros((Bp, RET_HEADS, RET_DK, RET_DV), jnp.float32)
            yp, sp = retention_mix(hp, pos_p, S0, ret_w_in[j], ret_w_out[j], RET_CHUNK)
            ys, ss = retention_mix(hs, pos_s, state_ret[j], ret_w_in[j], ret_w_out[j], Ls)
            ret_p.append(sp)
            ret_s.append(ss)
        elif kind == 1:
            yp, kp, vp = swa_prompt(hp, swa_w_qkv[j], swa_w_o[j], swa_sinks[j])
            ys, ks_, vs_ = swa_sample(hs, cache_swa_k[j], cache_swa_v[j], swa_w_qkv[j], swa_w_o[j], swa_sinks[j])
            swk_p.append(kp)
            swv_p.append(vp)
            swk_s.append(ks_)
            swv_s.append(vs_)
        elif kind == 2:
            zbuf = jnp.zeros((Bp, CONV_WIDTH - 1, D_MODEL), hp.dtype)
            yp, bp = conv_mix(hp, zbuf, conv_w_in[j], conv_w[j], conv_w_out[j])
            ys, bs = conv_mix(hs, state_conv[j], conv_w_in[j], conv_w[j], conv_w_out[j])
            conv_p.append(bp)
            conv_s.append(bs)
        else:
            zbuf = jnp.zeros((Bp, POOL_BUF, D_MODEL), hp.dtype)
            yp, bp = pool_mix(hp, zbuf, 0, pool_w[j], pool_scale[j])
            ys, bs = pool_mix(hs, state_pool[j], PAST_LEN, pool_w[j], pool_scale[j])
            pool_p.append(bp)
            pool_s.append(bs)
        xp = xp + yp
        xs = xs + ys
        mk, mv = mem_kv(mem_prompt, mem_norm_g[i], xattn_w_kv[i])
        memk_p.append(mk)
        memv_p.append(mv)
        xp = xp + mem_attend(rmsnorm(xp, norm_g[i, 1]), mk, mv, xattn_w_q[i], xattn_w_o[i])
        xs = xs + mem_attend(rmsnorm(xs, norm_g[i, 1]), cache_mem_k[i], cache_mem_v[i], xattn_w_q[i], xattn_w_o[i])
        xp = xp + swiglu(rmsnorm(xp, norm_g[i, 2]), ffn_w_in[i], ffn_w_out[i])
        xs = xs + swiglu(rmsnorm(xs, norm_g[i, 2]), ffn_w_in[i], ffn_w_out[i])
    y_prompt = rmsnorm(xp, final_norm_g)
    y_sample = rmsnorm(xs, final_norm_g)
    return (y_prompt, y_sample,
            jnp.stack(ret_p), jnp.stack(ret_s),
            jnp.stack(swk_p), jnp.stack(swv_p), jnp.stack(swk_s), jnp.stack(swv_s),
            jnp.stack(conv_p), jnp.stack(conv_s),
            jnp.stack(pool_p), jnp.stack(pool_s),
            jnp.stack(memk_p), jnp.stack(memv_p))
```

```python
import contextlib
import os
import numpy as np
import concourse.bass as bass
import concourse.mybir as mybir
from concourse.bass_utils import run_bass_kernel_spmd

F32 = mybir.dt.float32
BF16 = mybir.dt.bfloat16
AF = mybir.ActivationFunctionType
ALU = mybir.AluOpType
AX = mybir.AxisListType

D = 1024
EPS = 1e-6
NCORES = 8
DFF = 2816
NMEM = 256
PAST = 8192


class Trk:
    def __init__(self, nc, es):
        self.nc = nc
        self.q = {e: [] for e in ('pe', 'act', 'dve', 'pool', 'sp')}
        self.cnt = {e: 0 for e in self.q}
        self.waited = {e: {} for e in self.q}
        self.sem = {e: es.enter_context(nc.semaphore('sem_' + e)) for e in ('pe', 'act', 'dve', 'pool')}
        self.ND = 8
        self.dsem = {e: [es.enter_context(nc.semaphore(f'dsem_{e}{i}')) for i in range(self.ND)]
                     for e in ('sp', 'pool', 'act')}
        self.dval = {e: [0] * self.ND for e in self.dsem}
        self.dnext = {e: 0 for e in self.dsem}
        self.ccsem = es.enter_context(nc.semaphore('ccsem'))
        self.ccval = 0
        self.lastw = {}
        self.rd = {}
        self.open = {e: False for e in self.q}
        self.alias = {}

    def _x(self, keys):
        out = []
        for k in keys:
            out.extend(self.alias.get(k, (k,)))
        return out

    def _semof(self, k):
        if k[0] == 'e':
            return self.sem[k[1]]
        if k[0] == 'c':
            return self.ccsem
        return self.dsem[k[1][0]][k[1][1]]

    def _deps(self, eng, reads, writes):
        deps = []
        for r in reads:
            w = self.lastw.get(r)
            if w is not None:
                deps.append((w, 'raw'))
            if isinstance(r, tuple) and r[0] == 'ps':
                for r2 in self.rd.get(r, ()):
                    if not (r2[0] == 'e' and r2[1] == eng):
                        deps.append((r2, 'rar'))
        for w_ in writes:
            w = self.lastw.get(w_)
            if w is not None:
                deps.append((w, 'waw'))
            for r in self.rd.get(w_, ()):
                deps.append((r, 'war'))
        waits = []
        for (ev, kind) in deps:
            k = ev[:2]
            val = ev[2]
            if ev[0] == 'e' and ev[1] == eng:
                if eng == 'pe':
                    continue
            if self.waited[eng].get(k, 0) >= val:
                continue
            self.waited[eng][k] = val
            waits.append((k, val))
        return waits

    def _commit(self, ev, reads, writes):
        for w_ in writes:
            self.lastw[w_] = ev
            self.rd[w_] = set()
        for r in reads:
            self.rd.setdefault(r, set()).add(ev)

    def op(self, eng, fn, r=(), w=(), inc=True):
        r = self._x(r); w = self._x(w)
        waits = self._deps(eng, r, w)
        ev = ('e', eng, self.cnt[eng] + 1)
        if inc:
            self.cnt[eng] += 1
            self.open[eng] = False
            self.q[eng].append((waits, fn, (self.sem[eng], 1)))
        else:
            self.open[eng] = True
            self.q[eng].append((waits, fn, None))
        self._commit(ev, r, w)

    def dma(self, eng, out, in_, r=(), w=(), **kw):
        r = self._x(r); w = self._x(w)
        waits = self._deps(eng, r, w)
        i = self.dnext[eng]
        self.dnext[eng] = (i + 1) % self.ND
        k = ('d', (eng, i))
        if self.dval[eng][i] > 0 and self.waited[eng].get(k, 0) < self.dval[eng][i]:
            self.waited[eng][k] = self.dval[eng][i]
            waits.append((k, self.dval[eng][i]))
        self.dval[eng][i] += 16
        ev = ('d', (eng, i), self.dval[eng][i])
        self.q[eng].append((waits, lambda e: e.dma_start(out=out, in_=in_, **kw), (self.dsem[eng][i], 16)))
        self._commit(ev, r, w)

    def cc(self, ins, outs, r=(), w=()):
        r = self._x(r); w = self._x(w)
        waits = self._deps('pool', r, w)
        self.ccval += 1
        ev = ('c', 0, self.ccval)
        self.q['pool'].append((waits, lambda e: e.collective_compute(
            "AllGather", ALU.bypass, replica_groups=[list(range(NCORES))], ins=[ins], outs=[outs]),
            (self.ccsem, 1)))
        self._commit(ev, r, w)

    def finish(self, eng_list):
        for eng in eng_list:
            waits = []
            for e2 in ('pe', 'act', 'dve', 'pool'):
                if e2 != eng and self.cnt[e2] > self.waited[eng].get(('e', e2), 0):
                    waits.append((('e', e2), self.cnt[e2]))
            for e2 in self.dsem:
                for i in range(self.ND):
                    if self.dval[e2][i] > self.waited[eng].get(('d', (e2, i)), 0):
                        waits.append((('d', (e2, i)), self.dval[e2][i]))
            self.q[eng].append((waits, None, None))

    def emit(self, name, e):
        assert not self.open[name], name
        for waits, fn, inc in self.q[name]:
            for (k, val) in waits:
                e.wait_ge(self._semof(k), val)
            if fn is None:
                continue
            ins = fn(e)
            if inc is not None:
                ins.then_inc(inc[0], inc[1])


def build(T, NS, stage=99):
    nc = bass.Bass("TRN2", target_bir_lowering=False)
    es = contextlib.ExitStack()
    NT = T // 128
    GS = min(512, T)
    NG = T // GS
    TPG = GS // 128
    TT = T + NS
    K = Trk(nc, es)
    NSEG = 4

    used = set()
    nc.used_inputs = used

    class Lz:
        def __init__(self, name, shape):
            self.name = name; self.shp = list(shape); self._ap = None

        def ap(self):
            if self._ap is None:
                self._ap = nc.dram_tensor(self.name, self.shp, F32, kind="ExternalInput").ap()
                used.add(self.name)
            return self._ap

        def __getitem__(self, idx):
            return self.ap()[idx]

        def rearrange(self, *a, **k):
            return self.ap().rearrange(*a, **k)

    def din(name, shape, dt=F32):
        return Lz(name, shape)

    def dout(name, shape):
        return nc.dram_tensor(name, list(shape), F32, kind="ExternalOutput").ap()

    xp = din("xp", [NSEG * T, D]); xs = din("xs", [NS, D])
    sret = din("sret", [NS, 4, 256, 512])
    swk = din("swk", [NS, 128, 256]); swv = din("swv", [NS, 128, 256])
    sconv = din("sconv", [NS, 2, D]); spool = din("spool", [NS, 15, D])
    cmk = din("cmk", [4, NS, NMEM, D]); cmv = din("cmv", [4, NS, NMEM, D])
    memp = din("memp", [NMEM, D])
    norm_g = din("norm_g", [12, D]); mem_norm_g = din("mem_norm_g", [4, D]); final_norm_g = din("final_norm_g", [1, D])
    ret_w_in = din("ret_w_in", [D, 6 * D]); ret_w_out = din("ret_w_out", [2 * D, D])
    swa_w_qkv = din("swa_w_qkv", [D, 1536]); swa_w_o = din("swa_w_o", [D, D]); swa_sinks = din("swa_sinks", [1, 16])
    conv_w_in = din("conv_w_in", [D, 3 * D]); conv_w = din("conv_w", [3, D]); conv_w_out = din("conv_w_out", [D, D])
    pool_w = din("pool_w", [D, 256]); pool_scale = din("pool_scale", [1, D])
    xattn_w_q = din("xattn_w_q", [4, D, D]); xattn_w_kv = din("xattn_w_kv", [4, D, 2 * D]); xattn_w_o = din("xattn_w_o", [4, D, D])
    ffn_w_in = din("ffn_w_in", [4, D, 2 * DFF]); ffn_w_out = din("ffn_w_out", [4, DFF, D])
    c_ident = din("c_ident", [128, 128])
    c_rcosT = din("c_rcosT", [128, NSEG * T]); c_rsinT = din("c_rsinT", [128, NSEG * T])
    c_rsT = din("c_rsT", [128, 2])
    c_dec = din("c_dec", [128, 4 * 128])
    c_gq = din("c_gq", [128, 4 * 128])
    c_gk = din("c_gk", [128, 12])
    c_gkA = din("c_gkA", [128, 4 * (T // 128)])
    c_id16 = din("c_id16", [NS, NS])
    c_coef = din("c_coef", [128, 32])
    c_sel = din("c_sel", [128, 8])
    c_swcs = din("c_swcs", [128, NSEG * NT + 1, 16])
    c_mask = din("c_mask", [128, 3 * 128])
    c_poolA = din("c_poolA", [128, 12 * 128])
    c_poolB = din("c_poolB", [128, 2 * 4 * NS])
    c_poolC = din("c_poolC", [NS, 4 * NS])
    c_idns = din("c_idns", [128, NS * NS])
    c_erow = din("c_erow", [NS, NS * 128])

    yp = dout("yp", [NSEG * T, D]); ys = dout("ys", [NS, D])
    o_retp = dout("o_retp", [4, 256, 512]); o_rets = dout("o_rets", [NS, 4, 256, 512])
    o_swkp = dout("o_swkp", [128, 256]); o_swvp = dout("o_swvp", [128, 256])
    o_swks = dout("o_swks", [NS, 128, 256]); o_swvs = dout("o_swvs", [NS, 128, 256])
    o_convp = dout("o_convp", [2, D]); o_convs = dout("o_convs", [NS, 2, D])
    o_poolp = dout("o_poolp", [15, D]); o_pools = dout("o_pools", [NS, 15, D])
    o_memk = dout("o_memk", [4, NMEM, D]); o_memv = dout("o_memv", [4, NMEM, D])
    dbg = dout("dbg", [TT, D]) if stage < 10 else None
    dbg2 = dout("dbg2", [128, 8192]) if stage < 10 else None

    st_ret = nc.dram_tensor("st_ret", [4 * 256, 512], F32)
    st_h = nc.dram_tensor("st_h", [128, 512], F32)
    st_c = nc.dram_tensor("st_c", [128, 16], F32)
    st_p = nc.dram_tensor("st_p", [16, D], F32)
    cc_loc_ret = nc.dram_tensor("cc_loc_ret", [4 * 256, 512], F32)
    cc_gat_ret = nc.dram_tensor("cc_gat_ret", [NCORES * 4 * 256, 512], F32)
    cc_loc_h = nc.dram_tensor("cc_loc_h", [128, 512], F32)
    cc_gat_h = nc.dram_tensor("cc_gat_h", [NCORES * 128, 512], F32)
    cc_loc_c = nc.dram_tensor("cc_loc_c", [128, 16], F32)
    cc_gat_c = nc.dram_tensor("cc_gat_c", [NCORES * 128, 16], F32)
    cc_loc_p = nc.dram_tensor("cc_loc_p", [16, D], F32)
    cc_gat_p = nc.dram_tensor("cc_gat_p", [NCORES * 16, D], F32)

    def sb(name, shape, dt=F32):
        return nc.alloc_sbuf_tensor(name, list(shape), dt)

    x = sb("x", [128, NT + 1, D])
    hT = sb("hT", [128, 8, TT], BF16)
    BIGE = 18496
    big = sb("big", [128, BIGE], BF16)
    NW = 3
    wslot = [sb(f"w{i}", [128, 4096], BF16) for i in range(NW)]
    junk = sb("junk", [128, D], BF16)
    ident = sb("ident", [128, 128], BF16)
    ones = sb("ones", [128, 128], BF16)
    gcols = sb("gcols", [128, 13 * 8])
    mgcols = sb("mgcols", [128, 4 * 8])
    ss = sb("ss", [128, NT + 1]); rs = sb("rs", [128, NT + 1]); rstd = sb("rstd", [128, NT + 1])
    tabs = sb("tabs", [128, 2048])
    scr = [sb(f"scr{i}", [128, 2048], BF16) for i in range(4)]
    LE = 7680
    Lreg = sb("Lreg", [128, LE], BF16)
    lst = {'o': 0, 'keys': set()}

    def lreset():
        lst['o'] = 0
        lst['keys'] = set()

    def la(key, shape, dt=F32):
        n = 1
        for d_ in shape[1:]:
            n *= d_
        nb = n * (2 if dt == F32 else 1)
        o = lst['o']; o += o % 2
        assert o + nb <= LE, (key, o, nb)
        lst['o'] = o + nb
        v = Lreg[0:shape[0], o:o + nb]
        if dt == F32:
            v = v.bitcast(F32)
        blocks = tuple(('L', i) for i in range(o // 512, (o + nb - 1) // 512 + 1))
        if key in lst['keys']:
            K.alias[key] = tuple(sorted(set(K.alias[key] + blocks)))
        else:
            K.alias[key] = blocks
            lst['keys'].add(key)
        if len(shape) == 3:
            v = v.rearrange("p (a b) -> p a b", a=shape[1])
        elif len(shape) == 4:
            v = v.rearrange("p (a b c) -> p a b c", a=shape[1], b=shape[2])
        return v

    smal = sb("smal", [128, 1024])
    stg = smal[:, :].rearrange("p (a b) -> p a b", a=2)
    tiny = sb("tiny", [128, 32])
    dstage = sb("dstage", [128, 8192]) if os.environ.get("KDBG") else None
    ps = [nc.alloc_psum_tensor(f"ps{i}", [128, 512], F32) for i in range(8)]

    st = {'ps': 0, 'w': 0, 'scr': 0}

    def bank():
        b = st['ps']; st['ps'] = (b + 1) % 5
        return ps[b], ('ps', b)

    def banks(n):
        return [bank() for _ in range(n)]

    def wload(src2d, r0, nrc, c0, ncols, eng='pool'):
        i = st['w']; st['w'] = (i + 1) % NW
        assert nrc * ncols <= 4096
        view = wslot[i][:, 0:nrc * ncols].rearrange("p (a b) -> p a b", a=nrc)
        src = src2d[r0:r0 + 128 * nrc, c0:c0 + ncols].rearrange("(a p) n -> p a n", p=128)
        K.dma(eng, view, src, w=[('w', i)])
        return view, ('w', i)

    def scratch():
        i = st['scr']; st['scr'] = (i + 1) % 4
        return scr[i], ('scr', i)

    dstate = {'c': 0}

    def dump(ap, keys, n):
        c0 = dstate['c']; dstate['c'] += n
        npart = ap.shape[0]
        if ap.dtype != F32:
            K.op('dve', lambda e: e.tensor_copy(out=dstage[0:npart, c0:c0 + n], in_=ap), r=keys, w=['dstage'])
            K.dma('sp', dbg2[0:npart, c0:c0 + n], dstage[0:npart, c0:c0 + n], r=['dstage'])
        else:
            K.dma('sp', dbg2[0:npart, c0:c0 + n], ap, r=keys)
        return c0

    K.dma('pool', ident[:], c_ident[:, :], w=['ident'])
    K.op('dve', lambda e: e.memset(ones[:], 1.0), w=['ones'])
    K.op('dve', lambda e: e.memset(x[:, NT, :], 0.0), w=[('x', NT)])
    K.dma('sp', gcols[:, 0:96].rearrange("p (n c) -> p n c", c=8),
          norm_g.rearrange("n (c p) -> p n c", p=128), w=['gcols'], allow_slow_non_contiguous=True)
    K.dma('sp', gcols[:, 96:104], final_norm_g.rearrange("n (c p) -> p (n c)", p=128), w=['gcols'], allow_slow_non_contiguous=True)
    K.dma('sp', mgcols[:].rearrange("p (n c) -> p n c", c=8),
          mem_norm_g.rearrange("n (c p) -> p n c", p=128), w=['mgcols'], allow_slow_non_contiguous=True)
    def load_x(seg):
        for t in range(NT):
            K.dma('sp', x[:, t, :], xp[seg * T + t * 128:seg * T + (t + 1) * 128, :], w=[('x', t)])
        K.dma('sp', x[0:NS, NT, :], xs[:, :], r=[], w=[('x', NT)])

    def sumsq_rstd(src_tiles, n, keyr):
        for i, (ap, key) in enumerate(src_tiles):
            K.op('act', lambda e, ap=ap, i=i: e.activation(out=junk[:], in_=ap, func=AF.Square,
                                                            accum_out=ss[:, i:i + 1]), r=[key], w=['junk', 'ss'])
        K.op('act', lambda e: e.activation(out=rs[:, 0:n], in_=ss[:, 0:n], func=AF.Sqrt, scale=1.0 / D, bias=epsc[:, 0:1]),
             r=['ss', 'epsc'], w=['rs'])
        K.op('dve', lambda e: e.reciprocal(out=rstd[:, 0:n], in_=rs[:, 0:n]), r=['rs'], w=[keyr])

    epsc = sb("epsc", [128, 1])
    K.op('dve', lambda e: e.memset(epsc[:], EPS), w=['epsc'])

    def norm_T(gi):
        sumsq_rstd([(x[:, t, :], ('x', t)) for t in range(NT + 1)], NT + 1, 'rstd')
        for n in range(NG + 1):
            tiles = list(range(n * TPG, (n + 1) * TPG)) if n < NG else [NT]
            xs_ = [scratch(), scratch()]

            def xnv(ti, xs_=xs_):
                return xs_[ti // 2][0][:, (ti % 2) * D:(ti % 2 + 1) * D], xs_[ti // 2][1]
            for ti, t in enumerate(tiles):
                K.op('dve', lambda e, ti=ti, t=t, xnv=xnv: e.tensor_scalar(out=xnv(ti)[0], in0=x[:, t, :], scalar1=rstd[:, t:t + 1],
                                                                          scalar2=None, op0=ALU.mult),
                     r=[('x', t), 'rstd'], w=[xnv(ti)[1]])
            bk = banks(4)
            for c in range(8):
                pb = bk[c // 2][0][:, :].bitcast(BF16)
                for ti, t in enumerate(tiles):
                    last = (ti == len(tiles) - 1) and (c % 2 == 1)
                    if n < NG:
                        K.op('pe', lambda e, pb=pb, c=c, ti=ti, xnv=xnv: e.transpose(
                            out=pb[:, (c % 2) * 512 + ti * 128:(c % 2) * 512 + ti * 128 + 128],
                            in_=xnv(ti)[0][:, c * 128:(c + 1) * 128], identity=ident[:]),
                            r=[xnv(ti)[1], 'ident'], w=[bk[c // 2][1]], inc=last)
                    else:
                        K.op('pe', lambda e, pb=pb, c=c, xnv=xnv: e.transpose(
                            out=pb[:, (c % 2) * 512:(c % 2) * 512 + NS],
                            in_=xnv(0)[0][0:NS, c * 128:(c + 1) * 128], identity=ident[0:NS, 0:NS]),
                            r=[xnv(0)[1], 'ident'], w=[bk[c // 2][1]], inc=last)
            ncol = GS if n < NG else NS
            c0 = n * GS if n < NG else T
            for c in range(8):
                pb = bk[c // 2][0][:, :].bitcast(BF16)
                src = pb[:, (c % 2) * 512:(c % 2) * 512 + ncol]
                dst = hT[:, c, c0:c0 + ncol]
                gc = gcols[:, gi * 8 + c:gi * 8 + c + 1]
                if c % 2 == 0:
                    K.op('act', lambda e, src=src, dst=dst, gc=gc: e.activation(out=dst, in_=src, func=AF.Copy, scale=gc),
                         r=[bk[c // 2][1], 'gcols'], w=[('hT', n)])
                else:
                    K.op('dve', lambda e, src=src, dst=dst, gc=gc: e.tensor_scalar(out=dst, in0=src, scalar1=gc, scalar2=None,
                                                                                  op0=ALU.mult),
                         r=[bk[c // 2][1], 'gcols'], w=[('hT', n)])

    def tokgroups():
        return [(n, n * GS, GS) for n in range(NG)] + [(NG, T, NS)]

    def toktiles():
        return [(t, t * 128, 128) for t in range(NT)] + [(NT, T, NS)]

    def projB(wv, wkey, wc0, src, srckey, evac):
        for (n, c0, ncol) in tokgroups():
            p, pk = bank()
            for kc in range(8):
                K.op('pe', lambda e, p=p, kc=kc, c0=c0, ncol=ncol: e.matmul(
                    out=p[:, 0:ncol], lhsT=wv[:, kc, wc0:wc0 + 128], rhs=src[:, kc, c0:c0 + ncol],
                    start=(kc == 0), stop=(kc == 7)), r=[wkey, (srckey, n)], w=[pk], inc=(kc == 7))
            evac(n, c0, ncol, p, pk)

    def resid_add(t, nrow, half, p, pk):
        K.op('dve', lambda e: e.tensor_tensor(out=x[0:nrow, t, half * 512:(half + 1) * 512], in0=p[0:nrow, :],
                                              in1=x[0:nrow, t, half * 512:(half + 1) * 512], op=ALU.add),
             r=[pk, ('x', t)], w=[('x', t)])

    def projA_resid(lhs_fn, lhs_keys, nk, wv, wkey):
        for (t, c0, nrow) in toktiles():
            for half in range(2):
                p, pk = bank()
                for k in range(nk):
                    K.op('pe', lambda e, p=p, k=k, c0=c0, nrow=nrow, half=half: e.matmul(
                        out=p[0:nrow, :], lhsT=lhs_fn(k, c0, nrow), rhs=wv[:, k, half * 512:(half + 1) * 512],
                        start=(k == 0), stop=(k == nk - 1)), r=[wkey] + lhs_keys(t), w=[pk], inc=(k == nk - 1))
                resid_add(t, nrow, half, p, pk)

    def ffn(l):
        norm_T(l * 3 + 2)
        gT = [big[:, i * 4 * TT:(i + 1) * 4 * TT].rearrange("p (a b) -> p a b", a=4) for i in range(2)]
        nmac = (22 + 3) // 4
        for m in range(nmac):
            chunks = list(range(m * 4, min(22, m * 4 + 4)))
            gt = gT[m % 2]
            for jj, j in enumerate(chunks):
                if jj % 2 == 0:
                    i = st['w']; st['w'] = (i + 1) % NW
                    nb = min(2, len(chunks) - jj) * 128
                    wv = wslot[i][:, 0:8 * 2 * nb].rearrange("p (k a f) -> p k a f", k=8, a=2)
                    for ab in range(2):
                        src = ffn_w_in[l][:, ab * DFF + j * 128:ab * DFF + j * 128 + nb].rearrange("(k p) f -> p k f", p=128)
                        K.dma('pool', wv[:, :, ab, :], src, w=[('w', i)])
                    wkey = ('w', i)
                fo = (jj % 2) * 128
                for (n, c0, ncol) in tokgroups():
                    pa, pak = bank()
                    pbk_ = bank()
                    pb_, pbk = pbk_
                    for ab, pp, ppk in ((0, pa, pak), (1, pb_, pbk)):
                        for kc in range(8):
                            K.op('pe', lambda e, pp=pp, kc=kc, c0=c0, ncol=ncol, ab=ab, wv=wv, fo=fo: e.matmul(
                                out=pp[:, 0:ncol], lhsT=wv[:, kc, ab, fo:fo + 128], rhs=hT[:, kc, c0:c0 + ncol],
                                start=(kc == 0), stop=(kc == 7)), r=[wkey, ('hT', n)], w=[ppk], inc=(kc == 7))
                    sa, sak = scratch()
                    K.op('act', lambda e, sa=sa, pa=pa, ncol=ncol: e.activation(out=sa[:, 0:ncol], in_=pa[:, 0:ncol], func=AF.Silu),
                         r=[pak], w=[sak])
                    K.op('dve', lambda e, sa=sa, pb_=pb_, ncol=ncol, c0=c0, gt=gt, jj=jj: e.tensor_tensor(
                        out=gt[:, jj, c0:c0 + ncol], in0=pb_[:, 0:ncol], in1=sa[:, 0:ncol], op=ALU.mult),
                        r=[pbk, sak], w=[('gT', m % 2, n)])
            nk = len(chunks)
            wo, wok = [], []
            for h2 in range((nk + 3) // 4):
                v_, k_ = wload(ffn_w_out[l], m * 512, nk, 0, D) if False else (None, None)
            wv2, wk2 = wload(ffn_w_out[l], m * 512, nk, 0, D)
            projA_resid(lambda k, c0, nrow, gt=gt: gt[:, k, c0:c0 + nrow],
                        lambda t, m=m: [('gT', m % 2, min(t // TPG, NG))], nk, wv2, wk2)

    XA = {}

    def mem_prep():
        memn = XA['memn']
        assert 8 * TT >= 4096
        mt = hT[:, :, :].rearrange("p a b -> p (a b)")[:, 0:4096].bitcast(F32).rearrange("p (a b) -> p a b", a=2)
        hk = [('hT', n) for n in range(NG + 1)]
        for i in range(2):
            K.dma('sp', mt[:, i, :], memp[i * 128:(i + 1) * 128, :], w=hk)
        sumsq_rstd([(mt[:, i, :], hk[0]) for i in range(2)], 2, 'rstd')
        for i in range(2):
            K.op('dve', lambda e, i=i: e.tensor_scalar(out=memn[:, i, :], in0=mt[:, i, :], scalar1=rstd[:, i:i + 1], scalar2=None,
                                                      op0=ALU.mult), r=hk + ['rstd'], w=['memn'])

    def mem_kv(l):
        memn = XA['memn']; KT = XA['KT']; Vb = XA['Vb']
        mT_, mTk = scratch()
        memT = mT_[:, :].rearrange("p (a b) -> p a b", a=8)
        bk = banks(2)
        for c in range(8):
            pb = bk[c // 4][0][:, :].bitcast(BF16)
            for i in range(2):
                K.op('pe', lambda e, pb=pb, c=c, i=i: e.transpose(out=pb[:, (c % 4) * 256 + i * 128:(c % 4) * 256 + i * 128 + 128],
                                                                  in_=memn[:, i, c * 128:(c + 1) * 128], identity=ident[:]),
                     r=['memn', 'ident'], w=[bk[c // 4][1]], inc=(i == 1 and c % 4 == 3))
        for c in range(8):
            pb = bk[c // 4][0][:, :].bitcast(BF16)
            K.op('act', lambda e, pb=pb, c=c: e.activation(out=memT[:, c, :], in_=pb[:, (c % 4) * 256:(c % 4) * 256 + 256],
                                                          func=AF.Copy, scale=mgcols[:, l * 8 + c:l * 8 + c + 1]),
                 r=[bk[c // 4][1], 'mgcols'], w=[mTk])
        KMK = int(os.environ.get("KMK", "99"))
        if KMK <= 1:
            return
        for kv in range(2):
            for hf in range(2):
                wv, wk = wload(xattn_w_kv[l], 0, 8, kv * D + hf * 512, 512)
                if KMK <= 2:
                    continue
                for i in range(2):
                    p, pk = bank()
                    for kc in range(8):
                        K.op('pe', lambda e, p=p, kc=kc, i=i, wv=wv: e.matmul(out=p[:, :], lhsT=memT[:, kc, i * 128:(i + 1) * 128],
                                                                            rhs=wv[:, kc, :], start=(kc == 0), stop=(kc == 7)),
                             r=[wk, mTk], w=[pk], inc=(kc == 7))
                    if KMK <= 3:
                        continue
                    K.op('act', lambda e, p=p, i=i: e.activation(out=stg[:, i, :], in_=p[:, :], func=AF.Copy), r=[pk], w=[('smal', i)])
                    if KMK <= 4:
                        continue
                    if kv == 1:
                        K.op('act', lambda e, p=p, i=i, hf=hf: e.activation(out=Vb[:, i, hf * 512:(hf + 1) * 512], in_=p[:, :], func=AF.Copy),
                             r=[pk], w=['Vb'])
                    dst = (o_memk if kv == 0 else o_memv)[l, i * 128:(i + 1) * 128, hf * 512:(hf + 1) * 512]
                    if KMK <= 5:
                        continue
                    if XA['seg'] == 0:
                        K.dma('sp', dst, stg[:, i, :], r=[('smal', i)])
                if kv == 0 and KMK > 6:
                    for cc_ in range(4):
                        p, pk = bank()
                        for kc in range(8):
                            K.op('pe', lambda e, p=p, kc=kc, cc_=cc_, wv=wv: e.matmul(out=p[:, 0:NMEM], lhsT=wv[:, kc, cc_ * 128:(cc_ + 1) * 128],
                                                                                    rhs=memT[:, kc, :], start=(kc == 0), stop=(kc == 7)),
                                 r=[wk, mTk], w=[pk], inc=(kc == 7))
                        K.op('dve', lambda e, p=p, cc_=cc_, hf=hf: e.tensor_copy(out=KT[:, hf * 4 + cc_, :], in_=p[:, 0:NMEM]),
                             r=[pk], w=['KT'])

    def xattn(l):
        SUB = int(os.environ.get("KSUB", "99"))
        lreset()
        XA['memn'] = la('memn', [128, 2, D], BF16); XA['KT'] = la('KT', [128, 8, NMEM], BF16); XA['Vb'] = la('Vb', [128, 2, D], BF16)
        KT = XA['KT']; Vb = XA['Vb']
        mem_prep()
        norm_T(l * 3 + 1)
        if SUB <= 1:
            return
        mem_kv(l)
        if SUB <= 2:
            return
        aT = big[:, 0:8 * TT].rearrange("p (a b) -> p a b", a=8)
        o0 = 8 * TT
        PT = big[:, o0:o0 + 1024].rearrange("p (a b) -> p a b", a=2)
        rden = smal[:, 0:512]
        for hf in range(2):
            wv, wk = wload(xattn_w_q[l], 0, 8, hf * 512, 512)
            for cc_ in range(4):
                c = hf * 4 + cc_

                def ev(n, c0, ncol, p, pk, c=c):
                    K.op('act', lambda e: e.activation(out=aT[:, c, c0:c0 + ncol], in_=p[:, 0:ncol], func=AF.Copy),
                         r=[pk], w=[('aT', c // 2, n)])
                projB(wv, wk, cc_ * 128, hT, 'hT', ev)
        if SUB <= 3:
            return
        for n in range(NG):
            c0 = n * GS
            for h in range(4):
                sb_ = banks(2)
                for mc in range(2):
                    for dc in range(2):
                        K.op('pe', lambda e, mc=mc, dc=dc, p=sb_[mc][0], h=h, c0=c0: e.matmul(
                            out=p[:, 0:GS], lhsT=KT[:, h * 2 + dc, mc * 128:(mc + 1) * 128], rhs=aT[:, h * 2 + dc, c0:c0 + GS],
                            start=(dc == 0), stop=(dc == 1)), r=['KT', ('aT', h, n)], w=[sb_[mc][1]], inc=(dc == 1))
                for mc in range(2):
                    K.op('act', lambda e, mc=mc, p=sb_[mc][0]: e.activation(out=PT[:, mc, 0:GS], in_=p[:, 0:GS], func=AF.Exp, scale=1.0 / 16.0),
                         r=[sb_[mc][1]], w=[('PT', mc)])
                dn, dnk = bank()
                for mc in range(2):
                    K.op('pe', lambda e, mc=mc, dn=dn: e.matmul(out=dn[:, 0:GS], lhsT=ones[:, :], rhs=PT[:, mc, 0:GS], start=(mc == 0), stop=(mc == 1)),
                         r=['ones', ('PT', mc)], w=[dnk], inc=(mc == 1))
                ob = banks(2)
                for dc in range(2):
                    for mc in range(2):
                        K.op('pe', lambda e, mc=mc, dc=dc, p=ob[dc][0], h=h: e.matmul(
                            out=p[:, 0:GS], lhsT=Vb[:, mc, h * 256 + dc * 128:h * 256 + dc * 128 + 128], rhs=PT[:, mc, 0:GS],
                            start=(mc == 0), stop=(mc == 1)), r=['Vb', ('PT', mc)], w=[ob[dc][1]], inc=(mc == 1))
                K.op('dve', lambda e, dn=dn: e.reciprocal(out=rden[:, 0:GS], in_=dn[:, 0:GS]), r=[dnk], w=[('smal', 0)])
                for dc in range(2):
                    K.op('dve', lambda e, dc=dc, p=ob[dc][0], h=h, c0=c0: e.tensor_tensor(out=aT[:, h * 2 + dc, c0:c0 + GS], in0=p[:, 0:GS], in1=rden[:, 0:GS],
                                                                               op=ALU.mult), r=[ob[dc][1], ('smal', 0)], w=[('aT', h, n)])
        if SUB <= 4:
            return
        xattn_samples(l, aT)
        if SUB <= 5:
            return
        for hf in range(2):
            pass
        wvs = []
        for kh in range(2):
            wvs.append(wload(xattn_w_o[l], kh * 512, 4, 0, D))
        for (t, c0, nrow) in toktiles():
            for half in range(2):
                p, pk = bank()
                for k in range(8):
                    wv, wk = wvs[k // 4]
                    K.op('pe', lambda e, p=p, k=k, wv=wv, c0=c0, nrow=nrow, half=half: e.matmul(out=p[0:nrow, :], lhsT=aT[:, k, c0:c0 + nrow],
                                                                  rhs=wv[:, k % 4, half * 512:(half + 1) * 512], start=(k == 0), stop=(k == 7)),
                         r=[wk] + [('aT', hh, min(t // TPG, NG)) for hh in range(4)], w=[pk], inc=(k == 7))
                resid_add(t, nrow, half, p, pk)

    idns = sb("idns", [128, NS, NS], BF16)
    K.dma('pool', idns[:].rearrange("p a b -> p (a b)"), c_idns[:, :], w=['idns'])

    def xattn_samples(l, aT):
        qm = [la('qm0', [128, 8, NS], BF16), la('qm1', [128, 8, NS], BF16)]
        pm = [la('pm0', [128, 8, NS], BF16), la('pm1', [128, 8, NS], BF16)]
        PTs = la('PTs', [128, 8, NS], BF16)
        sc = [(ps[6], ('ps', 6)), (ps[7], ('ps', 7))]
        for s in range(NS):
            K.op('dve', lambda e, s=s: e.tensor_tensor(out=qm[s % 2], in0=aT[:, :, T:TT], in1=idns[:, s, :].unsqueeze(1).broadcast_to([128, 8, NS]), op=ALU.mult),
                 r=[('aT', hh, NG) for hh in range(4)] + ['idns'], w=['qm%d' % (s % 2)])
            kb, kbk = scratch()
            kbv = kb[:, :].rearrange("p (a b) -> p a b", a=2)
            K.dma('pool', kbv, cmk[l, s].rearrange("(a p) n -> p a n", p=128), w=[kbk])
            kt, ktk = scratch()
            ktv = kt[:, :].rearrange("p (a b) -> p a b", a=8)
            for g4 in range(2):
                bk = banks(2)
                for c4 in range(4):
                    c = g4 * 4 + c4
                    pb = bk[c4 // 2][0][:, :].bitcast(BF16)
                    for i in range(2):
                        K.op('pe', lambda e, pb=pb, c=c, c4=c4, i=i, kbv=kbv: e.transpose(
                            out=pb[:, (c4 % 2) * 256 + i * 128:(c4 % 2) * 256 + i * 128 + 128],
                            in_=kbv[:, i, c * 128:(c + 1) * 128], identity=ident[:]), r=[kbk, 'ident'], w=[bk[c4 // 2][1]],
                            inc=(i == 1 and c4 % 2 == 1))
                for b2 in range(2):
                    pb = bk[b2][0][:, :].bitcast(BF16)
                    eng = 'act' if b2 == 0 else 'dve'
                    dst = ktv[:, g4 * 4 + b2 * 2:g4 * 4 + b2 * 2 + 2, :]
                    if eng == 'act':
                        K.op('act', lambda e, pb=pb, dst=dst: e.activation(out=dst, in_=pb[:, 0:512].rearrange("p (a b) -> p a b", a=2), func=AF.Copy),
                             r=[bk[b2][1]], w=[ktk])
                    else:
                        K.op('dve', lambda e, pb=pb, dst=dst: e.tensor_copy(out=dst, in_=pb[:, 0:512].rearrange("p (a b) -> p a b", a=2)),
                             r=[bk[b2][1]], w=[ktk])
            for h in range(4):
                for dc in range(2):
                    first = (s == 0 and h % 2 == 0 and dc == 0)
                    lastm = (s == NS - 1 and dc == 1)
                    K.op('pe', lambda e, h=h, dc=dc, s=s, first=first, ktv=ktv: e.matmul(
                        out=sc[h // 2][0][0:NS, (h % 2) * 256:(h % 2) * 256 + 256], lhsT=qm[s % 2][:, h * 2 + dc, :], rhs=ktv[:, h * 2 + dc, :],
                        start=first, stop=(s == NS - 1 and dc == 1 and h % 2 == 1), skip_group_check=True),
                        r=['qm%d' % (s % 2), ktk], w=[sc[h // 2][1]], inc=True if (dc == 1) else False)
        if dstage is not None and l == 0:
            for b2 in range(2):
                K.op('act', lambda e, b2=b2: e.activation(out=tabs[0:NS, b2 * 512:(b2 + 1) * 512], in_=sc[b2][0][0:NS, :], func=AF.Copy), r=[sc[b2][1]], w=['tabsdbg'])
            dump(tabs[0:NS, 0:1024], ['tabsdbg'], 1024)
            dump(ktv[:, 0, :], [ktk], 256)
        mx = tiny[0:NS, 0:4]; nmx = tiny[0:NS, 4:8]; sm = tiny[0:NS, 8:12]; rsm = tiny[0:NS, 12:16]
        Pf = smal[0:NS, 0:1024]
        Pb = junk[0:NS, :]
        for h in range(4):
            src = sc[h // 2][0][0:NS, (h % 2) * 256:(h % 2) * 256 + 256]
            K.op('dve', lambda e, src=src, h=h: e.tensor_reduce(out=mx[:, h:h + 1], in_=src, axis=AX.X, op=ALU.max), r=[sc[h // 2][1]], w=['mx'])
        K.op('dve', lambda e: e.tensor_scalar(out=nmx, in0=mx, scalar1=-1.0 / 16.0, scalar2=None, op0=ALU.mult), r=['mx'], w=['nmx'])
        for h in range(4):
            src = sc[h // 2][0][0:NS, (h % 2) * 256:(h % 2) * 256 + 256]
            K.op('act', lambda e, src=src, h=h: e.activation(out=Pf[:, h * 256:(h + 1) * 256], in_=src, func=AF.Exp, scale=1.0 / 16.0,
                                                            bias=nmx[:, h:h + 1], accum_out=sm[:, h:h + 1]), r=[sc[h // 2][1], 'nmx'], w=[('smal', 0), ('smal', 1), 'sm'])
        K.op('dve', lambda e: e.reciprocal(out=rsm, in_=sm), r=['sm'], w=['rsm'])
        for h in range(4):
            K.op('dve', lambda e, h=h: e.tensor_scalar(out=Pb[:, h * 256:(h + 1) * 256], in0=Pf[:, h * 256:(h + 1) * 256], scalar1=rsm[:, h:h + 1],
                                                      scalar2=None, op0=ALU.mult), r=[('smal', 0), ('smal', 1), 'rsm'], w=['junk'])
        if dstage is not None and l == 0:
            dump(Pb, ['junk'], 1024)
            dump(aT[:, :, T:TT].rearrange("p a b -> p (a b)") if False else aT[:, 0, T:TT], [('aT', 0, NG)], NS)
        bkp = bank()
        pbp = bkp[0][:, :].bitcast(BF16)
        for j in range(8):
            K.op('pe', lambda e, j=j: e.transpose(out=pbp[:, j * NS:(j + 1) * NS], in_=Pb[:, j * 128:(j + 1) * 128], identity=ident[0:NS, 0:NS]),
                 r=['junk', 'ident'], w=[bkp[1]], inc=(j == 7))
        K.op('dve', lambda e: e.tensor_copy(out=PTs[:, :, :], in_=pbp[:, 0:8 * NS].rearrange("p (a b) -> p a b", a=8)), r=[bkp[1]], w=['PTs'])
        ob = [(ps[6], ('ps', 6)), (ps[7], ('ps', 7))]
        for s in range(NS):
            K.op('dve', lambda e, s=s: e.tensor_tensor(out=pm[s % 2], in0=PTs[:, :, :], in1=idns[:, s, :].unsqueeze(1).broadcast_to([128, 8, NS]), op=ALU.mult),
                 r=['PTs', 'idns'], w=['pm%d' % (s % 2)])
            vb, vbk = scratch()
            vbv = vb[:, :].rearrange("p (a b) -> p a b", a=2)
            K.dma('pool', vbv, cmv[l, s].rearrange("(a p) n -> p a n", p=128), w=[vbk])
            for h in range(4):
                for mc in range(2):
                    first = (s == 0 and h % 2 == 0 and mc == 0)
                    K.op('pe', lambda e, h=h, mc=mc, s=s, first=first, vbv=vbv: e.matmul(
                        out=ob[h // 2][0][0:NS, (h % 2) * 256:(h % 2) * 256 + 256], lhsT=pm[s % 2][:, h * 2 + mc, :], rhs=vbv[:, mc, h * 256:(h + 1) * 256],
                        start=first, stop=(s == NS - 1 and mc == 1 and h % 2 == 1), skip_group_check=True),
                        r=['pm%d' % (s % 2), vbk], w=[ob[h // 2][1]], inc=(mc == 1))
        osb = junk[0:NS, :]
        for b2 in range(2):
            K.op('act', lambda e, b2=b2: e.activation(out=osb[:, b2 * 512:(b2 + 1) * 512], in_=ob[b2][0][0:NS, :], func=AF.Copy), r=[ob[b2][1]], w=['junk'])
        if dstage is not None and l == 0:
            dump(osb, ['junk'], 1024)
        bko = bank()
        pbo = bko[0][:, :].bitcast(BF16)
        for c in range(8):
            K.op('pe', lambda e, c=c: e.transpose(out=pbo[:, c * NS:(c + 1) * NS], in_=osb[:, c * 128:(c + 1) * 128], identity=ident[0:NS, 0:NS]),
                 r=['junk', 'ident'], w=[bko[1]], inc=(c == 7))
        K.op('dve', lambda e: e.tensor_copy(out=aT[:, :, T:TT], in_=pbo[:, 0:8 * NS].rearrange("p (a b) -> p a b", a=8)),
             r=[bko[1]], w=[('aT', hh, NG) for hh in range(4)])

    def retention(seg):
        norm_T(0)
        tb16 = tabs[:, :].bitcast(BF16)
        cosT = tb16[:, 0:T]; sinT = tb16[:, T:2 * T]
        K.dma('pool', cosT, c_rcosT[:, seg * T:(seg + 1) * T], w=['tabs'])
        K.dma('pool', sinT, c_rsinT[:, seg * T:(seg + 1) * T], w=['tabs'])
        lreset()
        dec = la('rtab', [128, 512], BF16); gq = la('rtab', [128, 512], BF16)
        gk = la('rtab', [128, 12]); gkA = la('rtab', [128, 4 * NT]); coef = la('rtab', [128, 32])
        rsT = la('rtab', [128, 2]); id16 = la('rtab', [NS, NS])
        K.dma('pool', dec[:], c_dec[:, :], w=['rtab']); K.dma('pool', gq[:], c_gq[:, :], w=['rtab'])
        K.dma('sp', gk[:], c_gk[:, :], w=['rtab']); K.dma('sp', gkA[:], c_gkA[:, :], w=['rtab'])
        K.dma('sp', coef[:], c_coef[:, :], w=['rtab']); K.dma('sp', rsT[:], c_rsT[:, :], w=['rtab'])
        K.dma('sp', id16[:], c_id16[:, :], w=['rtab'])
        Sf = la('Sf', [128, 2, 512]); Sbf = la('Sbf', [128, 2, 512], BF16)
        Ss = la('Ss', [128, 2, 512]); Ssb = la('Ssb', [128, 2, 512], BF16)
        stat = la('stat', [128, 16])
        for k_ in ('stat2', 'stat3', 'stat4'):
            K.alias[k_] = K.alias['stat']
        qT = big[:, 0:2 * TT].rearrange("p (a b) -> p a b", a=2)
        kT = big[:, 2 * TT:4 * TT].rearrange("p (a b) -> p a b", a=2)
        vt = big[:, 4 * TT:4 * TT + (NT + 1) * 512].rearrange("p (a b) -> p a b", b=512)
        o_ = 4 * TT + (NT + 1) * 512
        kdec = big[:, o_:o_ + 256].rearrange("p (a b) -> p a b", a=1); o_ += 256
        ATb = big[:, o_:o_ + 256].rearrange("p (a b) -> p a b", a=2); o_ += 256
        qdt = big[:, o_:o_ + 512].rearrange("p (i a b) -> p i a b", i=2, a=2); o_ += 512
        yT = big[:, o_:o_ + 512].rearrange("p (i a b) -> p i a b", i=1, a=4); o_ += 512
        assert o_ <= BIGE, o_
        a_ = smal[:, 0:512]; b_ = smal[:, 512:1024]

        def proj_rope(wv, wkey, wc0, dst, dkey):
            for (n, c0, ncol) in tokgroups():
                pp = banks(2)
                for dc in range(2):
                    for kc in range(8):
                        K.op('pe', lambda e, p=pp[dc][0], kc=kc, c0=c0, ncol=ncol, dc=dc: e.matmul(
                            out=p[:, 0:ncol], lhsT=wv[:, kc, wc0 + dc * 128:wc0 + dc * 128 + 128], rhs=hT[:, kc, c0:c0 + ncol],
                            start=(kc == 0), stop=(kc == 7)), r=[wkey, ('hT', n)], w=[pp[dc][1]], inc=(kc == 7))
                p1, k1 = pp[0]; p2, k2 = pp[1]
                for half in range(2):
                    pa, ka, pb_, kb = (p1, k1, p2, k2) if half == 0 else (p2, k2, p1, k1)
                    if n < NG:
                        K.op('dve', lambda e, pa=pa, c0=c0, ncol=ncol: e.tensor_tensor(out=a_[:, 0:ncol], in0=pa[:, 0:ncol], in1=cosT[:, c0:c0 + ncol], op=ALU.mult),
                             r=[ka, 'tabs'], w=[('smal', 0)])
                        K.op('dve', lambda e, pb_=pb_, c0=c0, ncol=ncol: e.tensor_tensor(out=b_[:, 0:ncol], in0=pb_[:, 0:ncol], in1=sinT[:, c0:c0 + ncol], op=ALU.mult),
                             r=[kb, 'tabs'], w=[('smal', 1)])
                    else:
                        K.op('dve', lambda e, pa=pa, ncol=ncol: e.tensor_scalar(out=a_[:, 0:ncol], in0=pa[:, 0:ncol], scalar1=rsT[:, 0:1], scalar2=None, op0=ALU.mult),
                             r=[ka, 'rtab'], w=[('smal', 0)])
                        K.op('dve', lambda e, pb_=pb_, ncol=ncol: e.tensor_scalar(out=b_[:, 0:ncol], in0=pb_[:, 0:ncol], scalar1=rsT[:, 1:2], scalar2=None, op0=ALU.mult),
                             r=[kb, 'rtab'], w=[('smal', 1)])
                    K.op('dve', lambda e, half=half, c0=c0, ncol=ncol: e.tensor_tensor(out=dst[:, half, c0:c0 + ncol], in0=a_[:, 0:ncol], in1=b_[:, 0:ncol],
                                                                                    op=(ALU.subtract if half == 0 else ALU.add)),
                         r=[('smal', 0), ('smal', 1)], w=[(dkey, n)])

        def proj_v(wv, wkey):
            for (t, c0, nrow) in toktiles():
                p, pk = bank()
                for kc in range(8):
                    K.op('pe', lambda e, p=p, kc=kc, c0=c0, nrow=nrow: e.matmul(out=p[0:nrow, :], lhsT=hT[:, kc, c0:c0 + nrow], rhs=wv[:, kc, :],
                                                                               start=(kc == 0), stop=(kc == 7)),
                         r=[wkey, ('hT', min(t // TPG, NG))], w=[pk], inc=(kc == 7))
                K.op('act', lambda e, p=p, t=t, nrow=nrow: e.activation(out=vt[0:nrow, t, :], in_=p[0:nrow, :], func=AF.Copy), r=[pk], w=[('vt', t)])

        def kdec_tile(h, t, scal, slot):
            bk, bkk = bank()
            pb = bk[:, :].bitcast(BF16)
            for dc in range(2):
                K.op('pe', lambda e, dc=dc, t=t: e.transpose(out=pb[:, dc * 128:(dc + 1) * 128], in_=kT[:, dc, t * 128:(t + 1) * 128], identity=ident[:]),
                     r=[('kT', t // TPG), 'ident'], w=[bkk], inc=(dc == 1))
            K.op('dve', lambda e, slot=slot: e.tensor_scalar(out=kdec[:, slot, :], in0=pb[:, 0:256], scalar1=scal, scalar2=None, op0=ALU.mult),
                 r=[bkk, 'rtab'], w=[('kdec', slot)])

        RB = [(ps[6], ('ps', 6)), (ps[7], ('ps', 7))]
        def phaseB(h):
            if seg == 0:
                K.op('dve', lambda e: e.memset(Sf[:, :, :], 0.0), w=['Sf'])
            else:
                K.dma('sp', Sf[:, :, :], st_ret[h * 256:(h + 1) * 256, :].rearrange("(a p) n -> p a n", p=128), r=[('st_ret', h)], w=['Sf'])
            K.op('act', lambda e: e.activation(out=Sbf[:, :, :], in_=Sf[:, :, :], func=AF.Copy), r=['Sf'], w=['Sbf'])
            wq_, wqk = wload(ret_w_in, 0, 8, h * 256, 256)
            proj_rope(wq_, wqk, 0, qT, 'qT')
            wk_, wkk = wload(ret_w_in, 0, 8, D + h * 256, 256)
            proj_rope(wk_, wkk, 0, kT, 'kT')
            wv_, wvk = wload(ret_w_in, 0, 8, 2 * D + h * 512, 512)
            proj_v(wv_, wvk)

            def gnorm(p, pk, nrow, t):
                K.op('dve', lambda e: e.bn_stats(out=stat[0:nrow, 0:6], in_=p[0:nrow, :]), r=[pk], w=['stat'])
                K.op('dve', lambda e: e.bn_aggr(out=stat[0:nrow, 8:10], in_=stat[0:nrow, 0:6]), r=['stat'], w=['stat2'])
                K.op('act', lambda e: e.activation(out=stat[0:nrow, 10:11], in_=stat[0:nrow, 9:10], func=AF.Sqrt, bias=epsc[0:nrow, 0:1]), r=['stat2', 'epsc'], w=['stat3'])
                K.op('dve', lambda e: e.reciprocal(out=stat[0:nrow, 11:12], in_=stat[0:nrow, 10:11]), r=['stat3'], w=['stat4'])
                K.op('dve', lambda e: e.tensor_scalar(out=vt[0:nrow, t, :], in0=p[0:nrow, :], scalar1=stat[0:nrow, 8:9], scalar2=stat[0:nrow, 11:12],
                                                      op0=ALU.subtract, op1=ALU.mult), r=[pk, 'stat2', 'stat4'], w=[('vt', t)])

            for t in range(NT):
                sl = t % 2
                tc0 = t * 128
                kdec_tile(h, t, gk[:, h:h + 1], 0)
                pi, pik = bank()
                for dc in range(2):
                    K.op('pe', lambda e, dc=dc, tc0=tc0, pi=pi: e.matmul(out=pi[:, 0:128], lhsT=kT[:, dc, tc0:tc0 + 128], rhs=qT[:, dc, tc0:tc0 + 128],
                                                                        start=(dc == 0), stop=(dc == 1)),
                         r=[('kT', t // TPG), ('qT', t // TPG)], w=[pik], inc=(dc == 1))
                K.op('dve', lambda e, sl=sl, pi=pi, h=h: e.tensor_tensor(out=ATb[:, sl, :], in0=pi[:, 0:128], in1=dec[:, h * 128:(h + 1) * 128], op=ALU.mult),
                     r=[pik, 'rtab'], w=[('AT', sl)])
                K.op('dve', lambda e, sl=sl, tc0=tc0, h=h: e.tensor_tensor(out=qdt[:, sl, :, :], in0=qT[:, :, tc0:tc0 + 128],
                                                                          in1=gq[:, h * 128:(h + 1) * 128].unsqueeze(1).broadcast_to([128, 2, 128]), op=ALU.mult),
                     r=[('qT', t // TPG), 'rtab'], w=[('qdt', sl)])
                po, pok = bank()
                K.op('pe', lambda e, sl=sl, t=t, po=po: e.matmul(out=po[:, :], lhsT=ATb[:, sl, :], rhs=vt[:, t, :], start=True, stop=False),
                     r=[('AT', sl), ('vt', t)], w=[pok], inc=False)
                for dc in range(2):
                    K.op('pe', lambda e, sl=sl, dc=dc, po=po: e.matmul(out=po[:, :], lhsT=qdt[:, sl, dc, :], rhs=Sbf[:, dc, :], start=False, stop=(dc == 1)),
                         r=[('qdt', sl), 'Sbf'], w=[pok], inc=(dc == 1))
                pkv = banks(2)
                for dc in range(2):
                    K.op('pe', lambda e, sl=sl, dc=dc, t=t, p=pkv[dc][0]: e.matmul(out=p[:, :], lhsT=kdec[:, 0, dc * 128:(dc + 1) * 128], rhs=vt[:, t, :],
                                                                                  start=True, stop=True),
                         r=[('kdec', 0), ('vt', t)], w=[pkv[dc][1]], inc=True)
                for dc in range(2):
                    K.op('dve', lambda e, dc=dc, p=pkv[dc][0], h=h: e.scalar_tensor_tensor(out=Sf[:, dc, :], in0=Sf[:, dc, :], scalar=gk[:, 4 + h:5 + h], in1=p[:, :],
                                                                                          op0=ALU.mult, op1=ALU.add), r=['Sf', 'rtab', pkv[dc][1]], w=['Sf'])
                K.op('act', lambda e: e.activation(out=Sbf[:, :, :], in_=Sf[:, :, :], func=AF.Copy), r=['Sf'], w=['Sbf'])
                gnorm(po, pok, 128, t)
            if seg == NSEG - 1:
                K.dma('sp', o_retp[h].rearrange("(a p) n -> p a n", p=128), Sf[:, :, :], r=['Sf'])
            else:
                K.dma('sp', st_ret[h * 256:(h + 1) * 256, :].rearrange("(a p) n -> p a n", p=128), Sf[:, :, :], r=['Sf'], w=[('st_ret', h)])

            bks, bksk = bank()
            pbs = bks[:, :].bitcast(BF16)
            for dc in range(2):
                K.op('pe', lambda e, dc=dc: e.transpose(out=pbs[0:NS, dc * 128:(dc + 1) * 128], in_=kT[:, dc, T:TT], identity=ident[:]),
                     r=[('kT', NG), 'ident'], w=[bksk], inc=(dc == 1))
            kst = junk[0:NS, 0:256]
            K.op('dve', lambda e: e.tensor_copy(out=kst, in_=pbs[0:NS, 0:256]), r=[bksk], w=['junk'])
            qds_, qdsk = scratch()
            qds = qds_[:, 0:2 * NS * NS].rearrange("p (c a b) -> p c a b", c=2, a=NS)
            K.op('dve', lambda e: e.tensor_tensor(out=qds[:, :, :, :].rearrange("p c a b -> p (c a) b") if False else qds[:, 0, :, :],
                                                  in0=qT[:, 0, T:TT].unsqueeze(1).broadcast_to([128, NS, NS]), in1=idns[:, :, :], op=ALU.mult),
                 r=[('qT', NG), 'idns'], w=[qdsk])
            K.op('dve', lambda e: e.tensor_tensor(out=qds[:, 1, :, :], in0=qT[:, 1, T:TT].unsqueeze(1).broadcast_to([128, NS, NS]), in1=idns[:, :, :], op=ALU.mult),
                 r=[('qT', NG), 'idns'], w=[qdsk])
            kms = junk[0:NS, 256:512]
            for s_ in range(NS):
                K.dma('sp', Ss[:, :, :], sret[s_, h].rearrange("(a p) n -> p a n", p=128), w=['Ss'])
                K.op('dve', lambda e, s_=s_: e.tensor_scalar(out=kms, in0=kst, scalar1=id16[:, s_:s_ + 1], scalar2=None, op0=ALU.mult),
                     r=['junk', 'rtab'], w=['kms'])
                pkv = banks(2)
                for dc in range(2):
                    K.op('pe', lambda e, dc=dc, p=pkv[dc][0]: e.matmul(out=p[:, :], lhsT=kms[:, dc * 128:(dc + 1) * 128], rhs=vt[0:NS, NT, :], start=True, stop=True),
                         r=['kms', ('vt', NT)], w=[pkv[dc][1]], inc=True)
                for dc in range(2):
                    K.op('dve', lambda e, dc=dc, p=pkv[dc][0], h=h: e.scalar_tensor_tensor(out=Ss[:, dc, :], in0=Ss[:, dc, :], scalar=gk[:, 8 + h:9 + h], in1=p[:, :],
                                                                                          op0=ALU.mult, op1=ALU.add), r=['Ss', 'rtab', pkv[dc][1]], w=['Ss'])
                if seg == 0:
                    K.dma('sp', o_rets[s_, h].rearrange("(a p) n -> p a n", p=128), Ss[:, :, :], r=['Ss'])
                K.op('act', lambda e: e.activation(out=Ssb[:, :, :], in_=Ss[:, :, :], func=AF.Copy), r=['Ss'], w=['Ssb'])
                for dc in range(2):
                    K.op('pe', lambda e, dc=dc, s_=s_: e.matmul(out=RB[0][0][0:NS, :], lhsT=qds[:, dc, s_, :], rhs=Ssb[:, dc, :],
                                                                start=(s_ == 0 and dc == 0), stop=(s_ == NS - 1 and dc == 1), skip_group_check=True),
                         r=[qdsk, 'Ssb'], w=[RB[0][1]], inc=(dc == 1))
            gnorm(RB[0][0], RB[0][1], NS, NT)

            wg_, wgk = wload(ret_w_in, 0, 8, 4 * D + h * 512, 512)
            wo_, wok = wload(ret_w_out, h * 512, 4, 0, D)
            for (t, c0, nrow) in toktiles():
                p, pk = bank()
                for kc in range(8):
                    K.op('pe', lambda e, p=p, kc=kc, c0=c0, nrow=nrow: e.matmul(out=p[0:nrow, :], lhsT=hT[:, kc, c0:c0 + nrow], rhs=wg_[:, kc, :],
                                                                               start=(kc == 0), stop=(kc == 7)),
                         r=[wgk, ('hT', min(t // TPG, NG))], w=[pk], inc=(kc == 7))
                sg, sgk = scratch()
                K.op('act', lambda e, p=p, nrow=nrow, sg=sg: e.activation(out=sg[0:nrow, 0:512], in_=p[0:nrow, :], func=AF.Silu), r=[pk], w=[sgk])
                K.op('dve', lambda e, nrow=nrow, sg=sg, t=t: e.tensor_tensor(out=vt[0:nrow, t, :], in0=vt[0:nrow, t, :], in1=sg[0:nrow, 0:512], op=ALU.mult),
                     r=[sgk, ('vt', t)], w=[('vt', t)])
                bk, bkk = bank()
                pb = bk[:, :].bitcast(BF16)
                for j in range(4):
                    K.op('pe', lambda e, j=j, t=t, nrow=nrow: e.transpose(out=pb[:, j * 128:j * 128 + nrow], in_=vt[0:nrow, t, j * 128:(j + 1) * 128],
                                                                        identity=ident[0:nrow, 0:nrow]), r=[('vt', t), 'ident'], w=[bkk], inc=(j == 3))
                ys = 0
                K.op('act', lambda e, ys=ys, nrow=nrow: e.activation(out=yT[:, ys, :, 0:nrow], in_=pb[:, 0:512].rearrange("p (a b) -> p a b", a=4)[:, :, 0:nrow],
                                                                    func=AF.Copy), r=[bkk], w=[('yT', ys)])
                for half in range(2):
                    p2, p2k = bank()
                    for j in range(4):
                        K.op('pe', lambda e, p2=p2, j=j, ys=ys, nrow=nrow, half=half: e.matmul(out=p2[0:nrow, :], lhsT=yT[:, ys, j, 0:nrow],
                                                                                            rhs=wo_[:, j, half * 512:(half + 1) * 512], start=(j == 0), stop=(j == 3)),
                             r=[wok, ('yT', ys)], w=[p2k], inc=(j == 3))
                    resid_add(t, nrow, half, p2, p2k)

        for h in range(4):
            phaseB(h)

    def conv_layer(seg):
        norm_T(6)
        lreset()
        wc = la('cvtab', [128, 24]); selc = la('cvtab', [128, 8])
        for j in range(3):
            K.dma('sp', wc[:, j * 8:(j + 1) * 8], conv_w[j:j + 1, :].rearrange("o (c p) -> p (o c)", p=128), w=['cvtab'], allow_slow_non_contiguous=True)
        bg01 = la('bg01', [128, 8, 2]); cl = la('cl', [128, 8, 2]); bufs = la('cvbufs', [128, 8, 2, NS]); cins = la('cins', [128, 8, NS])
        halo = la('halo', [128, 16]); hst = la('hst', [128, 16]); afix = la('afix', [128, 8, 2]); ufix = la('ufix', [128, 8, 2], BF16)
        uT = big[:, 0:8 * TT].rearrange("p (a b) -> p a b", a=8)
        o_ = 8 * TT + (8 * TT) % 2
        cinb = big[:, o_:o_ + 2 * (GS + 2)].bitcast(F32)
        a_ = smal[:, 0:512]; b_ = smal[:, 512:1024]
        if seg == 0:
            K.op('dve', lambda e: e.memset(halo[:, :], 0.0), w=['halo'])
        else:
            K.dma('sp', halo[:, :], st_c[:, :], r=['st_c'], w=['halo'])
        hv = halo[:, :].rearrange("p (c j) -> p c j", j=2)
        for c in range(8):
            for j in range(2):
                K.dma('sp', bufs[:, c, j, :], sconv[:, j, c * 128:(c + 1) * 128].rearrange("s p -> p s"), w=['cvbufs'], allow_slow_non_contiguous=True)

        def chunk(c):
            i = st['w']; st['w'] = (i + 1) % NW
            wv = wslot[i][:, 0:8 * 3 * 128].rearrange("p (k a f) -> p k a f", k=8, a=3)
            for a3 in range(3):
                K.dma('pool', wv[:, :, a3, :], conv_w_in[:, a3 * D + c * 128:a3 * D + (c + 1) * 128].rearrange("(k p) f -> p k f", p=128), w=[('w', i)])
            wkey = ('w', i)
            K.op('dve', lambda e: e.tensor_copy(out=cinb[:, 0:2], in_=hv[:, c, :]), r=['halo'], w=['cinb'])
            for (n, c0, ncol) in tokgroups():
                pp = banks(3)
                for a3 in range(3):
                    for kc in range(8):
                        K.op('pe', lambda e, p=pp[a3][0], kc=kc, a3=a3, c0=c0, ncol=ncol: e.matmul(
                            out=p[:, 0:ncol], lhsT=wv[:, kc, a3, :], rhs=hT[:, kc, c0:c0 + ncol], start=(kc == 0), stop=(kc == 7)),
                            r=[wkey, ('hT', n)], w=[pp[a3][1]], inc=(kc == 7))
                (pbg, kbg), (pcg, kcg), (pz, kz) = pp
                K.op('act', lambda e, pz=pz, ncol=ncol: e.activation(out=a_[:, 0:ncol], in_=pz[:, 0:ncol], func=AF.Copy), r=[kz], w=[('smal', 0)])
                if n < NG:
                    K.op('dve', lambda e, pcg=pcg, c0=c0, ncol=ncol: e.tensor_tensor(out=cinb[:, 2:2 + ncol], in0=pcg[:, 0:ncol], in1=a_[:, 0:ncol], op=ALU.mult),
                         r=[kcg, ('smal', 0)], w=['cinb'])
                    if n == 0:
                        K.op('act', lambda e, pbg=pbg: e.activation(out=bg01[:, c, :], in_=pbg[:, 0:2], func=AF.Copy), r=[kbg], w=['bg01'])
                    K.op('dve', lambda e, c0=c0, ncol=ncol: e.tensor_scalar(out=b_[:, 0:ncol], in0=cinb[:, 2:2 + ncol], scalar1=wc[:, 16 + c:17 + c], scalar2=None, op0=ALU.mult),
                         r=['cinb', 'cvtab'], w=[('smal', 1)])
                    for j in (1, 0):
                        K.op('dve', lambda e, c0=c0, ncol=ncol, j=j: e.scalar_tensor_tensor(out=b_[:, 0:ncol], in0=cinb[:, j:j + ncol], scalar=wc[:, j * 8 + c:j * 8 + c + 1],
                                                                                      in1=b_[:, 0:ncol], op0=ALU.mult, op1=ALU.add), r=['cinb', 'cvtab', ('smal', 1)], w=[('smal', 1)])
                else:
                    K.op('dve', lambda e, pcg=pcg: e.tensor_tensor(out=cins[:, c, :], in0=pcg[:, 0:NS], in1=a_[:, 0:NS], op=ALU.mult), r=[kcg, ('smal', 0)], w=['cins'])
                    K.op('dve', lambda e: e.tensor_scalar(out=b_[:, 0:NS], in0=cins[:, c, :], scalar1=wc[:, 16 + c:17 + c], scalar2=None, op0=ALU.mult),
                         r=['cins', 'cvtab'], w=[('smal', 1)])
                    for j in (1, 0):
                        K.op('dve', lambda e, j=j: e.scalar_tensor_tensor(out=b_[:, 0:NS], in0=bufs[:, c, j, :], scalar=wc[:, j * 8 + c:j * 8 + c + 1], in1=b_[:, 0:NS],
                                                                        op0=ALU.mult, op1=ALU.add), r=['cvbufs', 'cvtab', ('smal', 1)], w=[('smal', 1)])
                K.op('dve', lambda e, pbg=pbg, c0=c0, ncol=ncol: e.tensor_tensor(out=uT[:, c, c0:c0 + ncol], in0=pbg[:, 0:ncol], in1=b_[:, 0:ncol], op=ALU.mult),
                     r=[kbg, ('smal', 1)], w=[('aT', c // 2, n)])
                if n < NG:
                    if n == NG - 1:
                        K.op('dve', lambda e: e.tensor_copy(out=cl[:, c, :], in_=cinb[:, GS:GS + 2]), r=['cinb'], w=['cl'])
                    K.op('dve', lambda e: e.tensor_copy(out=cinb[:, 0:2], in_=cinb[:, GS:GS + 2]), r=['cinb'], w=['cinb'])

        for c in range(8):
            chunk(c)
        if seg == NSEG - 1:
            for c in range(8):
                K.dma('sp', o_convp[:, c * 128:(c + 1) * 128].rearrange("j p -> p j"), cl[:, c, :], r=['cl'], allow_slow_non_contiguous=True)
        else:
            K.dma('sp', st_c[:, :], cl[:, :, :].rearrange("p c j -> p (c j)"), r=['cl'], w=['st_c'])
        if seg == 0:
            K.dma('sp', o_convs[:, 0, :], sconv[:, 1, :])
            for c in range(8):
                K.dma('sp', o_convs[:, 1, c * 128:(c + 1) * 128].rearrange("s p -> p s"), cins[:, c, :], r=['cins'], allow_slow_non_contiguous=True)
        wvs = [wload(conv_w_out, kh * 512, 4, 0, D) for kh in range(2)]

        def outp(rows_fn, nrow, t, keys):
            for half in range(2):
                p, pk = bank()
                for k in range(8):
                    wv, wk = wvs[k // 4]
                    K.op('pe', lambda e, p=p, k=k, wv=wv, half=half: e.matmul(out=p[0:nrow, :], lhsT=rows_fn(k), rhs=wv[:, k % 4, half * 512:(half + 1) * 512],
                                                                             start=(k == 0), stop=(k == 7)), r=[wk] + keys, w=[pk], inc=(k == 7))
                resid_add(t, nrow, half, p, pk)
        for (t, c0, nrow) in toktiles():
            outp(lambda k, c0=c0, nrow=nrow: uT[:, k, c0:c0 + nrow], nrow, t, [('aT', hh, min(t // TPG, NG)) for hh in range(4)])

    def pool_layer(seg):
        sumsq_rstd([(x[:, t, :], ('x', t)) for t in range(NT + 1)], NT + 1, 'rstd')
        Am = tabs[:, 0:1536].rearrange("p (a b) -> p a b", a=12)
        K.dma('sp', tabs[:, 0:1536], c_poolA[:, :], w=['tabs'])
        lreset()
        pB = la('pltab', [128, 8 * NS]); pC = la('pltab', [NS, 4 * NS]); selc = la('pltab', [128, 8])
        K.dma('sp', pB[:], c_poolB[:, :], w=['pltab']); K.dma('sp', pC[:], c_poolC[:, :], w=['pltab']); pass
        dT = big[:, 0:8 * TT].rearrange("p (a b) -> p a b", a=8)
        f32slots = [scr[i][:, :].bitcast(F32) for i in range(4)]
        gb, gbk = f32slots[3], ('scr', 3)
        K.dma('sp', gb, norm_g[9:10, :].partition_broadcast(128).rearrange("p o n -> p (o n)"), w=[gbk])
        wg_, wgk = wload(pool_w, 0, 8, 0, 256)
        scb = junk[:, :]
        K.dma('pool', scb, pool_scale[0:1, :].partition_broadcast(128).rearrange("p o n -> p (o n)"), w=['junk'])
        for r8 in range(8):
            K.op('dve', lambda e, r8=r8: e.tensor_tensor(out=wg_[:, r8, :], in0=wg_[:, r8, :], in1=scb[:, (r8 // 2) * 256:(r8 // 2 + 1) * 256], op=ALU.mult),
                 r=[wgk, 'junk'], w=[wgk])

        def xnf(t, slot):
            K.op('dve', lambda e: e.tensor_scalar(out=f32slots[slot][:, :], in0=x[:, t, :], scalar1=rstd[:, t:t + 1], scalar2=None, op0=ALU.mult),
                 r=[('x', t), 'rstd'], w=[('scr', slot)])

        def pooled_tile(t, cur, prv, first):
            bk = banks(2)
            for c in range(8):
                gi = c // 2
                p = bk[c // 4][0]
                col = (c % 4) * 128
                K.op('pe', lambda e, p=p, c=c, gi=gi, col=col: e.matmul(out=p[:, col:col + 128], lhsT=f32slots[cur][:, c * 128:(c + 1) * 128],
                                                                       rhs=Am[:, gi * 3 + (2 if first else 0), :], start=True, stop=False),
                     r=[('scr', cur), 'tabs'], w=[bk[c // 4][1]], inc=False)
                K.op('pe', lambda e, p=p, c=c, gi=gi, col=col: e.matmul(out=p[:, col:col + 128], lhsT=f32slots[prv][:, c * 128:(c + 1) * 128],
                                                                       rhs=Am[:, gi * 3 + 1, :], start=False, stop=True),
                     r=[('scr', prv), 'tabs'], w=[bk[c // 4][1]], inc=True)
            for c in range(8):
                p = bk[c // 4][0]
                col = (c % 4) * 128
                K.op('act', lambda e, p=p, c=c, col=col: e.activation(out=dT[:, c, t * 128:(t + 1) * 128], in_=p[:, col:col + 128], func=AF.Copy,
                                                                     scale=gcols[:, 72 + c:73 + c]), r=[bk[c // 4][1], 'gcols'], w=[('aT', c // 2, t // TPG)])

        K.op('dve', lambda e: e.tensor_scalar(out=f32slots[0][:, :], in0=x[:, NT, :], scalar1=rstd[:, NT:NT + 1], scalar2=None, op0=ALU.mult),
             r=[('x', NT), 'rstd'], w=[('scr', 0)])
        K.op('dve', lambda e: e.tensor_tensor(out=f32slots[0][0:NS, :], in0=f32slots[0][0:NS, :], in1=gb[0:NS, :], op=ALU.mult), r=[('scr', 0), gbk], w=[('scr', 0)])
        if seg == 0:
            K.dma('sp', o_pools[:, 14, :], f32slots[0][0:NS, :], r=[('scr', 0)])
            K.dma('sp', o_pools[:, 0:14, :], spool[:, 1:15, :])
        nh = (NS + 7) // 8
        bk = banks(2)
        for hf in range(nh):
            ns_h = min(8, NS - hf * 8)
            K.dma('sp', f32slots[1 + hf][0:ns_h * 15, :], spool[hf * 8:hf * 8 + ns_h].rearrange("s j n -> (s j) n"), w=[('scr', 1 + hf)])
        for c in range(8):
            gi = c // 2
            p = bk[c // 4][0]
            col = (c % 4) * NS
            for hf in range(nh):
                ns_h = min(8, NS - hf * 8)
                K.op('pe', lambda e, p=p, c=c, gi=gi, col=col, hf=hf, ns_h=ns_h: e.matmul(
                    out=p[:, col:col + NS], lhsT=f32slots[1 + hf][0:ns_h * 15, c * 128:(c + 1) * 128], rhs=pB[0:ns_h * 15, (hf * 4 + gi) * NS:(hf * 4 + gi + 1) * NS],
                    start=(hf == 0), stop=False), r=[('scr', 1 + hf), 'pltab'], w=[bk[c // 4][1]], inc=False)
            K.op('pe', lambda e, p=p, c=c, gi=gi, col=col: e.matmul(out=p[:, col:col + NS], lhsT=f32slots[0][0:NS, c * 128:(c + 1) * 128], rhs=pC[:, gi * NS:(gi + 1) * NS],
                                                                   start=False, stop=True), r=[('scr', 0), 'pltab'], w=[bk[c // 4][1]], inc=True)
        for c in range(8):
            p = bk[c // 4][0]
            col = (c % 4) * NS
            K.op('act', lambda e, p=p, c=c, col=col: e.activation(out=dT[:, c, T:TT], in_=p[:, col:col + NS], func=AF.Copy), r=[bk[c // 4][1]], w=[('aT', c // 2, NG)])
        K.op('dve', lambda e: e.memset(f32slots[2][:, :], 0.0), w=[('scr', 2)])
        if seg > 0:
            K.dma('sp', f32slots[2][112:128, :], st_p[:, :], r=['st_p'], w=[('scr', 2)])
        for t in range(NT):
            xnf(t, t % 2)
            pooled_tile(t, t % 2, 2 if t == 0 else (t - 1) % 2, (t == 0 and seg == 0))
        last = (NT - 1) % 2
        if seg == NSEG - 1:
            K.op('dve', lambda e: e.tensor_tensor(out=f32slots[2][:, :], in0=f32slots[last][:, :], in1=gb, op=ALU.mult), r=[('scr', last), gbk], w=[('scr', 2)])
            K.dma('sp', o_poolp[:, :], f32slots[2][113:128, :], r=[('scr', 2)])
        else:
            K.dma('sp', st_p[:, :], f32slots[last][112:128, :], r=[('scr', last)], w=['st_p'])
        for (t, c0, nrow) in toktiles():
            for half in range(2):
                p, pk = bank()
                for g2 in range(2):
                    g = half * 2 + g2
                    for kc in range(2):
                        K.op('pe', lambda e, p=p, g=g, g2=g2, kc=kc, c0=c0, nrow=nrow: e.matmul(out=p[0:nrow, g2 * 256:(g2 + 1) * 256], lhsT=dT[:, g * 2 + kc, c0:c0 + nrow],
                                                                                         rhs=wg_[:, g * 2 + kc, :], start=(g2 == 0 and kc == 0), stop=(g2 == 1 and kc == 1),
                                                                                         skip_group_check=True),
                             r=[wgk] + [('aT', hh, min(t // TPG, NG)) for hh in range(4)], w=[pk], inc=(g2 == 1 and kc == 1))
                resid_add(t, nrow, half, p, pk)

    def final_norm(seg):
        sumsq_rstd([(x[:, t, :], ('x', t)) for t in range(NT + 1)], NT + 1, 'rstd')
        gb_, gbk = scratch()
        gb = gb_[:, :].bitcast(F32)
        K.dma('sp', gb, final_norm_g[0:1, :].partition_broadcast(128).rearrange("p o n -> p (o n)"), w=[gbk])
        for (t, c0, nrow) in toktiles():
            K.op('dve', lambda e, t=t, nrow=nrow: e.scalar_tensor_tensor(out=x[0:nrow, t, :], in0=x[0:nrow, t, :], scalar=rstd[0:nrow, t:t + 1], in1=gb[0:nrow, :],
                                                                       op0=ALU.mult, op1=ALU.mult), r=[('x', t), 'rstd', gbk], w=[('x', t)])
            dst = yp[seg * T + t * 128:seg * T + (t + 1) * 128, :] if t < NT else ys[:, :]
            if t < NT or seg == 0:
                K.dma('sp', dst, x[0:nrow, t, :], r=[('x', t)])

    def swa_layer(seg):
        norm_T(3)
        lreset()
        swcs = la('swtab', [128, NT + 1, 16]); msk = la('swtab', [128, 3, 128], BF16); selc = la('swtab', [128, 8])
        esk = la('esk', [128, 16]); ksv = la('ksv', [128, 512]); ktn = la('ktn', [128, 4, NS], BF16)
        swqT = la('swqT', [128, 2, 8, 128], BF16)
        K.op('dve', lambda e: e.memset(swqT[:, :, :, :], 0.0), w=['swqT'])
        K.dma('sp', swcs[:, 0:NT, :], c_swcs[:, seg * NT:(seg + 1) * NT, :], w=['swtab'])
        K.dma('sp', swcs[:, NT, :], c_swcs[:, NSEG * NT, :], w=['swtab'])
        K.dma('pool', msk[:, :, :].rearrange("p a b -> p (a b)"), c_mask[:, :], w=['swtab'])
        K.dma('sp', esk[:], swa_sinks[0:1, :].partition_broadcast(128).rearrange("p o n -> p (o n)"), w=['esk'])
        K.op('act', lambda e: e.activation(out=esk[:], in_=esk[:], func=AF.Exp), r=['esk'], w=['esk'])
        KW = T + 128
        KTd = big[:, 0:4 * KW].rearrange("p (a b) -> p a b", a=4)
        Vd = big[:, 4 * KW:4 * KW + (NT + 1) * 512].rearrange("p (t k u d) -> p t k u d", k=4, u=2, d=64)
        assert 4 * KW + (NT + 1) * 512 <= BIGE
        R5 = (ps[5], ('ps', 5)); R6 = (ps[6], ('ps', 6)); R7 = (ps[7], ('ps', 7))
        tmpv = smal[:, 512:1024]

        def rope_tm(buf, H, nrow, idx, bkey):
            b3 = buf.rearrange("p (h d) -> p h d", d=64)
            x1 = b3[0:nrow, :, 0:8]; x2 = b3[0:nrow, :, 8:16]
            cs = swcs[0:nrow, idx, 0:8].unsqueeze(1).broadcast_to([nrow, H, 8])
            sn = swcs[0:nrow, idx, 8:16].unsqueeze(1).broadcast_to([nrow, H, 8])
            tv = [tmpv[0:nrow, i * 128:i * 128 + H * 8].rearrange("p (h d) -> p h d", d=8) for i in range(4)]
            K.op('dve', lambda e: e.tensor_tensor(out=tv[0], in0=x1, in1=cs, op=ALU.mult), r=[bkey, 'swtab'], w=[('smal', 1)])
            K.op('dve', lambda e: e.tensor_tensor(out=tv[1], in0=x2, in1=sn, op=ALU.mult), r=[bkey, 'swtab'], w=[('smal', 1)])
            K.op('dve', lambda e: e.tensor_tensor(out=tv[2], in0=x2, in1=cs, op=ALU.mult), r=[bkey, 'swtab'], w=[('smal', 1)])
            K.op('dve', lambda e: e.tensor_tensor(out=tv[3], in0=x1, in1=sn, op=ALU.mult), r=[bkey, 'swtab'], w=[('smal', 1)])
            K.op('dve', lambda e: e.tensor_tensor(out=x1, in0=tv[0], in1=tv[1], op=ALU.subtract), r=[('smal', 1)], w=[bkey])
            K.op('dve', lambda e: e.tensor_tensor(out=x2, in0=tv[2], in1=tv[3], op=ALU.add), r=[('smal', 1)], w=[bkey])

        def kv_store(kvf, kvkey, nrow, slot, ktdst, ktkey='KTd'):
            k4 = kvf[0:nrow, 0:256].rearrange("p (k d) -> p k d", d=64)
            v4 = kvf[0:nrow, 256:512].rearrange("p (k d) -> p k d", d=64)
            kd = junk[0:nrow, 0:512].rearrange("p (k u d) -> p k u d", k=4, u=2)
            K.op('dve', lambda e: e.tensor_copy(out=kd, in_=k4.unsqueeze(2).broadcast_to([nrow, 4, 2, 64])), r=[kvkey], w=['junk'])
            if slot is not None:
                K.op('dve', lambda e: e.tensor_copy(out=Vd[0:nrow, slot, :, :, :], in_=v4.unsqueeze(2).broadcast_to([nrow, 4, 2, 64])), r=[kvkey], w=[('Vd', slot)])
            bk, bkk = bank()
            pb = bk[:, :].bitcast(BF16)
            for kh in range(4):
                K.op('pe', lambda e, kh=kh: e.transpose(out=pb[:, kh * 128:kh * 128 + nrow], in_=junk[0:nrow, kh * 128:(kh + 1) * 128], identity=ident[0:nrow, 0:nrow]),
                     r=['junk', 'ident'], w=[bkk], inc=(kh == 3))
            K.op('act', lambda e: e.activation(out=ktdst, in_=pb[:, 0:512].rearrange("p (a b) -> p a b", a=4)[:, :, 0:nrow], func=AF.Copy), r=[bkk], w=[ktkey])

        kvf = smal[:, 0:512]
        if seg == 0:
            K.op('dve', lambda e: e.memset(kvf, 0.0), w=[('smal', 0)])
        else:
            K.dma('sp', kvf, st_h[:, :], r=['st_h'], w=[('smal', 0)])
        kv_store(kvf, ('smal', 0), 128, 0, KTd[:, :, 0:128])
        wkv, wkvk = wload(swa_w_qkv, 0, 8, 1024, 512)
        for (t, c0, nrow) in toktiles():
            p, pk = bank()
            for kc in range(8):
                K.op('pe', lambda e, p=p, kc=kc, c0=c0, nrow=nrow: e.matmul(out=p[0:nrow, :], lhsT=hT[:, kc, c0:c0 + nrow], rhs=wkv[:, kc, :], start=(kc == 0), stop=(kc == 7)),
                     r=[wkvk, ('hT', min(t // TPG, NG))], w=[pk], inc=(kc == 7))
            K.op('act', lambda e, p=p, nrow=nrow: e.activation(out=kvf[0:nrow, :], in_=p[0:nrow, :], func=AF.Copy), r=[pk], w=[('smal', 0)])
            rope_tm(kvf[:, 0:256], 4, nrow, t, ('smal', 0))
            if t < NT:
                kv_store(kvf, ('smal', 0), 128, t + 1, KTd[:, :, 128 + t * 128:128 + (t + 1) * 128])
                if t == NT - 1:
                    if seg == NSEG - 1:
                        K.dma('sp', o_swkp[:, :], kvf[:, 0:256], r=[('smal', 0)])
                        K.dma('sp', o_swvp[:, :], kvf[:, 256:512], r=[('smal', 0)])
                    else:
                        K.dma('sp', st_h[:, :], kvf[:, :], r=[('smal', 0)], w=['st_h'])
            else:
                K.op('dve', lambda e: e.tensor_copy(out=ksv[0:NS, :], in_=kvf[0:NS, :]), r=[('smal', 0)], w=['ksv'])
                kv_store(kvf, ('smal', 0), NS, None, ktn[:, :, :], 'ktn')
                if seg == 0:
                    K.dma('sp', o_swks[:, 127, :], ksv[0:NS, 0:256], r=['ksv'])
                    K.dma('sp', o_swvs[:, 127, :], ksv[0:NS, 256:512], r=['ksv'])
                    K.dma('sp', o_swks[:, 0:127, :], swk[:, 1:128, :])
                    K.dma('sp', o_swvs[:, 0:127, :], swv[:, 1:128, :])

        wq = [wload(swa_w_qkv, 0, 8, hf * 512, 512) for hf in range(2)]

        def q_block(t, c0, nrow):
            pp = banks(2)
            for hf in range(2):
                for kc in range(8):
                    K.op('pe', lambda e, p=pp[hf][0], kc=kc, hf=hf: e.matmul(out=p[0:nrow, :], lhsT=hT[:, kc, c0:c0 + nrow], rhs=wq[hf][0][:, kc, :], start=(kc == 0), stop=(kc == 7)),
                         r=[wq[hf][1], ('hT', min(t // TPG, NG))], w=[pp[hf][1]], inc=(kc == 7))
            qf_, qfk = scratch()
            qf = qf_[:, :].bitcast(F32)
            for hf in range(2):
                K.op('act', lambda e, hf=hf: e.activation(out=qf[0:nrow, hf * 512:(hf + 1) * 512], in_=pp[hf][0][0:nrow, :], func=AF.Copy), r=[pp[hf][1]], w=[qfk])
            rope_tm(qf, 16, nrow, t, qfk)
            K.op('dve', lambda e: e.tensor_copy(out=junk[0:nrow, :], in_=qf[0:nrow, :]), r=[qfk], w=['junk'])
            bk, bkk = bank()
            pb = bk[:, :].bitcast(BF16)
            for c in range(8):
                K.op('pe', lambda e, c=c: e.transpose(out=pb[:, c * 128:c * 128 + nrow], in_=junk[0:nrow, c * 128:(c + 1) * 128], identity=ident[0:nrow, 0:nrow]),
                     r=['junk', 'ident'], w=[bkk], inc=(c == 7))
            qtk = 'swqT'
            qTt = swqT[:, :, :, :]
            K.op('act', lambda e: e.activation(out=swqT[0:64, 0, :, 0:nrow], in_=pb[0:64, :].rearrange("p (a b) -> p a b", a=8)[:, :, 0:nrow], func=AF.Copy), r=[bkk], w=[qtk])
            K.op('act', lambda e: e.activation(out=swqT[64:128, 1, :, 0:nrow], in_=pb[64:128, :].rearrange("p (a b) -> p a b", a=8)[:, :, 0:nrow], func=AF.Copy), r=[bkk], w=[qtk])
            return qTt, qtk, qf, qfk

        def attend_block(t, qTt, qtk):
            def per_kh(kh):
                (po, pok), (pp_, ppk) = banks(2)
                for g in range(4):
                    ch = kh * 2 + g // 2; base = (g % 2) * 64
                    K.op('pe', lambda e, g=g, ch=ch, base=base, po=po: e.matmul(out=po[:, g * 128:(g + 1) * 128], lhsT=KTd[:, kh, 128 + t * 128:256 + t * 128],
                                                                               rhs=qTt[:, g % 2, ch, :], start=True, stop=True), r=['KTd', qtk], w=[pok], inc=(g == 3))
                for g in range(4):
                    ch = kh * 2 + g // 2; base = (g % 2) * 64
                    K.op('pe', lambda e, g=g, ch=ch, base=base, pp_=pp_: e.matmul(out=pp_[:, g * 128:(g + 1) * 128], lhsT=KTd[:, kh, t * 128:128 + t * 128],
                                                                                 rhs=qTt[:, g % 2, ch, :], start=True, stop=True), r=['KTd', qtk], w=[ppk], inc=(g == 3))
                E_, Ek = scratch()
                Eo = E_[:, 0:512]; Ep = E_[:, 512:1024]
                K.op('act', lambda e, po=po: e.activation(out=Eo, in_=po[:, :], func=AF.Exp, scale=0.125), r=[pok], w=[Ek])
                K.op('act', lambda e, pp_=pp_: e.activation(out=Ep, in_=pp_[:, :], func=AF.Exp, scale=0.125), r=[ppk], w=[Ek])
                mi = 2 if (t == 0 and seg == 0) else 1
                K.op('dve', lambda e: e.tensor_tensor(out=Eo.rearrange("p (g l) -> p g l", g=4), in0=Eo.rearrange("p (g l) -> p g l", g=4),
                                                      in1=msk[:, 0, :].unsqueeze(1).broadcast_to([128, 4, 128]), op=ALU.mult), r=[Ek, 'swtab'], w=[Ek])
                K.op('dve', lambda e: e.tensor_tensor(out=Ep.rearrange("p (g l) -> p g l", g=4), in0=Ep.rearrange("p (g l) -> p g l", g=4),
                                                      in1=msk[:, mi, :].unsqueeze(1).broadcast_to([128, 4, 128]), op=ALU.mult), r=[Ek, 'swtab'], w=[Ek])
                (pd_, pdk), (pv_, pvk) = banks(2)
                K.op('pe', lambda e, pd_=pd_: e.matmul(out=pd_[:, :], lhsT=ones[:, :], rhs=Eo, start=True, stop=False), r=['ones', Ek], w=[pdk], inc=False)
                K.op('pe', lambda e, pd_=pd_: e.matmul(out=pd_[:, :], lhsT=ones[:, :], rhs=Ep, start=False, stop=True), r=['ones', Ek], w=[pdk], inc=True)
                K.op('pe', lambda e, pv_=pv_: e.matmul(out=pv_[:, :], lhsT=Vd[:, t + 1, kh, :, :].rearrange("p u d -> p (u d)"), rhs=Eo, start=True, stop=False),
                     r=[('Vd', t + 1), Ek], w=[pvk], inc=False)
                K.op('pe', lambda e, pv_=pv_: e.matmul(out=pv_[:, :], lhsT=Vd[:, t, kh, :, :].rearrange("p u d -> p (u d)"), rhs=Ep, start=False, stop=True),
                     r=[('Vd', t), Ek], w=[pvk], inc=True)
                rd = smal[:, 0:512]
                K.op('dve', lambda e, pd_=pd_: e.tensor_tensor(out=rd.rearrange("p (g l) -> p g l", g=4), in0=pd_[:, :].rearrange("p (g l) -> p g l", g=4),
                                                              in1=esk[:, kh * 4:(kh + 1) * 4].unsqueeze(2).broadcast_to([128, 4, 128]), op=ALU.add), r=[pdk, 'esk'], w=[('smal', 0)])
                K.op('dve', lambda e: e.reciprocal(out=rd, in_=rd), r=[('smal', 0)], w=[('smal', 0)])
                for g in range(4):
                    ch = kh * 2 + g // 2; base = (g % 2) * 64
                    K.op('dve', lambda e, g=g, ch=ch, base=base, pv_=pv_: e.tensor_tensor(out=hT[base:base + 64, ch, t * 128:(t + 1) * 128], in0=pv_[base:base + 64, g * 128:(g + 1) * 128],
                                                                                         in1=rd[base:base + 64, g * 128:(g + 1) * 128], op=ALU.mult),
                         r=[pvk, ('smal', 0)], w=[('hT', t // TPG)])
            for kh in range(4):
                per_kh(kh)

        def do_block(t):
            qTt, qtk, _, _ = q_block(t, t * 128, 128)
            attend_block(t, qTt, qtk)

        for t in range(NT):
            do_block(t)

        qTs, qsk, qf, qfk = q_block(NT, T, NS)
        prod_, prk = scratch()
        prod = prod_[:, :].bitcast(F32)
        K.op('dve', lambda e: e.tensor_tensor(out=prod[0:NS, :].rearrange("p (k g d) -> p k g d", k=4, g=4), in0=qf[0:NS, :].rearrange("p (k g d) -> p k g d", k=4, g=4),
                                              in1=ksv[0:NS, 0:256].rearrange("p (k d) -> p k d", d=64).unsqueeze(2).broadcast_to([NS, 4, 4, 64]), op=ALU.mult),
             r=[qfk, 'ksv'], w=[prk])
        snew = tiny[0:NS, 16:32]
        K.op('dve', lambda e: e.tensor_reduce(out=snew, in_=prod[0:NS, :].rearrange("p (h d) -> p h d", d=64), axis=AX.X, op=ALU.add), r=[prk], w=['snew'])
        K.op('act', lambda e: e.activation(out=snew, in_=snew, func=AF.Exp, scale=0.125), r=['snew'], w=['snew'])
        for s_ in range(NS):
            cb_, cbk = scratch()
            kb = cb_[:, 0:256]; vb = cb_[:, 256:512]
            K.dma('pool', kb, swk[s_], w=[cbk]); K.dma('pool', vb, swv[s_], w=[cbk])
            kd = cb_[:, 512:1024].rearrange("p (k u d) -> p k u d", k=4, u=2)
            K.op('dve', lambda e, kb=kb, kd=kd: e.tensor_copy(out=kd, in_=kb.rearrange("p (k d) -> p k d", d=64).unsqueeze(2).broadcast_to([128, 4, 2, 64])), r=[cbk], w=[cbk])
            bk, bkk = bank()
            pb = bk[:, :].bitcast(BF16)
            for kh in range(4):
                K.op('pe', lambda e, kh=kh, cb_=cb_, pb=pb: e.transpose(out=pb[:, kh * 128:(kh + 1) * 128], in_=cb_[:, 512 + kh * 128:512 + (kh + 1) * 128], identity=ident[:]),
                     r=[cbk, 'ident'], w=[bkk], inc=(kh == 3))
            KTs = cb_[:, 1024:1536].rearrange("p (a b) -> p a b", a=4)
            K.op('act', lambda e, KTs=KTs, pb=pb: e.activation(out=KTs, in_=pb[:, 0:512].rearrange("p (a b) -> p a b", a=4), func=AF.Copy), r=[bkk], w=[cbk])
            sc_, sck = bank()
            for hq in range(16):
                kh = hq // 4; g = hq % 4; ch = kh * 2 + g // 2; base = (g % 2) * 64
                K.op('pe', lambda e, hq=hq, kh=kh, ch=ch, base=base, KTs=KTs, sc_=sc_, s_=s_: e.matmul(out=sc_[:, hq:hq + 1], lhsT=KTs[:, kh, :], rhs=qTs[:, (hq % 4) % 2, ch, s_:s_ + 1],
                                                                                                start=True, stop=True), r=[cbk, qsk], w=[sck], inc=(hq == 15))
            Es = cb_[:, 1536:1552]
            K.op('act', lambda e, Es=Es, sc_=sc_: e.activation(out=Es, in_=sc_[:, 0:16], func=AF.Exp, scale=0.125), r=[sck], w=[cbk])
            Em = cb_[:, 1552:1552 + 16 * NS].rearrange("p (h s) -> p h s", h=16)
            K.op('dve', lambda e, Es=Es, Em=Em, s_=s_: e.tensor_tensor(out=Em, in0=Es.unsqueeze(2).broadcast_to([128, 16, NS]), in1=idns[:, s_, :].unsqueeze(1).broadcast_to([128, 16, NS]), op=ALU.mult),
                 r=[cbk, 'idns'], w=[cbk])
            for hq in range(16):
                kh = hq // 4
                Rb = R6 if hq < 8 else R7
                K.op('pe', lambda e, hq=hq, kh=kh, Em=Em, vb=vb, Rb=Rb, s_=s_: e.matmul(out=Rb[0][0:NS, (hq % 8) * 64:(hq % 8 + 1) * 64], lhsT=Em[:, hq, :], rhs=vb[:, kh * 64:(kh + 1) * 64],
                                                                                   start=(s_ == 0 and hq % 8 == 0), stop=(s_ == NS - 1 and hq % 8 == 7), skip_group_check=True),
                     r=[cbk], w=[Rb[1]], inc=(hq % 8 == 7))
            for hq in range(16):
                K.op('pe', lambda e, hq=hq, Em=Em, s_=s_: e.matmul(out=R5[0][0:NS, hq:hq + 1], lhsT=Em[:, hq, :], rhs=ones[:, 0:1], start=(s_ == 0 and hq == 0),
                                                               stop=(s_ == NS - 1 and hq == 15), skip_group_check=True), r=[cbk, 'ones'], w=[R5[1]], inc=(hq == 15))
        den = tiny[0:NS, 0:16]
        K.op('dve', lambda e: e.tensor_tensor(out=den, in0=R5[0][0:NS, 0:16], in1=snew, op=ALU.add), r=[R5[1], 'snew'], w=['mx'])
        K.op('dve', lambda e: e.tensor_tensor(out=den, in0=den, in1=esk[0:NS, :], op=ALU.add), r=['mx', 'esk'], w=['mx'])
        K.op('dve', lambda e: e.reciprocal(out=den, in_=den), r=['mx'], w=['mx'])
        num = prod[0:NS, :].rearrange("p (k g d) -> p k g d", k=4, g=4)
        K.op('dve', lambda e: e.tensor_tensor(out=num, in0=ksv[0:NS, 256:512].rearrange("p (k d) -> p k d", d=64).unsqueeze(2).broadcast_to([NS, 4, 4, 64]),
                                              in1=snew.rearrange("p (k g) -> p k g", g=4).unsqueeze(3).broadcast_to([NS, 4, 4, 64]), op=ALU.mult), r=['ksv', 'snew', prk], w=[prk])
        for b2, Rb in enumerate((R6, R7)):
            K.op('dve', lambda e, b2=b2, Rb=Rb: e.tensor_tensor(out=prod[0:NS, b2 * 512:(b2 + 1) * 512], in0=Rb[0][0:NS, :], in1=prod[0:NS, b2 * 512:(b2 + 1) * 512], op=ALU.add),
                 r=[Rb[1], prk], w=[prk])
        K.op('dve', lambda e: e.tensor_tensor(out=junk[0:NS, :].rearrange("p (h d) -> p h d", d=64), in0=prod[0:NS, :].rearrange("p (h d) -> p h d", d=64),
                                              in1=den.unsqueeze(2).broadcast_to([NS, 16, 64]), op=ALU.mult), r=[prk, 'mx'], w=['junk'])
        bko = bank()
        pbo = bko[0][:, :].bitcast(BF16)
        for c in range(8):
            K.op('pe', lambda e, c=c: e.transpose(out=pbo[:, c * NS:(c + 1) * NS], in_=junk[0:NS, c * 128:(c + 1) * 128], identity=ident[0:NS, 0:NS]),
                 r=['junk', 'ident'], w=[bko[1]], inc=(c == 7))
        K.op('dve', lambda e: e.tensor_copy(out=hT[:, :, T:TT], in_=pbo[:, 0:8 * NS].rearrange("p (a b) -> p a b", a=8)), r=[bko[1]], w=[('hT', NG)])

        wvs = [wload(swa_w_o, kh2 * 512, 4, 0, D) for kh2 in range(2)]
        for (t, c0, nrow) in toktiles():
            for half in range(2):
                p, pk = bank()
                for k in range(8):
                    wv, wk = wvs[k // 4]
                    K.op('pe', lambda e, p=p, k=k, wv=wv, c0=c0, nrow=nrow, half=half: e.matmul(out=p[0:nrow, :], lhsT=hT[:, k, c0:c0 + nrow], rhs=wv[:, k % 4, half * 512:(half + 1) * 512],
                                                                                             start=(k == 0), stop=(k == 7)), r=[wk, ('hT', min(t // TPG, NG))], w=[pk], inc=(k == 7))
                resid_add(t, nrow, half, p, pk)

    if stage in (1, 2):
        load_x(0) if False else None
        xattn(0)
    if stage == 2:
        ffn(0)
    XA['seg'] = 0
    if stage < 10:
        load_x(0)
        if stage == 3:
            retention(0)
        if stage == 4:
            conv_layer(0)
        if stage == 5:
            pool_layer(0)
        if stage == 6:
            swa_layer(0)
    if stage >= 10:
        for seg in range(NSEG):
            XA['seg'] = seg
            load_x(seg)
            retention(seg); xattn(0); ffn(0)
            swa_layer(seg); xattn(1); ffn(1)
            conv_layer(seg); xattn(2); ffn(2)
            pool_layer(seg); xattn(3); ffn(3)
            final_norm(seg)

    if stage < 10:
        for t in range(NT):
            K.dma('sp', dbg[t * 128:(t + 1) * 128, :], x[:, t, :], r=[('x', t)])
        K.dma('sp', dbg[T:TT, :], x[0:NS, NT, :], r=[('x', NT)])

    K.finish(['sp'])
    with nc.Block() as block:
        @block.tensor
        def _(e):
            K.emit('pe', e)

        @block.scalar
        def _(e):
            K.emit('act', e)

        @block.vector
        def _(e):
            K.emit('dve', e)

        @block.gpsimd
        def _(e):
            K.emit('pool', e)

        @block.sync
        def _(e):
            K.emit('sp', e)
    return nc


def _tables(c, T, NS):
    f32 = np.float32
    NT = T // 128
    b, s = divmod(c, 4)
    NSEG = 4
    pos = np.arange(NSEG * T).astype(f32)
    tb = {}
    tb["c_ident"] = np.eye(128, dtype=f32)
    inv = (1.0 / (f32(10000.0) ** (np.arange(128, dtype=f32) / f32(128)))).astype(f32)
    ang = (pos[None, :] * inv[:, None]).astype(f32)
    tb["c_rcosT"] = np.cos(ang).astype(f32)
    tb["c_rsinT"] = np.sin(ang).astype(f32)
    angs = (f32(PAST) * inv).astype(f32)
    tb["c_rsT"] = np.stack([np.cos(angs), np.sin(angs)], 1).astype(f32)
    logg = np.log1p(-np.exp2(-5.0 - np.arange(4, dtype=f32))).astype(f32)
    l = np.arange(128, dtype=f32)
    dec = np.zeros((128, 4, 128), f32)
    for h in range(4):
        diff = l[None, :] - l[:, None]
        dec[:, h, :] = np.where(diff >= 0, np.exp(logg[h] * np.maximum(diff, 0)), 0.0)
    tb["c_dec"] = (dec.reshape(128, 512) / 16.0).astype(f32)
    gq = np.stack([np.exp(logg[h] * (l + 1.0)) for h in range(4)], 0).astype(f32)
    tb["c_gq"] = np.tile(gq.reshape(1, 512), (128, 1)).astype(f32)
    gk = np.zeros((128, 12), f32)
    gkA = np.zeros((128, 4, NT), f32)
    for h in range(4):
        gk[:, h] = np.exp(logg[h] * (127.0 - l)) / 16.0
        gk[:, 4 + h] = np.exp(logg[h] * 128.0)
        gk[:, 8 + h] = np.exp(logg[h])
        for t in range(NT):
            gkA[:, h, t] = np.exp(logg[h] * (127.0 - l + 128.0 * (NT - 1 - t))) / 16.0
    tb["c_gk"] = gk
    tb["c_gkA"] = gkA.reshape(128, 4 * NT)
    tb["c_id16"] = (np.eye(NS, dtype=f32) / 16.0).astype(f32)
    coef = np.zeros((8, 4), f32)
    for c2 in range(8):
        if c2 // 4 == b and c2 < c:
            coef[c2] = np.exp(logg * f32(T * (c - c2 - 1)))
    tb["c_coef"] = np.tile(coef.reshape(1, 32), (128, 1)).astype(f32)
    sel = np.zeros((8,), f32)
    if s > 0:
        sel[c - 1] = 1.0
    tb["c_sel"] = np.tile(sel[None, :], (128, 1)).astype(f32)
    inv8 = (1.0 / (f32(500000.0) ** (np.arange(8, dtype=f32) / f32(8)))).astype(f32)
    a8 = (pos[:, None] * inv8[None, :]).astype(f32)
    cs = np.concatenate([np.cos(a8), np.sin(a8)], 1).astype(f32)
    sw = np.zeros((128, NSEG * NT + 1, 16), f32)
    sw[:, :NSEG * NT, :] = cs.reshape(NSEG * NT, 128, 16).transpose(1, 0, 2)
    a8s = (f32(PAST) * inv8).astype(f32)
    sw[:, NSEG * NT, :] = np.concatenate([np.cos(a8s), np.sin(a8s)])[None, :]
    tb["c_swcs"] = sw
    kk = np.arange(128)[:, None]; ll = np.arange(128)[None, :]
    m_own = (ll >= kk).astype(f32); m_prev = (ll <= kk).astype(f32)
    m_first = np.zeros_like(m_prev)
    tb["c_mask"] = np.concatenate([m_own, m_prev, m_first], 1).astype(f32)
    pa = np.zeros((128, 12, 128), f32)
    for gi, w in enumerate((2, 4, 8, 16)):
        own = np.zeros((128, 128), f32); prev = np.zeros((128, 128), f32); first = np.zeros((128, 128), f32)
        for lq in range(128):
            for i in range(w):
                lp = lq - i
                if lp >= 0:
                    own[lp, lq] += 1.0 / w
                else:
                    prev[128 + lp, lq] += 1.0 / w
            cnt = min(lq + 1, w)
            for i in range(cnt):
                first[lq - i, lq] += 1.0 / cnt
        eye = np.eye(128, dtype=f32)
        pa[:, gi * 3 + 0, :] = own - eye
        pa[:, gi * 3 + 1, :] = prev
        pa[:, gi * 3 + 2, :] = first - eye
    tb["c_poolA"] = pa.reshape(128, 12 * 128)
    pbm = np.zeros((128, 2, 4, NS), f32)
    per = 8
    for sp_ in range(NS):
        hf, r = divmod(sp_, per)
        for j in range(15):
            for gi, w in enumerate((2, 4, 8, 16)):
                if j >= 15 - (w - 1):
                    pbm[r * 15 + j, hf, gi, sp_] = 1.0 / w
    tb["c_poolB"] = pbm.reshape(128, 2 * 4 * NS)
    tb["c_poolC"] = np.concatenate([np.eye(NS, dtype=f32) * (1.0 / w - 1.0) for w in (2, 4, 8, 16)], 1).astype(f32)
    tb["c_idns"] = np.tile(np.eye(NS, dtype=f32).reshape(1, NS * NS), (128, 1))
    er = np.zeros((NS, NS, 128), f32)
    for i in range(NS):
        er[i, i, :] = 1.0
    tb["c_erow"] = er.reshape(NS, NS * 128)
    return tb


def prep(inputs, T, NS):
    f32 = np.float32
    g = lambda k: np.asarray(inputs[k], dtype=f32)
    shared = {
        "norm_g": g("norm_g").reshape(12, D), "mem_norm_g": g("mem_norm_g"), "final_norm_g": g("final_norm_g").reshape(1, D),
        "ret_w_in": g("ret_w_in")[0], "ret_w_out": g("ret_w_out")[0], "swa_w_qkv": g("swa_w_qkv")[0], "swa_w_o": g("swa_w_o")[0],
        "swa_sinks": g("swa_sinks").reshape(1, 16), "conv_w_in": g("conv_w_in")[0], "conv_w": g("conv_w")[0], "conv_w_out": g("conv_w_out")[0],
        "pool_w": g("pool_w")[0].reshape(D, 256), "pool_scale": g("pool_scale").reshape(1, D),
        "xattn_w_q": g("xattn_w_q"), "xattn_w_kv": g("xattn_w_kv"), "xattn_w_o": g("xattn_w_o"),
        "ffn_w_in": g("ffn_w_in"), "ffn_w_out": g("ffn_w_out"),
    }
    xp = g("x_prompt"); xs = g("x_sample")
    maps = []
    for c in range(NCORES):
        b, s = divmod(c, 4)
        sl = slice(c * NS, (c + 1) * NS)
        m = dict(shared)
        m["xp"] = np.ascontiguousarray(xp[b])
        m["xs"] = np.ascontiguousarray(xs[sl, 0])
        m["sret"] = np.ascontiguousarray(g("state_ret")[0, sl])
        m["swk"] = np.ascontiguousarray(g("cache_swa_k")[0, sl].reshape(NS, 128, 256))
        m["swv"] = np.ascontiguousarray(g("cache_swa_v")[0, sl].reshape(NS, 128, 256))
        m["sconv"] = np.ascontiguousarray(g("state_conv")[0, sl])
        m["spool"] = np.ascontiguousarray(g("state_pool")[0, sl])
        m["cmk"] = np.ascontiguousarray(g("cache_mem_k")[:, sl].reshape(4, NS, NMEM, D))
        m["cmv"] = np.ascontiguousarray(g("cache_mem_v")[:, sl].reshape(4, NS, NMEM, D))
        m["memp"] = np.ascontiguousarray(g("mem_prompt")[b])
        m.update(_tables(c, T, NS))
        maps.append(m)
    return maps


_NC_CACHE = {}


def kernel(**inputs):
    T = inputs["x_prompt"].shape[1] // 4
    NS = inputs["x_sample"].shape[0] // NCORES
    key = (T, NS)
    if key not in _NC_CACHE:
        _NC_CACHE[key] = build(T, NS)
    nc = _NC_CACHE[key]
    maps = prep(inputs, T, NS)
    maps = [{k: v for k, v in m.items() if k in nc.used_inputs} for m in maps]
    res = run_bass_kernel_spmd(nc, maps, core_ids=list(range(NCORES))).results
    return assemble(res, T, NS)


def assemble(res, T, NS):
    B = 2
    cat = lambda name: np.concatenate([res[c][name] for c in range(NCORES)], 0)
    y_prompt = np.stack([res[c]["yp"] for c in (0, 4)]).reshape(B, 4 * T, D)
    y_sample = cat("ys").reshape(NCORES * NS, 1, D)
    last = [0, 4]
    ret_p = np.stack([res[c]["o_retp"] for c in last])[None]
    ret_s = cat("o_rets")[None]
    swk_p = np.stack([res[c]["o_swkp"].reshape(128, 4, 64) for c in last])[None]
    swv_p = np.stack([res[c]["o_swvp"].reshape(128, 4, 64) for c in last])[None]
    swk_s = cat("o_swks").reshape(1, NCORES * NS, 128, 4, 64)
    swv_s = cat("o_swvs").reshape(1, NCORES * NS, 128, 4, 64)
    conv_p = np.stack([res[c]["o_convp"] for c in last])[None]
    conv_s = cat("o_convs")[None]
    pool_p = np.stack([res[c]["o_poolp"] for c in last])[None]
    pool_s = cat("o_pools")[None]
    memk = np.stack([res[c]["o_memk"] for c in (0, 4)], 1).reshape(4, B, NMEM, 4, 256)
    memv = np.stack([res[c]["o_memv"] for c in (0, 4)], 1).reshape(4, B, NMEM, 4, 256)
    outs = (y_prompt, y_sample, ret_p, ret_s, swk_p, swv_p, swk_s, swv_s, conv_p, conv_s, pool_p, pool_s, memk, memv)
    return tuple(np.ascontiguousarray(o, dtype=np.float32) for o in outs)
```

```python
import contextlib
import os
import numpy as np
import concourse.bass as bass
import concourse.mybir as mybir
from concourse.bass_utils import run_bass_kernel_spmd

F32 = mybir.dt.float32
BF16 = mybir.dt.bfloat16
AF = mybir.ActivationFunctionType
ALU = mybir.AluOpType
AX = mybir.AxisListType

D = 1024
EPS = 1e-6
NCORES = 8
DFF = 2816
NMEM = 256
PAST = 8192


class Trk:
    def __init__(self, nc, es):
        self.nc = nc
        self.q = {e: [] for e in ('pe', 'act', 'dve', 'pool', 'sp')}
        self.cnt = {e: 0 for e in self.q}
        self.waited = {e: {} for e in self.q}
        self.sem = {e: es.enter_context(nc.semaphore('sem_' + e)) for e in ('pe', 'act', 'dve', 'pool')}
        self.ND = 8
        self.dsem = {e: [es.enter_context(nc.semaphore(f'dsem_{e}{i}')) for i in range(self.ND)]
                     for e in ('sp', 'pool', 'act')}
        self.dval = {e: [0] * self.ND for e in self.dsem}
        self.dnext = {e: 0 for e in self.dsem}
        self.ccsem = es.enter_context(nc.semaphore('ccsem'))
        self.ccval = 0
        self.lastw = {}
        self.rd = {}
        self.open = {e: False for e in self.q}
        self.alias = {}

    def _x(self, keys):
        out = []
        for k in keys:
            out.extend(self.alias.get(k, (k,)))
        return out

    def _semof(self, k):
        if k[0] == 'e':
            return self.sem[k[1]]
        if k[0] == 'c':
            return self.ccsem
        return self.dsem[k[1][0]][k[1][1]]

    def _deps(self, eng, reads, writes):
        deps = []
        for r in reads:
            w = self.lastw.get(r)
            if w is not None:
                deps.append((w, 'raw'))
            if isinstance(r, tuple) and r[0] == 'ps':
                for r2 in self.rd.get(r, ()):
                    if not (r2[0] == 'e' and r2[1] == eng):
                        deps.append((r2, 'rar'))
        for w_ in writes:
            w = self.lastw.get(w_)
            if w is not None:
                deps.append((w, 'waw'))
            for r in self.rd.get(w_, ()):
                deps.append((r, 'war'))
        waits = []
        for (ev, kind) in deps:
            k = ev[:2]
            val = ev[2]
            if ev[0] == 'e' and ev[1] == eng:
                if eng == 'pe':
                    continue
            if self.waited[eng].get(k, 0) >= val:
                continue
            self.waited[eng][k] = val
            waits.append((k, val))
        return waits

    def _commit(self, ev, reads, writes):
        for w_ in writes:
            self.lastw[w_] = ev
            self.rd[w_] = set()
        for r in reads:
            self.rd.setdefault(r, set()).add(ev)

    def op(self, eng, fn, r=(), w=(), inc=True):
        r = self._x(r); w = self._x(w)
        waits = self._deps(eng, r, w)
        ev = ('e', eng, self.cnt[eng] + 1)
        if inc:
            self.cnt[eng] += 1
            self.open[eng] = False
            self.q[eng].append((waits, fn, (self.sem[eng], 1)))
        else:
            self.open[eng] = True
            self.q[eng].append((waits, fn, None))
        self._commit(ev, r, w)

    def dma(self, eng, out, in_, r=(), w=(), **kw):
        r = self._x(r); w = self._x(w)
        waits = self._deps(eng, r, w)
        i = self.dnext[eng]
        self.dnext[eng] = (i + 1) % self.ND
        k = ('d', (eng, i))
        if self.dval[eng][i] > 0 and self.waited[eng].get(k, 0) < self.dval[eng][i]:
            self.waited[eng][k] = self.dval[eng][i]
            waits.append((k, self.dval[eng][i]))
        self.dval[eng][i] += 16
        ev = ('d', (eng, i), self.dval[eng][i])
        self.q[eng].append((waits, lambda e: e.dma_start(out=out, in_=in_, **kw), (self.dsem[eng][i], 16)))
        self._commit(ev, r, w)

    def cc(self, ins, outs, r=(), w=()):
        r = self._x(r); w = self._x(w)
        waits = self._deps('pool', r, w)
        self.ccval += 1
        ev = ('c', 0, self.ccval)
        self.q['pool'].append((waits, lambda e: e.collective_compute(
            "AllGather", ALU.bypass, replica_groups=[list(range(NCORES))], ins=[ins], outs=[outs]),
            (self.ccsem, 1)))
        self._commit(ev, r, w)

    def finish(self, eng_list):
        for eng in eng_list:
            waits = []
            for e2 in ('pe', 'act', 'dve', 'pool'):
                if e2 != eng and self.cnt[e2] > self.waited[eng].get(('e', e2), 0):
                    waits.append((('e', e2), self.cnt[e2]))
            for e2 in self.dsem:
                for i in range(self.ND):
                    if self.dval[e2][i] > self.waited[eng].get(('d', (e2, i)), 0):
                        waits.append((('d', (e2, i)), self.dval[e2][i]))
            self.q[eng].append((waits, None, None))

    def emit(self, name, e):
        assert not self.open[name], name
        for waits, fn, inc in self.q[name]:
            for (k, val) in waits:
                e.wait_ge(self._semof(k), val)
            if fn is None:
                continue
            ins = fn(e)
            if inc is not None:
                ins.then_inc(inc[0], inc[1])


def build(T, NS, stage=99):
    nc = bass.Bass("TRN2", target_bir_lowering=False)
    es = contextlib.ExitStack()
    NT = T // 128
    GS = min(512, T)
    NG = T // GS
    TPG = GS // 128
    TT = T + NS
    K = Trk(nc, es)
    NSEG = 4

    used = set()
    nc.used_inputs = used

    class Lz:
        def __init__(self, name, shape):
            self.name = name; self.shp = list(shape); self._ap = None

        def ap(self):
            if self._ap is None:
                self._ap = nc.dram_tensor(self.name, self.shp, F32, kind="ExternalInput").ap()
                used.add(self.name)
            return self._ap

        def __getitem__(self, idx):
            return self.ap()[idx]

        def rearrange(self, *a, **k):
            return self.ap().rearrange(*a, **k)

    def din(name, shape, dt=F32):
        return Lz(name, shape)

    def dout(name, shape):
        return nc.dram_tensor(name, list(shape), F32, kind="ExternalOutput").ap()

    xp = din("xp", [NSEG * T, D]); xs = din("xs", [NS, D])
    sret = din("sret", [NS, 4, 256, 512])
    swk = din("swk", [NS, 128, 256]); swv = din("swv", [NS, 128, 256])
    sconv = din("sconv", [NS, 2, D]); spool = din("spool", [NS, 15, D])
    cmk = din("cmk", [4, NS, NMEM, D]); cmv = din("cmv", [4, NS, NMEM, D])
    memp = din("memp", [NMEM, D])
    norm_g = din("norm_g", [12, D]); mem_norm_g = din("mem_norm_g", [4, D]); final_norm_g = din("final_norm_g", [1, D])
    ret_w_in = din("ret_w_in", [D, 6 * D]); ret_w_out = din("ret_w_out", [2 * D, D])
    swa_w_qkv = din("swa_w_qkv", [D, 1536]); swa_w_o = din("swa_w_o", [D, D]); swa_sinks = din("swa_sinks", [1, 16])
    conv_w_in = din("conv_w_in", [D, 3 * D]); conv_w = din("conv_w", [3, D]); conv_w_out = din("conv_w_out", [D, D])
    pool_w = din("pool_w", [D, 256]); pool_scale = din("pool_scale", [1, D])
    xattn_w_q = din("xattn_w_q", [4, D, D]); xattn_w_kv = din("xattn_w_kv", [4, D, 2 * D]); xattn_w_o = din("xattn_w_o", [4, D, D])
    ffn_w_in = din("ffn_w_in", [4, D, 2 * DFF]); ffn_w_out = din("ffn_w_out", [4, DFF, D])
    c_ident = din("c_ident", [128, 128])
    c_rcosT = din("c_rcosT", [128, NSEG * T]); c_rsinT = din("c_rsinT", [128, NSEG * T])
    c_rsT = din("c_rsT", [128, 2])
    c_dec = din("c_dec", [128, 4 * 128])
    c_gq = din("c_gq", [128, 4 * 128])
    c_gk = din("c_gk", [128, 12])
    c_gkA = din("c_gkA", [128, 4 * (T // 128)])
    c_id16 = din("c_id16", [NS, NS])
    c_coef = din("c_coef", [128, 32])
    c_sel = din("c_sel", [128, 8])
    c_swcs = din("c_swcs", [128, NSEG * NT + 1, 16])
    c_mask = din("c_mask", [128, 3 * 128])
    c_poolA = din("c_poolA", [128, 12 * 128])
    c_poolB = din("c_poolB", [128, 2 * 4 * NS])
    c_poolC = din("c_poolC", [NS, 4 * NS])
    c_idns = din("c_idns", [128, NS * NS])
    c_erow = din("c_erow", [NS, NS * 128])

    yp = dout("yp", [NSEG * T, D]); ys = dout("ys", [NS, D])
    o_retp = dout("o_retp", [4, 256, 512]); o_rets = dout("o_rets", [NS, 4, 256, 512])
    o_swkp = dout("o_swkp", [128, 256]); o_swvp = dout("o_swvp", [128, 256])
    o_swks = dout("o_swks", [NS, 128, 256]); o_swvs = dout("o_swvs", [NS, 128, 256])
    o_convp = dout("o_convp", [2, D]); o_convs = dout("o_convs", [NS, 2, D])
    o_poolp = dout("o_poolp", [15, D]); o_pools = dout("o_pools", [NS, 15, D])
    o_memk = dout("o_memk", [4, NMEM, D]); o_memv = dout("o_memv", [4, NMEM, D])
    dbg = dout("dbg", [TT, D]) if stage < 10 else None
    dbg2 = dout("dbg2", [128, 8192]) if stage < 10 else None

    st_ret = nc.dram_tensor("st_ret", [4 * 256, 512], F32)
    st_h = nc.dram_tensor("st_h", [128, 512], F32)
    st_c = nc.dram_tensor("st_c", [128, 16], F32)
    st_p = nc.dram_tensor("st_p", [16, D], F32)
    cc_loc_ret = nc.dram_tensor("cc_loc_ret", [4 * 256, 512], F32)
    cc_gat_ret = nc.dram_tensor("cc_gat_ret", [NCORES * 4 * 256, 512], F32)
    cc_loc_h = nc.dram_tensor("cc_loc_h", [128, 512], F32)
    cc_gat_h = nc.dram_tensor("cc_gat_h", [NCORES * 128, 512], F32)
    cc_loc_c = nc.dram_tensor("cc_loc_c", [128, 16], F32)
    cc_gat_c = nc.dram_tensor("cc_gat_c", [NCORES * 128, 16], F32)
    cc_loc_p = nc.dram_tensor("cc_loc_p", [16, D], F32)
    cc_gat_p = nc.dram_tensor("cc_gat_p", [NCORES * 16, D], F32)

    def sb(name, shape, dt=F32):
        return nc.alloc_sbuf_tensor(name, list(shape), dt)

    x = sb("x", [128, NT + 1, D])
    hT = sb("hT", [128, 8, TT], BF16)
    BIGE = 18496
    big = sb("big", [128, BIGE], BF16)
    NW = 3
    wslot = [sb(f"w{i}", [128, 4096], BF16) for i in range(NW)]
    junk = sb("junk", [128, D], BF16)
    ident = sb("ident", [128, 128], BF16)
    ones = sb("ones", [128, 128], BF16)
    gcols = sb("gcols", [128, 13 * 8])
    mgcols = sb("mgcols", [128, 4 * 8])
    ss = sb("ss", [128, NT + 1]); rs = sb("rs", [128, NT + 1]); rstd = sb("rstd", [128, NT + 1])
    tabs = sb("tabs", [128, 2048])
    scr = [sb(f"scr{i}", [128, 2048], BF16) for i in range(4)]
    LE = 7680
    Lreg = sb("Lreg", [128, LE], BF16)
    lst = {'o': 0, 'keys': set()}

    def lreset():
        lst['o'] = 0
        lst['keys'] = set()

    def la(key, shape, dt=F32):
        n = 1
        for d_ in shape[1:]:
            n *= d_
        nb = n * (2 if dt == F32 else 1)
        o = lst['o']; o += o % 2
        assert o + nb <= LE, (key, o, nb)
        lst['o'] = o + nb
        v = Lreg[0:shape[0], o:o + nb]
        if dt == F32:
            v = v.bitcast(F32)
        blocks = tuple(('L', i) for i in range(o // 512, (o + nb - 1) // 512 + 1))
        if key in lst['keys']:
            K.alias[key] = tuple(sorted(set(K.alias[key] + blocks)))
        else:
            K.alias[key] = blocks
            lst['keys'].add(key)
        if len(shape) == 3:
            v = v.rearrange("p (a b) -> p a b", a=shape[1])
        elif len(shape) == 4:
            v = v.rearrange("p (a b c) -> p a b c", a=shape[1], b=shape[2])
        return v

    smal = sb("smal", [128, 1024])
    stg = smal[:, :].rearrange("p (a b) -> p a b", a=2)
    tiny = sb("tiny", [128, 32])
    dstage = sb("dstage", [128, 8192]) if os.environ.get("KDBG") else None
    ps = [nc.alloc_psum_tensor(f"ps{i}", [128, 512], F32) for i in range(8)]

    st = {'ps': 0, 'w': 0, 'scr': 0}
    XA = {}

    def bank():
        b = st['ps']; st['ps'] = (b + 1) % 5
        return ps[b], ('ps', b)

    def banks(n):
        return [bank() for _ in range(n)]

    def wload(src2d, r0, nrc, c0, ncols, eng='pool'):
        i = st['w']; st['w'] = (i + 1) % NW
        assert nrc * ncols <= 4096
        view = wslot[i][:, 0:nrc * ncols].rearrange("p (a b) -> p a b", a=nrc)
        src = src2d[r0:r0 + 128 * nrc, c0:c0 + ncols].rearrange("(a p) n -> p a n", p=128)
        K.dma(eng, view, src, w=[('w', i)])
        return view, ('w', i)

    def scratch():
        i = st['scr']; st['scr'] = (i + 1) % 4
        return scr[i], ('scr', i)

    dstate = {'c': 0}

    def dump(ap, keys, n):
        c0 = dstate['c']; dstate['c'] += n
        npart = ap.shape[0]
        if ap.dtype != F32:
            K.op('dve', lambda e: e.tensor_copy(out=dstage[0:npart, c0:c0 + n], in_=ap), r=keys, w=['dstage'])
            K.dma('sp', dbg2[0:npart, c0:c0 + n], dstage[0:npart, c0:c0 + n], r=['dstage'])
        else:
            K.dma('sp', dbg2[0:npart, c0:c0 + n], ap, r=keys)
        return c0

    K.dma('pool', ident[:], c_ident[:, :], w=['ident'])
    K.op('dve', lambda e: e.memset(ones[:], 1.0), w=['ones'])
    K.op('dve', lambda e: e.memset(x[:, NT, :], 0.0), w=[('x', NT)])
    K.dma('sp', gcols[:, 0:96].rearrange("p (n c) -> p n c", c=8),
          norm_g.rearrange("n (c p) -> p n c", p=128), w=['gcols'], allow_slow_non_contiguous=True)
    K.dma('sp', gcols[:, 96:104], final_norm_g.rearrange("n (c p) -> p (n c)", p=128), w=['gcols'], allow_slow_non_contiguous=True)
    K.dma('sp', mgcols[:].rearrange("p (n c) -> p n c", c=8),
          mem_norm_g.rearrange("n (c p) -> p n c", p=128), w=['mgcols'], allow_slow_non_contiguous=True)
    def load_x(seg):
        for t in range(NT):
            K.dma('sp', x[:, t, :], xp[seg * T + t * 128:seg * T + (t + 1) * 128, :], w=[('x', t)])
        K.dma('sp', x[0:NS, NT, :], xs[:, :], r=[], w=[('x', NT)])

    def sumsq_rstd(src_tiles, n, keyr):
        for i, (ap, key) in enumerate(src_tiles):
            K.op('act', lambda e, ap=ap, i=i: e.activation(out=junk[:], in_=ap, func=AF.Square,
                                                            accum_out=ss[:, i:i + 1]), r=[key], w=['junk', 'ss'])
        K.op('act', lambda e: e.activation(out=rs[:, 0:n], in_=ss[:, 0:n], func=AF.Sqrt, scale=1.0 / D, bias=epsc[:, 0:1]),
             r=['ss', 'epsc'], w=['rs'])
        K.op('dve', lambda e: e.reciprocal(out=rstd[:, 0:n], in_=rs[:, 0:n]), r=['rs'], w=[keyr])

    epsc = sb("epsc", [128, 1])
    K.op('dve', lambda e: e.memset(epsc[:], EPS), w=['epsc'])

    def norm_T(gi):
        sumsq_rstd([(x[:, t, :], ('x', t)) for t in range(NT + 1)], NT + 1, 'rstd')
        for n in range(NG + 1 if XA.get('smp', True) else NG):
            tiles = list(range(n * TPG, (n + 1) * TPG)) if n < NG else [NT]
            xs_ = [scratch(), scratch()]

            def xnv(ti, xs_=xs_):
                return xs_[ti // 2][0][:, (ti % 2) * D:(ti % 2 + 1) * D], xs_[ti // 2][1]
            for ti, t in enumerate(tiles):
                K.op('dve', lambda e, ti=ti, t=t, xnv=xnv: e.tensor_scalar(out=xnv(ti)[0], in0=x[:, t, :], scalar1=rstd[:, t:t + 1],
                                                                          scalar2=None, op0=ALU.mult),
                     r=[('x', t), 'rstd'], w=[xnv(ti)[1]])
            bk = banks(4)
            for c in range(8):
                pb = bk[c // 2][0][:, :].bitcast(BF16)
                for ti, t in enumerate(tiles):
                    last = (ti == len(tiles) - 1) and (c % 2 == 1)
                    if n < NG:
                        K.op('pe', lambda e, pb=pb, c=c, ti=ti, xnv=xnv: e.transpose(
                            out=pb[:, (c % 2) * 512 + ti * 128:(c % 2) * 512 + ti * 128 + 128],
                            in_=xnv(ti)[0][:, c * 128:(c + 1) * 128], identity=ident[:]),
                            r=[xnv(ti)[1], 'ident'], w=[bk[c // 2][1]], inc=last)
                    else:
                        K.op('pe', lambda e, pb=pb, c=c, xnv=xnv: e.transpose(
                            out=pb[:, (c % 2) * 512:(c % 2) * 512 + NS],
                            in_=xnv(0)[0][0:NS, c * 128:(c + 1) * 128], identity=ident[0:NS, 0:NS]),
                            r=[xnv(0)[1], 'ident'], w=[bk[c // 2][1]], inc=last)
            ncol = GS if n < NG else NS
            c0 = n * GS if n < NG else T
            for c in range(8):
                pb = bk[c // 2][0][:, :].bitcast(BF16)
                src = pb[:, (c % 2) * 512:(c % 2) * 512 + ncol]
                dst = hT[:, c, c0:c0 + ncol]
                gc = gcols[:, gi * 8 + c:gi * 8 + c + 1]
                if c % 2 == 0:
                    K.op('act', lambda e, src=src, dst=dst, gc=gc: e.activation(out=dst, in_=src, func=AF.Copy, scale=gc),
                         r=[bk[c // 2][1], 'gcols'], w=[('hT', n)])
                else:
                    K.op('dve', lambda e, src=src, dst=dst, gc=gc: e.tensor_scalar(out=dst, in0=src, scalar1=gc, scalar2=None,
                                                                                  op0=ALU.mult),
                         r=[bk[c // 2][1], 'gcols'], w=[('hT', n)])

    def tokgroups():
        return [(n, n * GS, GS) for n in range(NG)] + ([(NG, T, NS)] if XA.get('smp', True) else [])

    def toktiles():
        return [(t, t * 128, 128) for t in range(NT)] + ([(NT, T, NS)] if XA.get('smp', True) else [])

    def projB(wv, wkey, wc0, src, srckey, evac):
        for (n, c0, ncol) in tokgroups():
            p, pk = bank()
            for kc in range(8):
                K.op('pe', lambda e, p=p, kc=kc, c0=c0, ncol=ncol: e.matmul(
                    out=p[:, 0:ncol], lhsT=wv[:, kc, wc0:wc0 + 128], rhs=src[:, kc, c0:c0 + ncol],
                    start=(kc == 0), stop=(kc == 7)), r=[wkey, (srckey, n)], w=[pk], inc=(kc == 7))
            evac(n, c0, ncol, p, pk)

    def resid_add(t, nrow, half, p, pk):
        K.op('dve', lambda e: e.tensor_tensor(out=x[0:nrow, t, half * 512:(half + 1) * 512], in0=p[0:nrow, :],
                                              in1=x[0:nrow, t, half * 512:(half + 1) * 512], op=ALU.add),
             r=[pk, ('x', t)], w=[('x', t)])

    def projA_resid(lhs_fn, lhs_keys, nk, wv, wkey):
        for (t, c0, nrow) in toktiles():
            for half in range(2):
                p, pk = bank()
                for k in range(nk):
                    K.op('pe', lambda e, p=p, k=k, c0=c0, nrow=nrow, half=half: e.matmul(
                        out=p[0:nrow, :], lhsT=lhs_fn(k, c0, nrow), rhs=wv[:, k, half * 512:(half + 1) * 512],
                        start=(k == 0), stop=(k == nk - 1)), r=[wkey] + lhs_keys(t), w=[pk], inc=(k == nk - 1))
                resid_add(t, nrow, half, p, pk)

    def ffn(l):
        norm_T(l * 3 + 2)
        gT = [big[:, i * 4 * TT:(i + 1) * 4 * TT].rearrange("p (a b) -> p a b", a=4) for i in range(2)]
        nmac = (22 + 3) // 4
        for m in range(nmac):
            chunks = list(range(m * 4, min(22, m * 4 + 4)))
            gt = gT[m % 2]
            for jj, j in enumerate(chunks):
                if jj % 2 == 0:
                    i = st['w']; st['w'] = (i + 1) % NW
                    nb = min(2, len(chunks) - jj) * 128
                    wv = wslot[i][:, 0:8 * 2 * nb].rearrange("p (k a f) -> p k a f", k=8, a=2)
                    for ab in range(2):
                        src = ffn_w_in[l][:, ab * DFF + j * 128:ab * DFF + j * 128 + nb].rearrange("(k p) f -> p k f", p=128)
                        K.dma('pool', wv[:, :, ab, :], src, w=[('w', i)])
                    wkey = ('w', i)
                fo = (jj % 2) * 128
                for (n, c0, ncol) in tokgroups():
                    pa, pak = bank()
                    pbk_ = bank()
                    pb_, pbk = pbk_
                    for ab, pp, ppk in ((0, pa, pak), (1, pb_, pbk)):
                        for kc in range(8):
                            K.op('pe', lambda e, pp=pp, kc=kc, c0=c0, ncol=ncol, ab=ab, wv=wv, fo=fo: e.matmul(
                                out=pp[:, 0:ncol], lhsT=wv[:, kc, ab, fo:fo + 128], rhs=hT[:, kc, c0:c0 + ncol],
                                start=(kc == 0), stop=(kc == 7)), r=[wkey, ('hT', n)], w=[ppk], inc=(kc == 7))
                    sa, sak = scratch()
                    K.op('act', lambda e, sa=sa, pa=pa, ncol=ncol: e.activation(out=sa[:, 0:ncol], in_=pa[:, 0:ncol], func=AF.Silu),
                         r=[pak], w=[sak])
                    K.op('dve', lambda e, sa=sa, pb_=pb_, ncol=ncol, c0=c0, gt=gt, jj=jj: e.tensor_tensor(
                        out=gt[:, jj, c0:c0 + ncol], in0=pb_[:, 0:ncol], in1=sa[:, 0:ncol], op=ALU.mult),
                        r=[pbk, sak], w=[('gT', m % 2, n)])
            nk = len(chunks)
            wo, wok = [], []
            for h2 in range((nk + 3) // 4):
                v_, k_ = wload(ffn_w_out[l], m * 512, nk, 0, D) if False else (None, None)
            wv2, wk2 = wload(ffn_w_out[l], m * 512, nk, 0, D)
            projA_resid(lambda k, c0, nrow, gt=gt: gt[:, k, c0:c0 + nrow],
                        lambda t, m=m: [('gT', m % 2, min(t // TPG, NG))], nk, wv2, wk2)

    def mem_prep():
        memn = XA['memn']
        assert 8 * TT >= 4096
        mt = hT[:, :, :].rearrange("p a b -> p (a b)")[:, 0:4096].bitcast(F32).rearrange("p (a b) -> p a b", a=2)
        hk = [('hT', n) for n in range(NG + 1)]
        for i in range(2):
            K.dma('sp', mt[:, i, :], memp[i * 128:(i + 1) * 128, :], w=hk)
        sumsq_rstd([(mt[:, i, :], hk[0]) for i in range(2)], 2, 'rstd')
        for i in range(2):
            K.op('dve', lambda e, i=i: e.tensor_scalar(out=memn[:, i, :], in0=mt[:, i, :], scalar1=rstd[:, i:i + 1], scalar2=None,
                                                      op0=ALU.mult), r=hk + ['rstd'], w=['memn'])

    def mem_kv(l):
        memn = XA['memn']; KT = XA['KT']; Vb = XA['Vb']
        mT_, mTk = scratch()
        memT = mT_[:, :].rearrange("p (a b) -> p a b", a=8)
        bk = banks(2)
        for c in range(8):
            pb = bk[c // 4][0][:, :].bitcast(BF16)
            for i in range(2):
                K.op('pe', lambda e, pb=pb, c=c, i=i: e.transpose(out=pb[:, (c % 4) * 256 + i * 128:(c % 4) * 256 + i * 128 + 128],
                                                                  in_=memn[:, i, c * 128:(c + 1) * 128], identity=ident[:]),
                     r=['memn', 'ident'], w=[bk[c // 4][1]], inc=(i == 1 and c % 4 == 3))
        for c in range(8):
            pb = bk[c // 4][0][:, :].bitcast(BF16)
            K.op('act', lambda e, pb=pb, c=c: e.activation(out=memT[:, c, :], in_=pb[:, (c % 4) * 256:(c % 4) * 256 + 256],
                                                          func=AF.Copy, scale=mgcols[:, l * 8 + c:l * 8 + c + 1]),
                 r=[bk[c // 4][1], 'mgcols'], w=[mTk])
        KMK = int(os.environ.get("KMK", "99"))
        if KMK <= 1:
            return
        for kv in range(2):
            for hf in range(2):
                wv, wk = wload(xattn_w_kv[l], 0, 8, kv * D + hf * 512, 512)
                if KMK <= 2:
                    continue
                for i in range(2):
                    p, pk = bank()
                    for kc in range(8):
                        K.op('pe', lambda e, p=p, kc=kc, i=i, wv=wv: e.matmul(out=p[:, :], lhsT=memT[:, kc, i * 128:(i + 1) * 128],
                                                                            rhs=wv[:, kc, :], start=(kc == 0), stop=(kc == 7)),
                             r=[wk, mTk], w=[pk], inc=(kc == 7))
                    if KMK <= 3:
                        continue
                    K.op('act', lambda e, p=p, i=i: e.activation(out=stg[:, i, :], in_=p[:, :], func=AF.Copy), r=[pk], w=[('smal', i)])
                    if KMK <= 4:
                        continue
                    if kv == 1:
                        K.op('act', lambda e, p=p, i=i, hf=hf: e.activation(out=Vb[:, i, hf * 512:(hf + 1) * 512], in_=p[:, :], func=AF.Copy),
                             r=[pk], w=['Vb'])
                    dst = (o_memk if kv == 0 else o_memv)[l, i * 128:(i + 1) * 128, hf * 512:(hf + 1) * 512]
                    if KMK <= 5:
                        continue
                    if XA['seg'] == 0:
                        K.dma('sp', dst, stg[:, i, :], r=[('smal', i)])
                if kv == 0 and KMK > 6:
                    for cc_ in range(4):
                        p, pk = bank()
                        for kc in range(8):
                            K.op('pe', lambda e, p=p, kc=kc, cc_=cc_, wv=wv: e.matmul(out=p[:, 0:NMEM], lhsT=wv[:, kc, cc_ * 128:(cc_ + 1) * 128],
                                                                                    rhs=memT[:, kc, :], start=(kc == 0), stop=(kc == 7)),
                                 r=[wk, mTk], w=[pk], inc=(kc == 7))
                        K.op('dve', lambda e, p=p, cc_=cc_, hf=hf: e.tensor_copy(out=KT[:, hf * 4 + cc_, :], in_=p[:, 0:NMEM]),
                             r=[pk], w=['KT'])

    def xattn(l):
        SUB = int(os.environ.get("KSUB", "99"))
        lreset()
        XA['memn'] = la('memn', [128, 2, D], BF16); XA['KT'] = la('KT', [128, 8, NMEM], BF16); XA['Vb'] = la('Vb', [128, 2, D], BF16)
        KT = XA['KT']; Vb = XA['Vb']
        mem_prep()
        norm_T(l * 3 + 1)
        if SUB <= 1:
            return
        mem_kv(l)
        if SUB <= 2:
            return
        aT = big[:, 0:8 * TT].rearrange("p (a b) -> p a b", a=8)
        o0 = 8 * TT
        PT = big[:, o0:o0 + 1024].rearrange("p (a b) -> p a b", a=2)
        rden = smal[:, 0:512]
        for hf in range(2):
            wv, wk = wload(xattn_w_q[l], 0, 8, hf * 512, 512)
            for cc_ in range(4):
                c = hf * 4 + cc_

                def ev(n, c0, ncol, p, pk, c=c):
                    K.op('act', lambda e: e.activation(out=aT[:, c, c0:c0 + ncol], in_=p[:, 0:ncol], func=AF.Copy),
                         r=[pk], w=[('aT', c // 2, n)])
                projB(wv, wk, cc_ * 128, hT, 'hT', ev)
        if SUB <= 3:
            return
        for n in range(NG):
            c0 = n * GS
            for h in range(4):
                sb_ = banks(2)
                for mc in range(2):
                    for dc in range(2):
                        K.op('pe', lambda e, mc=mc, dc=dc, p=sb_[mc][0], h=h, c0=c0: e.matmul(
                            out=p[:, 0:GS], lhsT=KT[:, h * 2 + dc, mc * 128:(mc + 1) * 128], rhs=aT[:, h * 2 + dc, c0:c0 + GS],
                            start=(dc == 0), stop=(dc == 1)), r=['KT', ('aT', h, n)], w=[sb_[mc][1]], inc=(dc == 1))
                for mc in range(2):
                    K.op('act', lambda e, mc=mc, p=sb_[mc][0]: e.activation(out=PT[:, mc, 0:GS], in_=p[:, 0:GS], func=AF.Exp, scale=1.0 / 16.0),
                         r=[sb_[mc][1]], w=[('PT', mc)])
                dn, dnk = bank()
                for mc in range(2):
                    K.op('pe', lambda e, mc=mc, dn=dn: e.matmul(out=dn[:, 0:GS], lhsT=ones[:, :], rhs=PT[:, mc, 0:GS], start=(mc == 0), stop=(mc == 1)),
                         r=['ones', ('PT', mc)], w=[dnk], inc=(mc == 1))
                ob = banks(2)
                for dc in range(2):
                    for mc in range(2):
                        K.op('pe', lambda e, mc=mc, dc=dc, p=ob[dc][0], h=h: e.matmul(
                            out=p[:, 0:GS], lhsT=Vb[:, mc, h * 256 + dc * 128:h * 256 + dc * 128 + 128], rhs=PT[:, mc, 0:GS],
                            start=(mc == 0), stop=(mc == 1)), r=['Vb', ('PT', mc)], w=[ob[dc][1]], inc=(mc == 1))
                K.op('dve', lambda e, dn=dn: e.reciprocal(out=rden[:, 0:GS], in_=dn[:, 0:GS]), r=[dnk], w=[('smal', 0)])
                for dc in range(2):
                    K.op('dve', lambda e, dc=dc, p=ob[dc][0], h=h, c0=c0: e.tensor_tensor(out=aT[:, h * 2 + dc, c0:c0 + GS], in0=p[:, 0:GS], in1=rden[:, 0:GS],
                                                                               op=ALU.mult), r=[ob[dc][1], ('smal', 0)], w=[('aT', h, n)])
        if SUB <= 4:
            return
        if XA.get('smp', True):
            xattn_samples(l, aT)
        if SUB <= 5:
            return
        for hf in range(2):
            pass
        wvs = []
        for kh in range(2):
            wvs.append(wload(xattn_w_o[l], kh * 512, 4, 0, D))
        for (t, c0, nrow) in toktiles():
            for half in range(2):
                p, pk = bank()
                for k in range(8):
                    wv, wk = wvs[k // 4]
                    K.op('pe', lambda e, p=p, k=k, wv=wv, c0=c0, nrow=nrow, half=half: e.matmul(out=p[0:nrow, :], lhsT=aT[:, k, c0:c0 + nrow],
                                                                  rhs=wv[:, k % 4, half * 512:(half + 1) * 512], start=(k == 0), stop=(k == 7)),
                         r=[wk] + [('aT', hh, min(t // TPG, NG)) for hh in range(4)], w=[pk], inc=(k == 7))
                resid_add(t, nrow, half, p, pk)

    idns = sb("idns", [128, NS, NS], BF16)
    K.dma('pool', idns[:].rearrange("p a b -> p (a b)"), c_idns[:, :], w=['idns'])

    def xattn_samples(l, aT):
        qm = [la('qm0', [128, 8, NS], BF16), la('qm1', [128, 8, NS], BF16)]
        pm = [la('pm0', [128, 8, NS], BF16), la('pm1', [128, 8, NS], BF16)]
        PTs = la('PTs', [128, 8, NS], BF16)
        sc = [(ps[6], ('ps', 6)), (ps[7], ('ps', 7))]
        for s in range(NS):
            K.op('dve', lambda e, s=s: e.tensor_tensor(out=qm[s % 2], in0=aT[:, :, T:TT], in1=idns[:, s, :].unsqueeze(1).broadcast_to([128, 8, NS]), op=ALU.mult),
                 r=[('aT', hh, NG) for hh in range(4)] + ['idns'], w=['qm%d' % (s % 2)])
            kb, kbk = scratch()
            kbv = kb[:, :].rearrange("p (a b) -> p a b", a=2)
            K.dma('pool', kbv, cmk[l, s].rearrange("(a p) n -> p a n", p=128), w=[kbk])
            kt, ktk = scratch()
            ktv = kt[:, :].rearrange("p (a b) -> p a b", a=8)
            for g4 in range(2):
                bk = banks(2)
                for c4 in range(4):
                    c = g4 * 4 + c4
                    pb = bk[c4 // 2][0][:, :].bitcast(BF16)
                    for i in range(2):
                        K.op('pe', lambda e, pb=pb, c=c, c4=c4, i=i, kbv=kbv: e.transpose(
                            out=pb[:, (c4 % 2) * 256 + i * 128:(c4 % 2) * 256 + i * 128 + 128],
                            in_=kbv[:, i, c * 128:(c + 1) * 128], identity=ident[:]), r=[kbk, 'ident'], w=[bk[c4 // 2][1]],
                            inc=(i == 1 and c4 % 2 == 1))
                for b2 in range(2):
                    pb = bk[b2][0][:, :].bitcast(BF16)
                    eng = 'act' if b2 == 0 else 'dve'
                    dst = ktv[:, g4 * 4 + b2 * 2:g4 * 4 + b2 * 2 + 2, :]
                    if eng == 'act':
                        K.op('act', lambda e, pb=pb, dst=dst: e.activation(out=dst, in_=pb[:, 0:512].rearrange("p (a b) -> p a b", a=2), func=AF.Copy),
                             r=[bk[b2][1]], w=[ktk])
                    else:
                        K.op('dve', lambda e, pb=pb, dst=dst: e.tensor_copy(out=dst, in_=pb[:, 0:512].rearrange("p (a b) -> p a b", a=2)),
                             r=[bk[b2][1]], w=[ktk])
            for h in range(4):
                for dc in range(2):
                    first = (s == 0 and h % 2 == 0 and dc == 0)
                    lastm = (s == NS - 1 and dc == 1)
                    K.op('pe', lambda e, h=h, dc=dc, s=s, first=first, ktv=ktv: e.matmul(
                        out=sc[h // 2][0][0:NS, (h % 2) * 256:(h % 2) * 256 + 256], lhsT=qm[s % 2][:, h * 2 + dc, :], rhs=ktv[:, h * 2 + dc, :],
                        start=first, stop=(s == NS - 1 and dc == 1 and h % 2 == 1), skip_group_check=True),
                        r=['qm%d' % (s % 2), ktk], w=[sc[h // 2][1]], inc=True if (dc == 1) else False)
        if dstage is not None and l == 0:
            for b2 in range(2):
                K.op('act', lambda e, b2=b2: e.activation(out=tabs[0:NS, b2 * 512:(b2 + 1) * 512], in_=sc[b2][0][0:NS, :], func=AF.Copy), r=[sc[b2][1]], w=['tabsdbg'])
            dump(tabs[0:NS, 0:1024], ['tabsdbg'], 1024)
            dump(ktv[:, 0, :], [ktk], 256)
        mx = tiny[0:NS, 0:4]; nmx = tiny[0:NS, 4:8]; sm = tiny[0:NS, 8:12]; rsm = tiny[0:NS, 12:16]
        Pf = smal[0:NS, 0:1024]
        Pb = junk[0:NS, :]
        for h in range(4):
            src = sc[h // 2][0][0:NS, (h % 2) * 256:(h % 2) * 256 + 256]
            K.op('dve', lambda e, src=src, h=h: e.tensor_reduce(out=mx[:, h:h + 1], in_=src, axis=AX.X, op=ALU.max), r=[sc[h // 2][1]], w=['mx'])
        K.op('dve', lambda e: e.tensor_scalar(out=nmx, in0=mx, scalar1=-1.0 / 16.0, scalar2=None, op0=ALU.mult), r=['mx'], w=['nmx'])
        for h in range(4):
            src = sc[h // 2][0][0:NS, (h % 2) * 256:(h % 2) * 256 + 256]
            K.op('act', lambda e, src=src, h=h: e.activation(out=Pf[:, h * 256:(h + 1) * 256], in_=src, func=AF.Exp, scale=1.0 / 16.0,
                                                            bias=nmx[:, h:h + 1], accum_out=sm[:, h:h + 1]), r=[sc[h // 2][1], 'nmx'], w=[('smal', 0), ('smal', 1), 'sm'])
        K.op('dve', lambda e: e.reciprocal(out=rsm, in_=sm), r=['sm'], w=['rsm'])
        for h in range(4):
            K.op('dve', lambda e, h=h: e.tensor_scalar(out=Pb[:, h * 256:(h + 1) * 256], in0=Pf[:, h * 256:(h + 1) * 256], scalar1=rsm[:, h:h + 1],
                                                      scalar2=None, op0=ALU.mult), r=[('smal', 0), ('smal', 1), 'rsm'], w=['junk'])
        if dstage is not None and l == 0:
            dump(Pb, ['junk'], 1024)
            dump(aT[:, :, T:TT].rearrange("p a b -> p (a b)") if False else aT[:, 0, T:TT], [('aT', 0, NG)], NS)
        bkp = bank()
        pbp = bkp[0][:, :].bitcast(BF16)
        for j in range(8):
            K.op('pe', lambda e, j=j: e.transpose(out=pbp[:, j * NS:(j + 1) * NS], in_=Pb[:, j * 128:(j + 1) * 128], identity=ident[0:NS, 0:NS]),
                 r=['junk', 'ident'], w=[bkp[1]], inc=(j == 7))
        K.op('dve', lambda e: e.tensor_copy(out=PTs[:, :, :], in_=pbp[:, 0:8 * NS].rearrange("p (a b) -> p a b", a=8)), r=[bkp[1]], w=['PTs'])
        ob = [(ps[6], ('ps', 6)), (ps[7], ('ps', 7))]
        for s in range(NS):
            K.op('dve', lambda e, s=s: e.tensor_tensor(out=pm[s % 2], in0=PTs[:, :, :], in1=idns[:, s, :].unsqueeze(1).broadcast_to([128, 8, NS]), op=ALU.mult),
                 r=['PTs', 'idns'], w=['pm%d' % (s % 2)])
            vb, vbk = scratch()
            vbv = vb[:, :].rearrange("p (a b) -> p a b", a=2)
            K.dma('pool', vbv, cmv[l, s].rearrange("(a p) n -> p a n", p=128), w=[vbk])
            for h in range(4):
                for mc in range(2):
                    first = (s == 0 and h % 2 == 0 and mc == 0)
                    K.op('pe', lambda e, h=h, mc=mc, s=s, first=first, vbv=vbv: e.matmul(
                        out=ob[h // 2][0][0:NS, (h % 2) * 256:(h % 2) * 256 + 256], lhsT=pm[s % 2][:, h * 2 + mc, :], rhs=vbv[:, mc, h * 256:(h + 1) * 256],
                        start=first, stop=(s == NS - 1 and mc == 1 and h % 2 == 1), skip_group_check=True),
                        r=['pm%d' % (s % 2), vbk], w=[ob[h // 2][1]], inc=(mc == 1))
        osb = junk[0:NS, :]
        for b2 in range(2):
            K.op('act', lambda e, b2=b2: e.activation(out=osb[:, b2 * 512:(b2 + 1) * 512], in_=ob[b2][0][0:NS, :], func=AF.Copy), r=[ob[b2][1]], w=['junk'])
        if dstage is not None and l == 0:
            dump(osb, ['junk'], 1024)
        bko = bank()
        pbo = bko[0][:, :].bitcast(BF16)
        for c in range(8):
            K.op('pe', lambda e, c=c: e.transpose(out=pbo[:, c * NS:(c + 1) * NS], in_=osb[:, c * 128:(c + 1) * 128], identity=ident[0:NS, 0:NS]),
                 r=['junk', 'ident'], w=[bko[1]], inc=(c == 7))
        K.op('dve', lambda e: e.tensor_copy(out=aT[:, :, T:TT], in_=pbo[:, 0:8 * NS].rearrange("p (a b) -> p a b", a=8)),
             r=[bko[1]], w=[('aT', hh, NG) for hh in range(4)])

    def retention(seg):
        norm_T(0)
        tb16 = tabs[:, :].bitcast(BF16)
        cosT = tb16[:, 0:T]; sinT = tb16[:, T:2 * T]
        K.dma('pool', cosT, c_rcosT[:, seg * T:(seg + 1) * T], w=['tabs'])
        K.dma('pool', sinT, c_rsinT[:, seg * T:(seg + 1) * T], w=['tabs'])
        lreset()
        dec = la('rtab', [128, 512], BF16); gq = la('rtab', [128, 512], BF16)
        gk = la('rtab', [128, 12]); gkA = la('rtab', [128, 4 * NT]); coef = la('rtab', [128, 32])
        rsT = la('rtab', [128, 2]); id16 = la('rtab', [NS, NS])
        K.dma('pool', dec[:], c_dec[:, :], w=['rtab']); K.dma('pool', gq[:], c_gq[:, :], w=['rtab'])
        K.dma('sp', gk[:], c_gk[:, :], w=['rtab']); K.dma('sp', gkA[:], c_gkA[:, :], w=['rtab'])
        K.dma('sp', coef[:], c_coef[:, :], w=['rtab']); K.dma('sp', rsT[:], c_rsT[:, :], w=['rtab'])
        K.dma('sp', id16[:], c_id16[:, :], w=['rtab'])
        Sf = la('Sf', [128, 2, 512]); Sbf = la('Sbf', [128, 2, 512], BF16)
        Ss = la('Ss', [128, 2, 512]); Ssb = la('Ssb', [128, 2, 512], BF16)
        stat = la('stat', [128, 16])
        for k_ in ('stat2', 'stat3', 'stat4'):
            K.alias[k_] = K.alias['stat']
        qT = big[:, 0:2 * TT].rearrange("p (a b) -> p a b", a=2)
        kT = big[:, 2 * TT:4 * TT].rearrange("p (a b) -> p a b", a=2)
        vt = big[:, 4 * TT:4 * TT + (NT + 1) * 512].rearrange("p (a b) -> p a b", b=512)
        o_ = 4 * TT + (NT + 1) * 512
        kdec = big[:, o_:o_ + 256].rearrange("p (a b) -> p a b", a=1); o_ += 256
        ATb = big[:, o_:o_ + 256].rearrange("p (a b) -> p a b", a=2); o_ += 256
        qdt = big[:, o_:o_ + 512].rearrange("p (i a b) -> p i a b", i=2, a=2); o_ += 512
        yT = big[:, o_:o_ + 512].rearrange("p (i a b) -> p i a b", i=1, a=4); o_ += 512
        assert o_ <= BIGE, o_
        a_ = smal[:, 0:512]; b_ = smal[:, 512:1024]

        def proj_rope(wv, wkey, wc0, dst, dkey):
            for (n, c0, ncol) in tokgroups():
                pp = banks(2)
                for dc in range(2):
                    for kc in range(8):
                        K.op('pe', lambda e, p=pp[dc][0], kc=kc, c0=c0, ncol=ncol, dc=dc: e.matmul(
                            out=p[:, 0:ncol], lhsT=wv[:, kc, wc0 + dc * 128:wc0 + dc * 128 + 128], rhs=hT[:, kc, c0:c0 + ncol],
                            start=(kc == 0), stop=(kc == 7)), r=[wkey, ('hT', n)], w=[pp[dc][1]], inc=(kc == 7))
                p1, k1 = pp[0]; p2, k2 = pp[1]
                for half in range(2):
                    pa, ka, pb_, kb = (p1, k1, p2, k2) if half == 0 else (p2, k2, p1, k1)
                    if n < NG:
                        K.op('dve', lambda e, pa=pa, c0=c0, ncol=ncol: e.tensor_tensor(out=a_[:, 0:ncol], in0=pa[:, 0:ncol], in1=cosT[:, c0:c0 + ncol], op=ALU.mult),
                             r=[ka, 'tabs'], w=[('smal', 0)])
                        K.op('dve', lambda e, pb_=pb_, c0=c0, ncol=ncol: e.tensor_tensor(out=b_[:, 0:ncol], in0=pb_[:, 0:ncol], in1=sinT[:, c0:c0 + ncol], op=ALU.mult),
                             r=[kb, 'tabs'], w=[('smal', 1)])
                    else:
                        K.op('dve', lambda e, pa=pa, ncol=ncol: e.tensor_scalar(out=a_[:, 0:ncol], in0=pa[:, 0:ncol], scalar1=rsT[:, 0:1], scalar2=None, op0=ALU.mult),
                             r=[ka, 'rtab'], w=[('smal', 0)])
                        K.op('dve', lambda e, pb_=pb_, ncol=ncol: e.tensor_scalar(out=b_[:, 0:ncol], in0=pb_[:, 0:ncol], scalar1=rsT[:, 1:2], scalar2=None, op0=ALU.mult),
                             r=[kb, 'rtab'], w=[('smal', 1)])
                    K.op('dve', lambda e, half=half, c0=c0, ncol=ncol: e.tensor_tensor(out=dst[:, half, c0:c0 + ncol], in0=a_[:, 0:ncol], in1=b_[:, 0:ncol],
                                                                                    op=(ALU.subtract if half == 0 else ALU.add)),
                         r=[('smal', 0), ('smal', 1)], w=[(dkey, n)])

        def proj_v(wv, wkey):
            for (t, c0, nrow) in toktiles():
                p, pk = bank()
                for kc in range(8):
                    K.op('pe', lambda e, p=p, kc=kc, c0=c0, nrow=nrow: e.matmul(out=p[0:nrow, :], lhsT=hT[:, kc, c0:c0 + nrow], rhs=wv[:, kc, :],
                                                                               start=(kc == 0), stop=(kc == 7)),
                         r=[wkey, ('hT', min(t // TPG, NG))], w=[pk], inc=(kc == 7))
                K.op('act', lambda e, p=p, t=t, nrow=nrow: e.activation(out=vt[0:nrow, t, :], in_=p[0:nrow, :], func=AF.Copy), r=[pk], w=[('vt', t)])

        def kdec_tile(h, t, scal, slot):
            bk, bkk = bank()
            pb = bk[:, :].bitcast(BF16)
            for dc in range(2):
                K.op('pe', lambda e, dc=dc, t=t: e.transpose(out=pb[:, dc * 128:(dc + 1) * 128], in_=kT[:, dc, t * 128:(t + 1) * 128], identity=ident[:]),
                     r=[('kT', t // TPG), 'ident'], w=[bkk], inc=(dc == 1))
            K.op('dve', lambda e, slot=slot: e.tensor_scalar(out=kdec[:, slot, :], in0=pb[:, 0:256], scalar1=scal, scalar2=None, op0=ALU.mult),
                 r=[bkk, 'rtab'], w=[('kdec', slot)])

        RB = [(ps[6], ('ps', 6)), (ps[7], ('ps', 7))]
        def phaseB(h):
            if seg == 0:
                K.op('dve', lambda e: e.memset(Sf[:, :, :], 0.0), w=['Sf'])
            else:
                K.dma('sp', Sf[:, :, :], st_ret[h * 256:(h + 1) * 256, :].rearrange("(a p) n -> p a n", p=128), r=[('st_ret', h)], w=['Sf'])
            K.op('act', lambda e: e.activation(out=Sbf[:, :, :], in_=Sf[:, :, :], func=AF.Copy), r=['Sf'], w=['Sbf'])
            wq_, wqk = wload(ret_w_in, 0, 8, h * 256, 256)
            proj_rope(wq_, wqk, 0, qT, 'qT')
            wk_, wkk = wload(ret_w_in, 0, 8, D + h * 256, 256)
            proj_rope(wk_, wkk, 0, kT, 'kT')
            wv_, wvk = wload(ret_w_in, 0, 8, 2 * D + h * 512, 512)
            proj_v(wv_, wvk)

            def gnorm(p, pk, nrow, t):
                K.op('dve', lambda e: e.bn_stats(out=stat[0:nrow, 0:6], in_=p[0:nrow, :]), r=[pk], w=['stat'])
                K.op('dve', lambda e: e.bn_aggr(out=stat[0:nrow, 8:10], in_=stat[0:nrow, 0:6]), r=['stat'], w=['stat2'])
                K.op('act', lambda e: e.activation(out=stat[0:nrow, 10:11], in_=stat[0:nrow, 9:10], func=AF.Sqrt, bias=epsc[0:nrow, 0:1]), r=['stat2', 'epsc'], w=['stat3'])
                K.op('dve', lambda e: e.reciprocal(out=stat[0:nrow, 11:12], in_=stat[0:nrow, 10:11]), r=['stat3'], w=['stat4'])
                K.op('dve', lambda e: e.tensor_scalar(out=vt[0:nrow, t, :], in0=p[0:nrow, :], scalar1=stat[0:nrow, 8:9], scalar2=stat[0:nrow, 11:12],
                                                      op0=ALU.subtract, op1=ALU.mult), r=[pk, 'stat2', 'stat4'], w=[('vt', t)])

            for t in range(NT):
                sl = t % 2
                tc0 = t * 128
                kdec_tile(h, t, gk[:, h:h + 1], 0)
                pi, pik = bank()
                for dc in range(2):
                    K.op('pe', lambda e, dc=dc, tc0=tc0, pi=pi: e.matmul(out=pi[:, 0:128], lhsT=kT[:, dc, tc0:tc0 + 128], rhs=qT[:, dc, tc0:tc0 + 128],
                                                                        start=(dc == 0), stop=(dc == 1)),
                         r=[('kT', t // TPG), ('qT', t // TPG)], w=[pik], inc=(dc == 1))
                K.op('dve', lambda e, sl=sl, pi=pi, h=h: e.tensor_tensor(out=ATb[:, sl, :], in0=pi[:, 0:128], in1=dec[:, h * 128:(h + 1) * 128], op=ALU.mult),
                     r=[pik, 'rtab'], w=[('AT', sl)])
                K.op('dve', lambda e, sl=sl, tc0=tc0, h=h: e.tensor_tensor(out=qdt[:, sl, :, :], in0=qT[:, :, tc0:tc0 + 128],
                                                                          in1=gq[:, h * 128:(h + 1) * 128].unsqueeze(1).broadcast_to([128, 2, 128]), op=ALU.mult),
                     r=[('qT', t // TPG), 'rtab'], w=[('qdt', sl)])
                po, pok = bank()
                K.op('pe', lambda e, sl=sl, t=t, po=po: e.matmul(out=po[:, :], lhsT=ATb[:, sl, :], rhs=vt[:, t, :], start=True, stop=False),
                     r=[('AT', sl), ('vt', t)], w=[pok], inc=False)
                for dc in range(2):
                    K.op('pe', lambda e, sl=sl, dc=dc, po=po: e.matmul(out=po[:, :], lhsT=qdt[:, sl, dc, :], rhs=Sbf[:, dc, :], start=False, stop=(dc == 1)),
                         r=[('qdt', sl), 'Sbf'], w=[pok], inc=(dc == 1))
                pkv = banks(2)
                for dc in range(2):
                    K.op('pe', lambda e, sl=sl, dc=dc, t=t, p=pkv[dc][0]: e.matmul(out=p[:, :], lhsT=kdec[:, 0, dc * 128:(dc + 1) * 128], rhs=vt[:, t, :],
                                                                                  start=True, stop=True),
                         r=[('kdec', 0), ('vt', t)], w=[pkv[dc][1]], inc=True)
                for dc in range(2):
                    K.op('dve', lambda e, dc=dc, p=pkv[dc][0], h=h: e.scalar_tensor_tensor(out=Sf[:, dc, :], in0=Sf[:, dc, :], scalar=gk[:, 4 + h:5 + h], in1=p[:, :],
                                                                                          op0=ALU.mult, op1=ALU.add), r=['Sf', 'rtab', pkv[dc][1]], w=['Sf'])
                K.op('act', lambda e: e.activation(out=Sbf[:, :, :], in_=Sf[:, :, :], func=AF.Copy), r=['Sf'], w=['Sbf'])
                gnorm(po, pok, 128, t)
            if seg == NSEG - 1:
                K.dma('sp', o_retp[h].rearrange("(a p) n -> p a n", p=128), Sf[:, :, :], r=['Sf'])
            else:
                K.dma('sp', st_ret[h * 256:(h + 1) * 256, :].rearrange("(a p) n -> p a n", p=128), Sf[:, :, :], r=['Sf'], w=[('st_ret', h)])

            if XA.get('smp', True):
                bks, bksk = bank()
                pbs = bks[:, :].bitcast(BF16)
                for dc in range(2):
                    K.op('pe', lambda e, dc=dc: e.transpose(out=pbs[0:NS, dc * 128:(dc + 1) * 128], in_=kT[:, dc, T:TT], identity=ident[:]),
                         r=[('kT', NG), 'ident'], w=[bksk], inc=(dc == 1))
                kst = junk[0:NS, 0:256]
                K.op('dve', lambda e: e.tensor_copy(out=kst, in_=pbs[0:NS, 0:256]), r=[bksk], w=['junk'])
                qds_, qdsk = scratch()
                qds = qds_[:, 0:2 * NS * NS].rearrange("p (c a b) -> p c a b", c=2, a=NS)
                K.op('dve', lambda e: e.tensor_tensor(out=qds[:, :, :, :].rearrange("p c a b -> p (c a) b") if False else qds[:, 0, :, :],
                                                      in0=qT[:, 0, T:TT].unsqueeze(1).broadcast_to([128, NS, NS]), in1=idns[:, :, :], op=ALU.mult),
                     r=[('qT', NG), 'idns'], w=[qdsk])
                K.op('dve', lambda e: e.tensor_tensor(out=qds[:, 1, :, :], in0=qT[:, 1, T:TT].unsqueeze(1).broadcast_to([128, NS, NS]), in1=idns[:, :, :], op=ALU.mult),
                     r=[('qT', NG), 'idns'], w=[qdsk])
                kms = junk[0:NS, 256:512]
                for s_ in range(NS):
                    K.dma('sp', Ss[:, :, :], sret[s_, h].rearrange("(a p) n -> p a n", p=128), w=['Ss'])
                    K.op('dve', lambda e, s_=s_: e.tensor_scalar(out=kms, in0=kst, scalar1=id16[:, s_:s_ + 1], scalar2=None, op0=ALU.mult),
                         r=['junk', 'rtab'], w=['kms'])
                    pkv = banks(2)
                    for dc in range(2):
                        K.op('pe', lambda e, dc=dc, p=pkv[dc][0]: e.matmul(out=p[:, :], lhsT=kms[:, dc * 128:(dc + 1) * 128], rhs=vt[0:NS, NT, :], start=True, stop=True),
                             r=['kms', ('vt', NT)], w=[pkv[dc][1]], inc=True)
                    for dc in range(2):
                        K.op('dve', lambda e, dc=dc, p=pkv[dc][0], h=h: e.scalar_tensor_tensor(out=Ss[:, dc, :], in0=Ss[:, dc, :], scalar=gk[:, 8 + h:9 + h], in1=p[:, :],
                                                                                              op0=ALU.mult, op1=ALU.add), r=['Ss', 'rtab', pkv[dc][1]], w=['Ss'])
                    if seg == 0:
                        K.dma('sp', o_rets[s_, h].rearrange("(a p) n -> p a n", p=128), Ss[:, :, :], r=['Ss'])
                    K.op('act', lambda e: e.activation(out=Ssb[:, :, :], in_=Ss[:, :, :], func=AF.Copy), r=['Ss'], w=['Ssb'])
                    for dc in range(2):
                        K.op('pe', lambda e, dc=dc, s_=s_: e.matmul(out=RB[0][0][0:NS, :], lhsT=qds[:, dc, s_, :], rhs=Ssb[:, dc, :],
                                                                    start=(s_ == 0 and dc == 0), stop=(s_ == NS - 1 and dc == 1), skip_group_check=True),
                             r=[qdsk, 'Ssb'], w=[RB[0][1]], inc=(dc == 1))
                gnorm(RB[0][0], RB[0][1], NS, NT)

            wg_, wgk = wload(ret_w_in, 0, 8, 4 * D + h * 512, 512)
            wo_, wok = wload(ret_w_out, h * 512, 4, 0, D)
            for (t, c0, nrow) in toktiles():
                p, pk = bank()
                for kc in range(8):
                    K.op('pe', lambda e, p=p, kc=kc, c0=c0, nrow=nrow: e.matmul(out=p[0:nrow, :], lhsT=hT[:, kc, c0:c0 + nrow], rhs=wg_[:, kc, :],
                                                                               start=(kc == 0), stop=(kc == 7)),
                         r=[wgk, ('hT', min(t // TPG, NG))], w=[pk], inc=(kc == 7))
                sg, sgk = scratch()
                K.op('act', lambda e, p=p, nrow=nrow, sg=sg: e.activation(out=sg[0:nrow, 0:512], in_=p[0:nrow, :], func=AF.Silu), r=[pk], w=[sgk])
                K.op('dve', lambda e, nrow=nrow, sg=sg, t=t: e.tensor_tensor(out=vt[0:nrow, t, :], in0=vt[0:nrow, t, :], in1=sg[0:nrow, 0:512], op=ALU.mult),
                     r=[sgk, ('vt', t)], w=[('vt', t)])
                bk, bkk = bank()
                pb = bk[:, :].bitcast(BF16)
                for j in range(4):
                    K.op('pe', lambda e, j=j, t=t, nrow=nrow: e.transpose(out=pb[:, j * 128:j * 128 + nrow], in_=vt[0:nrow, t, j * 128:(j + 1) * 128],
                                                                        identity=ident[0:nrow, 0:nrow]), r=[('vt', t), 'ident'], w=[bkk], inc=(j == 3))
                ys = 0
                K.op('act', lambda e, ys=ys, nrow=nrow: e.activation(out=yT[:, ys, :, 0:nrow], in_=pb[:, 0:512].rearrange("p (a b) -> p a b", a=4)[:, :, 0:nrow],
                                                                    func=AF.Copy), r=[bkk], w=[('yT', ys)])
                for half in range(2):
                    p2, p2k = bank()
                    for j in range(4):
                        K.op('pe', lambda e, p2=p2, j=j, ys=ys, nrow=nrow, half=half: e.matmul(out=p2[0:nrow, :], lhsT=yT[:, ys, j, 0:nrow],
                                                                                            rhs=wo_[:, j, half * 512:(half + 1) * 512], start=(j == 0), stop=(j == 3)),
                             r=[wok, ('yT', ys)], w=[p2k], inc=(j == 3))
                    resid_add(t, nrow, half, p2, p2k)

        for h in range(4):
            phaseB(h)

    def conv_layer(seg):
        norm_T(6)
        lreset()
        wc = la('cvtab', [128, 24]); selc = la('cvtab', [128, 8])
        for j in range(3):
            K.dma('sp', wc[:, j * 8:(j + 1) * 8], conv_w[j:j + 1, :].rearrange("o (c p) -> p (o c)", p=128), w=['cvtab'], allow_slow_non_contiguous=True)
        bg01 = la('bg01', [128, 8, 2]); cl = la('cl', [128, 8, 2]); bufs = la('cvbufs', [128, 8, 2, NS]); cins = la('cins', [128, 8, NS])
        halo = la('halo', [128, 16]); hst = la('hst', [128, 16]); afix = la('afix', [128, 8, 2]); ufix = la('ufix', [128, 8, 2], BF16)
        uT = big[:, 0:8 * TT].rearrange("p (a b) -> p a b", a=8)
        o_ = 8 * TT + (8 * TT) % 2
        cinb = big[:, o_:o_ + 2 * (GS + 2)].bitcast(F32)
        a_ = smal[:, 0:512]; b_ = smal[:, 512:1024]
        if seg == 0:
            K.op('dve', lambda e: e.memset(halo[:, :], 0.0), w=['halo'])
        else:
            K.dma('sp', halo[:, :], st_c[:, :], r=['st_c'], w=['halo'])
        hv = halo[:, :].rearrange("p (c j) -> p c j", j=2)
        for c in range(8 if XA.get('smp', True) else 0):
            for j in range(2):
                K.dma('sp', bufs[:, c, j, :], sconv[:, j, c * 128:(c + 1) * 128].rearrange("s p -> p s"), w=['cvbufs'], allow_slow_non_contiguous=True)

        def chunk(c):
            i = st['w']; st['w'] = (i + 1) % NW
            wv = wslot[i][:, 0:8 * 3 * 128].rearrange("p (k a f) -> p k a f", k=8, a=3)
            for a3 in range(3):
                K.dma('pool', wv[:, :, a3, :], conv_w_in[:, a3 * D + c * 128:a3 * D + (c + 1) * 128].rearrange("(k p) f -> p k f", p=128), w=[('w', i)])
            wkey = ('w', i)
            K.op('dve', lambda e: e.tensor_copy(out=cinb[:, 0:2], in_=hv[:, c, :]), r=['halo'], w=['cinb'])
            for (n, c0, ncol) in tokgroups():
                pp = banks(3)
                for a3 in range(3):
                    for kc in range(8):
                        K.op('pe', lambda e, p=pp[a3][0], kc=kc, a3=a3, c0=c0, ncol=ncol: e.matmul(
                            out=p[:, 0:ncol], lhsT=wv[:, kc, a3, :], rhs=hT[:, kc, c0:c0 + ncol], start=(kc == 0), stop=(kc == 7)),
                            r=[wkey, ('hT', n)], w=[pp[a3][1]], inc=(kc == 7))
                (pbg, kbg), (pcg, kcg), (pz, kz) = pp
                K.op('act', lambda e, pz=pz, ncol=ncol: e.activation(out=a_[:, 0:ncol], in_=pz[:, 0:ncol], func=AF.Copy), r=[kz], w=[('smal', 0)])
                if n < NG:
                    K.op('dve', lambda e, pcg=pcg, c0=c0, ncol=ncol: e.tensor_tensor(out=cinb[:, 2:2 + ncol], in0=pcg[:, 0:ncol], in1=a_[:, 0:ncol], op=ALU.mult),
                         r=[kcg, ('smal', 0)], w=['cinb'])
                    if n == 0:
                        K.op('act', lambda e, pbg=pbg: e.activation(out=bg01[:, c, :], in_=pbg[:, 0:2], func=AF.Copy), r=[kbg], w=['bg01'])
                    K.op('dve', lambda e, c0=c0, ncol=ncol: e.tensor_scalar(out=b_[:, 0:ncol], in0=cinb[:, 2:2 + ncol], scalar1=wc[:, 16 + c:17 + c], scalar2=None, op0=ALU.mult),
                         r=['cinb', 'cvtab'], w=[('smal', 1)])
                    for j in (1, 0):
                        K.op('dve', lambda e, c0=c0, ncol=ncol, j=j: e.scalar_tensor_tensor(out=b_[:, 0:ncol], in0=cinb[:, j:j + ncol], scalar=wc[:, j * 8 + c:j * 8 + c + 1],
                                                                                      in1=b_[:, 0:ncol], op0=ALU.mult, op1=ALU.add), r=['cinb', 'cvtab', ('smal', 1)], w=[('smal', 1)])
                else:
                    K.op('dve', lambda e, pcg=pcg: e.tensor_tensor(out=cins[:, c, :], in0=pcg[:, 0:NS], in1=a_[:, 0:NS], op=ALU.mult), r=[kcg, ('smal', 0)], w=['cins'])
                    K.op('dve', lambda e: e.tensor_scalar(out=b_[:, 0:NS], in0=cins[:, c, :], scalar1=wc[:, 16 + c:17 + c], scalar2=None, op0=ALU.mult),
                         r=['cins', 'cvtab'], w=[('smal', 1)])
                    for j in (1, 0):
                        K.op('dve', lambda e, j=j: e.scalar_tensor_tensor(out=b_[:, 0:NS], in0=bufs[:, c, j, :], scalar=wc[:, j * 8 + c:j * 8 + c + 1], in1=b_[:, 0:NS],
                                                                        op0=ALU.mult, op1=ALU.add), r=['cvbufs', 'cvtab', ('smal', 1)], w=[('smal', 1)])
                K.op('dve', lambda e, pbg=pbg, c0=c0, ncol=ncol: e.tensor_tensor(out=uT[:, c, c0:c0 + ncol], in0=pbg[:, 0:ncol], in1=b_[:, 0:ncol], op=ALU.mult),
                     r=[kbg, ('smal', 1)], w=[('aT', c // 2, n)])
                if n < NG:
                    if n == NG - 1:
                        K.op('dve', lambda e: e.tensor_copy(out=cl[:, c, :], in_=cinb[:, GS:GS + 2]), r=['cinb'], w=['cl'])
                    K.op('dve', lambda e: e.tensor_copy(out=cinb[:, 0:2], in_=cinb[:, GS:GS + 2]), r=['cinb'], w=['cinb'])

        for c in range(8):
            chunk(c)
        if seg == NSEG - 1:
            for c in range(8):
                K.dma('sp', o_convp[:, c * 128:(c + 1) * 128].rearrange("j p -> p j"), cl[:, c, :], r=['cl'], allow_slow_non_contiguous=True)
        else:
            K.dma('sp', st_c[:, :], cl[:, :, :].rearrange("p c j -> p (c j)"), r=['cl'], w=['st_c'])
        if seg == 0:
            K.dma('sp', o_convs[:, 0, :], sconv[:, 1, :])
            for c in range(8):
                K.dma('sp', o_convs[:, 1, c * 128:(c + 1) * 128].rearrange("s p -> p s"), cins[:, c, :], r=['cins'], allow_slow_non_contiguous=True)
        wvs = [wload(conv_w_out, kh * 512, 4, 0, D) for kh in range(2)]

        def outp(rows_fn, nrow, t, keys):
            for half in range(2):
                p, pk = bank()
                for k in range(8):
                    wv, wk = wvs[k // 4]
                    K.op('pe', lambda e, p=p, k=k, wv=wv, half=half: e.matmul(out=p[0:nrow, :], lhsT=rows_fn(k), rhs=wv[:, k % 4, half * 512:(half + 1) * 512],
                                                                             start=(k == 0), stop=(k == 7)), r=[wk] + keys, w=[pk], inc=(k == 7))
                resid_add(t, nrow, half, p, pk)
        for (t, c0, nrow) in toktiles():
            outp(lambda k, c0=c0, nrow=nrow: uT[:, k, c0:c0 + nrow], nrow, t, [('aT', hh, min(t // TPG, NG)) for hh in range(4)])

    def pool_layer(seg):
        sumsq_rstd([(x[:, t, :], ('x', t)) for t in range(NT + 1)], NT + 1, 'rstd')
        Am = tabs[:, 0:1536].rearrange("p (a b) -> p a b", a=12)
        K.dma('sp', tabs[:, 0:1536], c_poolA[:, :], w=['tabs'])
        lreset()
        pB = la('pltab', [128, 8 * NS]); pC = la('pltab', [NS, 4 * NS]); selc = la('pltab', [128, 8])
        K.dma('sp', pB[:], c_poolB[:, :], w=['pltab']); K.dma('sp', pC[:], c_poolC[:, :], w=['pltab']); pass
        dT = big[:, 0:8 * TT].rearrange("p (a b) -> p a b", a=8)
        f32slots = [scr[i][:, :].bitcast(F32) for i in range(4)]
        gb, gbk = f32slots[3], ('scr', 3)
        K.dma('sp', gb, norm_g[9:10, :].partition_broadcast(128).rearrange("p o n -> p (o n)"), w=[gbk])
        wg_, wgk = wload(pool_w, 0, 8, 0, 256)
        scb = junk[:, :]
        K.dma('pool', scb, pool_scale[0:1, :].partition_broadcast(128).rearrange("p o n -> p (o n)"), w=['junk'])
        for r8 in range(8):
            K.op('dve', lambda e, r8=r8: e.tensor_tensor(out=wg_[:, r8, :], in0=wg_[:, r8, :], in1=scb[:, (r8 // 2) * 256:(r8 // 2 + 1) * 256], op=ALU.mult),
                 r=[wgk, 'junk'], w=[wgk])

        def xnf(t, slot):
            K.op('dve', lambda e: e.tensor_scalar(out=f32slots[slot][:, :], in0=x[:, t, :], scalar1=rstd[:, t:t + 1], scalar2=None, op0=ALU.mult),
                 r=[('x', t), 'rstd'], w=[('scr', slot)])

        def pooled_tile(t, cur, prv, first):
            bk = banks(2)
            for c in range(8):
                gi = c // 2
                p = bk[c // 4][0]
                col = (c % 4) * 128
                K.op('pe', lambda e, p=p, c=c, gi=gi, col=col: e.matmul(out=p[:, col:col + 128], lhsT=f32slots[cur][:, c * 128:(c + 1) * 128],
                                                                       rhs=Am[:, gi * 3 + (2 if first else 0), :], start=True, stop=False),
                     r=[('scr', cur), 'tabs'], w=[bk[c // 4][1]], inc=False)
                K.op('pe', lambda e, p=p, c=c, gi=gi, col=col: e.matmul(out=p[:, col:col + 128], lhsT=f32slots[prv][:, c * 128:(c + 1) * 128],
                                                                       rhs=Am[:, gi * 3 + 1, :], start=False, stop=True),
                     r=[('scr', prv), 'tabs'], w=[bk[c // 4][1]], inc=True)
            for c in range(8):
                p = bk[c // 4][0]
                col = (c % 4) * 128
                K.op('act', lambda e, p=p, c=c, col=col: e.activation(out=dT[:, c, t * 128:(t + 1) * 128], in_=p[:, col:col + 128], func=AF.Copy,
                                                                     scale=gcols[:, 72 + c:73 + c]), r=[bk[c // 4][1], 'gcols'], w=[('aT', c // 2, t // TPG)])

        if XA.get('smp', True):
            K.op('dve', lambda e: e.tensor_scalar(out=f32slots[0][:, :], in0=x[:, NT, :], scalar1=rstd[:, NT:NT + 1], scalar2=None, op0=ALU.mult),
                 r=[('x', NT), 'rstd'], w=[('scr', 0)])
            K.op('dve', lambda e: e.tensor_tensor(out=f32slots[0][0:NS, :], in0=f32slots[0][0:NS, :], in1=gb[0:NS, :], op=ALU.mult), r=[('scr', 0), gbk], w=[('scr', 0)])
            if seg == 0:
                K.dma('sp', o_pools[:, 14, :], f32slots[0][0:NS, :], r=[('scr', 0)])
                K.dma('sp', o_pools[:, 0:14, :], spool[:, 1:15, :])
            nh = (NS + 7) // 8
            bk = banks(2)
            for hf in range(nh):
                ns_h = min(8, NS - hf * 8)
                K.dma('sp', f32slots[1 + hf][0:ns_h * 15, :], spool[hf * 8:hf * 8 + ns_h].rearrange("s j n -> (s j) n"), w=[('scr', 1 + hf)])
            for c in range(8):
                gi = c // 2
                p = bk[c // 4][0]
                col = (c % 4) * NS
                for hf in range(nh):
                    ns_h = min(8, NS - hf * 8)
                    K.op('pe', lambda e, p=p, c=c, gi=gi, col=col, hf=hf, ns_h=ns_h: e.matmul(
                        out=p[:, col:col + NS], lhsT=f32slots[1 + hf][0:ns_h * 15, c * 128:(c + 1) * 128], rhs=pB[0:ns_h * 15, (hf * 4 + gi) * NS:(hf * 4 + gi + 1) * NS],
                        start=(hf == 0), stop=False), r=[('scr', 1 + hf), 'pltab'], w=[bk[c // 4][1]], inc=False)
                K.op('pe', lambda e, p=p, c=c, gi=gi, col=col: e.matmul(out=p[:, col:col + NS], lhsT=f32slots[0][0:NS, c * 128:(c + 1) * 128], rhs=pC[:, gi * NS:(gi + 1) * NS],
                                                                       start=False, stop=True), r=[('scr', 0), 'pltab'], w=[bk[c // 4][1]], inc=True)
            for c in range(8):
                p = bk[c // 4][0]
                col = (c % 4) * NS
                K.op('act', lambda e, p=p, c=c, col=col: e.activation(out=dT[:, c, T:TT], in_=p[:, col:col + NS], func=AF.Copy), r=[bk[c // 4][1]], w=[('aT', c // 2, NG)])
        K.op('dve', lambda e: e.memset(f32slots[2][:, :], 0.0), w=[('scr', 2)])
        if seg > 0:
            K.dma('sp', f32slots[2][112:128, :], st_p[:, :], r=['st_p'], w=[('scr', 2)])
        for t in range(NT):
            xnf(t, t % 2)
            pooled_tile(t, t % 2, 2 if t == 0 else (t - 1) % 2, (t == 0 and seg == 0))
        last = (NT - 1) % 2
        if seg == NSEG - 1:
            K.op('dve', lambda e: e.tensor_tensor(out=f32slots[2][:, :], in0=f32slots[last][:, :], in1=gb, op=ALU.mult), r=[('scr', last), gbk], w=[('scr', 2)])
            K.dma('sp', o_poolp[:, :], f32slots[2][113:128, :], r=[('scr', 2)])
        else:
            K.dma('sp', st_p[:, :], f32slots[last][112:128, :], r=[('scr', last)], w=['st_p'])
        for (t, c0, nrow) in toktiles():
            for half in range(2):
                p, pk = bank()
                for g2 in range(2):
                    g = half * 2 + g2
                    for kc in range(2):
                        K.op('pe', lambda e, p=p, g=g, g2=g2, kc=kc, c0=c0, nrow=nrow: e.matmul(out=p[0:nrow, g2 * 256:(g2 + 1) * 256], lhsT=dT[:, g * 2 + kc, c0:c0 + nrow],
                                                                                         rhs=wg_[:, g * 2 + kc, :], start=(g2 == 0 and kc == 0), stop=(g2 == 1 and kc == 1),
                                                                                         skip_group_check=True),
                             r=[wgk] + [('aT', hh, min(t // TPG, NG)) for hh in range(4)], w=[pk], inc=(g2 == 1 and kc == 1))
                resid_add(t, nrow, half, p, pk)

    def final_norm(seg):
        sumsq_rstd([(x[:, t, :], ('x', t)) for t in range(NT + 1)], NT + 1, 'rstd')
        gb_, gbk = scratch()
        gb = gb_[:, :].bitcast(F32)
        K.dma('sp', gb, final_norm_g[0:1, :].partition_broadcast(128).rearrange("p o n -> p (o n)"), w=[gbk])
        for (t, c0, nrow) in toktiles():
            K.op('dve', lambda e, t=t, nrow=nrow: e.scalar_tensor_tensor(out=x[0:nrow, t, :], in0=x[0:nrow, t, :], scalar=rstd[0:nrow, t:t + 1], in1=gb[0:nrow, :],
                                                                       op0=ALU.mult, op1=ALU.mult), r=[('x', t), 'rstd', gbk], w=[('x', t)])
            dst = yp[seg * T + t * 128:seg * T + (t + 1) * 128, :] if t < NT else ys[:, :]
            if t < NT or seg == 0:
                K.dma('sp', dst, x[0:nrow, t, :], r=[('x', t)])

    def swa_layer(seg):
        norm_T(3)
        lreset()
        swcs = la('swtab', [128, NT + 1, 16]); msk = la('swtab', [128, 3, 128], BF16); selc = la('swtab', [128, 8])
        esk = la('esk', [128, 16]); ksv = la('ksv', [128, 512]); ktn = la('ktn', [128, 4, NS], BF16)
        swqT = la('swqT', [128, 2, 8, 128], BF16)
        K.op('dve', lambda e: e.memset(swqT[:, :, :, :], 0.0), w=['swqT'])
        K.dma('sp', swcs[:, 0:NT, :], c_swcs[:, seg * NT:(seg + 1) * NT, :], w=['swtab'])
        K.dma('sp', swcs[:, NT, :], c_swcs[:, NSEG * NT, :], w=['swtab'])
        K.dma('pool', msk[:, :, :].rearrange("p a b -> p (a b)"), c_mask[:, :], w=['swtab'])
        K.dma('sp', esk[:], swa_sinks[0:1, :].partition_broadcast(128).rearrange("p o n -> p (o n)"), w=['esk'])
        K.op('act', lambda e: e.activation(out=esk[:], in_=esk[:], func=AF.Exp), r=['esk'], w=['esk'])
        KW = T + 128
        KTd = big[:, 0:4 * KW].rearrange("p (a b) -> p a b", a=4)
        Vd = big[:, 4 * KW:4 * KW + (NT + 1) * 512].rearrange("p (t k u d) -> p t k u d", k=4, u=2, d=64)
        assert 4 * KW + (NT + 1) * 512 <= BIGE
        R5 = (ps[5], ('ps', 5)); R6 = (ps[6], ('ps', 6)); R7 = (ps[7], ('ps', 7))
        tmpv = smal[:, 512:1024]

        def rope_tm(buf, H, nrow, idx, bkey):
            b3 = buf.rearrange("p (h d) -> p h d", d=64)
            x1 = b3[0:nrow, :, 0:8]; x2 = b3[0:nrow, :, 8:16]
            cs = swcs[0:nrow, idx, 0:8].unsqueeze(1).broadcast_to([nrow, H, 8])
            sn = swcs[0:nrow, idx, 8:16].unsqueeze(1).broadcast_to([nrow, H, 8])
            tv = [tmpv[0:nrow, i * 128:i * 128 + H * 8].rearrange("p (h d) -> p h d", d=8) for i in range(4)]
            K.op('dve', lambda e: e.tensor_tensor(out=tv[0], in0=x1, in1=cs, op=ALU.mult), r=[bkey, 'swtab'], w=[('smal', 1)])
            K.op('dve', lambda e: e.tensor_tensor(out=tv[1], in0=x2, in1=sn, op=ALU.mult), r=[bkey, 'swtab'], w=[('smal', 1)])
            K.op('dve', lambda e: e.tensor_tensor(out=tv[2], in0=x2, in1=cs, op=ALU.mult), r=[bkey, 'swtab'], w=[('smal', 1)])
            K.op('dve', lambda e: e.tensor_tensor(out=tv[3], in0=x1, in1=sn, op=ALU.mult), r=[bkey, 'swtab'], w=[('smal', 1)])
            K.op('dve', lambda e: e.tensor_tensor(out=x1, in0=tv[0], in1=tv[1], op=ALU.subtract), r=[('smal', 1)], w=[bkey])
            K.op('dve', lambda e: e.tensor_tensor(out=x2, in0=tv[2], in1=tv[3], op=ALU.add), r=[('smal', 1)], w=[bkey])

        def kv_store(kvf, kvkey, nrow, slot, ktdst, ktkey='KTd'):
            k4 = kvf[0:nrow, 0:256].rearrange("p (k d) -> p k d", d=64)
            v4 = kvf[0:nrow, 256:512].rearrange("p (k d) -> p k d", d=64)
            kd = junk[0:nrow, 0:512].rearrange("p (k u d) -> p k u d", k=4, u=2)
            K.op('dve', lambda e: e.tensor_copy(out=kd, in_=k4.unsqueeze(2).broadcast_to([nrow, 4, 2, 64])), r=[kvkey], w=['junk'])
            if slot is not None:
                K.op('dve', lambda e: e.tensor_copy(out=Vd[0:nrow, slot, :, :, :], in_=v4.unsqueeze(2).broadcast_to([nrow, 4, 2, 64])), r=[kvkey], w=[('Vd', slot)])
            bk, bkk = bank()
            pb = bk[:, :].bitcast(BF16)
            for kh in range(4):
                K.op('pe', lambda e, kh=kh: e.transpose(out=pb[:, kh * 128:kh * 128 + nrow], in_=junk[0:nrow, kh * 128:(kh + 1) * 128], identity=ident[0:nrow, 0:nrow]),
                     r=['junk', 'ident'], w=[bkk], inc=(kh == 3))
            K.op('act', lambda e: e.activation(out=ktdst, in_=pb[:, 0:512].rearrange("p (a b) -> p a b", a=4)[:, :, 0:nrow], func=AF.Copy), r=[bkk], w=[ktkey])

        kvf = smal[:, 0:512]
        if seg == 0:
            K.op('dve', lambda e: e.memset(kvf, 0.0), w=[('smal', 0)])
        else:
            K.dma('sp', kvf, st_h[:, :], r=['st_h'], w=[('smal', 0)])
        kv_store(kvf, ('smal', 0), 128, 0, KTd[:, :, 0:128])
        wkv, wkvk = wload(swa_w_qkv, 0, 8, 1024, 512)
        for (t, c0, nrow) in toktiles():
            p, pk = bank()
            for kc in range(8):
                K.op('pe', lambda e, p=p, kc=kc, c0=c0, nrow=nrow: e.matmul(out=p[0:nrow, :], lhsT=hT[:, kc, c0:c0 + nrow], rhs=wkv[:, kc, :], start=(kc == 0), stop=(kc == 7)),
                     r=[wkvk, ('hT', min(t // TPG, NG))], w=[pk], inc=(kc == 7))
            K.op('act', lambda e, p=p, nrow=nrow: e.activation(out=kvf[0:nrow, :], in_=p[0:nrow, :], func=AF.Copy), r=[pk], w=[('smal', 0)])
            rope_tm(kvf[:, 0:256], 4, nrow, t, ('smal', 0))
            if t < NT:
                kv_store(kvf, ('smal', 0), 128, t + 1, KTd[:, :, 128 + t * 128:128 + (t + 1) * 128])
                if t == NT - 1:
                    if seg == NSEG - 1:
                        K.dma('sp', o_swkp[:, :], kvf[:, 0:256], r=[('smal', 0)])
                        K.dma('sp', o_swvp[:, :], kvf[:, 256:512], r=[('smal', 0)])
                    else:
                        K.dma('sp', st_h[:, :], kvf[:, :], r=[('smal', 0)], w=['st_h'])
            else:
                K.op('dve', lambda e: e.tensor_copy(out=ksv[0:NS, :], in_=kvf[0:NS, :]), r=[('smal', 0)], w=['ksv'])
                kv_store(kvf, ('smal', 0), NS, None, ktn[:, :, :], 'ktn')
                if seg == 0:
                    K.dma('sp', o_swks[:, 127, :], ksv[0:NS, 0:256], r=['ksv'])
                    K.dma('sp', o_swvs[:, 127, :], ksv[0:NS, 256:512], r=['ksv'])
                    K.dma('sp', o_swks[:, 0:127, :], swk[:, 1:128, :])
                    K.dma('sp', o_swvs[:, 0:127, :], swv[:, 1:128, :])

        wq = [wload(swa_w_qkv, 0, 8, hf * 512, 512) for hf in range(2)]

        def q_block(t, c0, nrow):
            pp = banks(2)
            for hf in range(2):
                for kc in range(8):
                    K.op('pe', lambda e, p=pp[hf][0], kc=kc, hf=hf: e.matmul(out=p[0:nrow, :], lhsT=hT[:, kc, c0:c0 + nrow], rhs=wq[hf][0][:, kc, :], start=(kc == 0), stop=(kc == 7)),
                         r=[wq[hf][1], ('hT', min(t // TPG, NG))], w=[pp[hf][1]], inc=(kc == 7))
            qf_, qfk = scratch()
            qf = qf_[:, :].bitcast(F32)
            for hf in range(2):
                K.op('act', lambda e, hf=hf: e.activation(out=qf[0:nrow, hf * 512:(hf + 1) * 512], in_=pp[hf][0][0:nrow, :], func=AF.Copy), r=[pp[hf][1]], w=[qfk])
            rope_tm(qf, 16, nrow, t, qfk)
            K.op('dve', lambda e: e.tensor_copy(out=junk[0:nrow, :], in_=qf[0:nrow, :]), r=[qfk], w=['junk'])
            bk, bkk = bank()
            pb = bk[:, :].bitcast(BF16)
            for c in range(8):
                K.op('pe', lambda e, c=c: e.transpose(out=pb[:, c * 128:c * 128 + nrow], in_=junk[0:nrow, c * 128:(c + 1) * 128], identity=ident[0:nrow, 0:nrow]),
                     r=['junk', 'ident'], w=[bkk], inc=(c == 7))
            qtk = 'swqT'
            qTt = swqT[:, :, :, :]
            K.op('act', lambda e: e.activation(out=swqT[0:64, 0, :, 0:nrow], in_=pb[0:64, :].rearrange("p (a b) -> p a b", a=8)[:, :, 0:nrow], func=AF.Copy), r=[bkk], w=[qtk])
            K.op('act', lambda e: e.activation(out=swqT[64:128, 1, :, 0:nrow], in_=pb[64:128, :].rearrange("p (a b) -> p a b", a=8)[:, :, 0:nrow], func=AF.Copy), r=[bkk], w=[qtk])
            return qTt, qtk, qf, qfk

        def attend_block(t, qTt, qtk):
            def per_kh(kh):
                (po, pok), (pp_, ppk) = banks(2)
                for g in range(4):
                    ch = kh * 2 + g // 2; base = (g % 2) * 64
                    K.op('pe', lambda e, g=g, ch=ch, base=base, po=po: e.matmul(out=po[:, g * 128:(g + 1) * 128], lhsT=KTd[:, kh, 128 + t * 128:256 + t * 128],
                                                                               rhs=qTt[:, g % 2, ch, :], start=True, stop=True), r=['KTd', qtk], w=[pok], inc=(g == 3))
                for g in range(4):
                    ch = kh * 2 + g // 2; base = (g % 2) * 64
                    K.op('pe', lambda e, g=g, ch=ch, base=base, pp_=pp_: e.matmul(out=pp_[:, g * 128:(g + 1) * 128], lhsT=KTd[:, kh, t * 128:128 + t * 128],
                                                                                 rhs=qTt[:, g % 2, ch, :], start=True, stop=True), r=['KTd', qtk], w=[ppk], inc=(g == 3))
                E_, Ek = scratch()
                Eo = E_[:, 0:512]; Ep = E_[:, 512:1024]
                K.op('act', lambda e, po=po: e.activation(out=Eo, in_=po[:, :], func=AF.Exp, scale=0.125), r=[pok], w=[Ek])
                K.op('act', lambda e, pp_=pp_: e.activation(out=Ep, in_=pp_[:, :], func=AF.Exp, scale=0.125), r=[ppk], w=[Ek])
                mi = 2 if (t == 0 and seg == 0) else 1
                K.op('dve', lambda e: e.tensor_tensor(out=Eo.rearrange("p (g l) -> p g l", g=4), in0=Eo.rearrange("p (g l) -> p g l", g=4),
                                                      in1=msk[:, 0, :].unsqueeze(1).broadcast_to([128, 4, 128]), op=ALU.mult), r=[Ek, 'swtab'], w=[Ek])
                K.op('dve', lambda e: e.tensor_tensor(out=Ep.rearrange("p (g l) -> p g l", g=4), in0=Ep.rearrange("p (g l) -> p g l", g=4),
                                                      in1=msk[:, mi, :].unsqueeze(1).broadcast_to([128, 4, 128]), op=ALU.mult), r=[Ek, 'swtab'], w=[Ek])
                (pd_, pdk), (pv_, pvk) = banks(2)
                K.op('pe', lambda e, pd_=pd_: e.matmul(out=pd_[:, :], lhsT=ones[:, :], rhs=Eo, start=True, stop=False), r=['ones', Ek], w=[pdk], inc=False)
                K.op('pe', lambda e, pd_=pd_: e.matmul(out=pd_[:, :], lhsT=ones[:, :], rhs=Ep, start=False, stop=True), r=['ones', Ek], w=[pdk], inc=True)
                K.op('pe', lambda e, pv_=pv_: e.matmul(out=pv_[:, :], lhsT=Vd[:, t + 1, kh, :, :].rearrange("p u d -> p (u d)"), rhs=Eo, start=True, stop=False),
                     r=[('Vd', t + 1), Ek], w=[pvk], inc=False)
                K.op('pe', lambda e, pv_=pv_: e.matmul(out=pv_[:, :], lhsT=Vd[:, t, kh, :, :].rearrange("p u d -> p (u d)"), rhs=Ep, start=False, stop=True),
                     r=[('Vd', t), Ek], w=[pvk], inc=True)
                rd = smal[:, 0:512]
                K.op('dve', lambda e, pd_=pd_: e.tensor_tensor(out=rd.rearrange("p (g l) -> p g l", g=4), in0=pd_[:, :].rearrange("p (g l) -> p g l", g=4),
                                                              in1=esk[:, kh * 4:(kh + 1) * 4].unsqueeze(2).broadcast_to([128, 4, 128]), op=ALU.add), r=[pdk, 'esk'], w=[('smal', 0)])
                K.op('dve', lambda e: e.reciprocal(out=rd, in_=rd), r=[('smal', 0)], w=[('smal', 0)])
                for g in range(4):
                    ch = kh * 2 + g // 2; base = (g % 2) * 64
                    K.op('dve', lambda e, g=g, ch=ch, base=base, pv_=pv_: e.tensor_tensor(out=hT[base:base + 64, ch, t * 128:(t + 1) * 128], in0=pv_[base:base + 64, g * 128:(g + 1) * 128],
                                                                                         in1=rd[base:base + 64, g * 128:(g + 1) * 128], op=ALU.mult),
                         r=[pvk, ('smal', 0)], w=[('hT', t // TPG)])
            for kh in range(4):
                per_kh(kh)

        def do_block(t):
            qTt, qtk, _, _ = q_block(t, t * 128, 128)
            attend_block(t, qTt, qtk)

        for t in range(NT):
            do_block(t)

        if XA.get('smp', True):
            qTs, qsk, qf, qfk = q_block(NT, T, NS)
            prod_, prk = scratch()
            prod = prod_[:, :].bitcast(F32)
            K.op('dve', lambda e: e.tensor_tensor(out=prod[0:NS, :].rearrange("p (k g d) -> p k g d", k=4, g=4), in0=qf[0:NS, :].rearrange("p (k g d) -> p k g d", k=4, g=4),
                                                  in1=ksv[0:NS, 0:256].rearrange("p (k d) -> p k d", d=64).unsqueeze(2).broadcast_to([NS, 4, 4, 64]), op=ALU.mult),
                 r=[qfk, 'ksv'], w=[prk])
            snew = tiny[0:NS, 16:32]
            K.op('dve', lambda e: e.tensor_reduce(out=snew, in_=prod[0:NS, :].rearrange("p (h d) -> p h d", d=64), axis=AX.X, op=ALU.add), r=[prk], w=['snew'])
            K.op('act', lambda e: e.activation(out=snew, in_=snew, func=AF.Exp, scale=0.125), r=['snew'], w=['snew'])
            for s_ in range(NS):
                cb_, cbk = scratch()
                kb = cb_[:, 0:256]; vb = cb_[:, 256:512]
                K.dma('pool', kb, swk[s_], w=[cbk]); K.dma('pool', vb, swv[s_], w=[cbk])
                kd = cb_[:, 512:1024].rearrange("p (k u d) -> p k u d", k=4, u=2)
                K.op('dve', lambda e, kb=kb, kd=kd: e.tensor_copy(out=kd, in_=kb.rearrange("p (k d) -> p k d", d=64).unsqueeze(2).broadcast_to([128, 4, 2, 64])), r=[cbk], w=[cbk])
                bk, bkk = bank()
                pb = bk[:, :].bitcast(BF16)
                for kh in range(4):
                    K.op('pe', lambda e, kh=kh, cb_=cb_, pb=pb: e.transpose(out=pb[:, kh * 128:(kh + 1) * 128], in_=cb_[:, 512 + kh * 128:512 + (kh + 1) * 128], identity=ident[:]),
                         r=[cbk, 'ident'], w=[bkk], inc=(kh == 3))
                KTs = cb_[:, 1024:1536].rearrange("p (a b) -> p a b", a=4)
                K.op('act', lambda e, KTs=KTs, pb=pb: e.activation(out=KTs, in_=pb[:, 0:512].rearrange("p (a b) -> p a b", a=4), func=AF.Copy), r=[bkk], w=[cbk])
                sc_, sck = bank()
                for hq in range(16):
                    kh = hq // 4; g = hq % 4; ch = kh * 2 + g // 2; base = (g % 2) * 64
                    K.op('pe', lambda e, hq=hq, kh=kh, ch=ch, base=base, KTs=KTs, sc_=sc_, s_=s_: e.matmul(out=sc_[:, hq:hq + 1], lhsT=KTs[:, kh, :], rhs=qTs[:, (hq % 4) % 2, ch, s_:s_ + 1],
                                                                                                    start=True, stop=True), r=[cbk, qsk], w=[sck], inc=(hq == 15))
                Es = cb_[:, 1536:1552]
                K.op('act', lambda e, Es=Es, sc_=sc_: e.activation(out=Es, in_=sc_[:, 0:16], func=AF.Exp, scale=0.125), r=[sck], w=[cbk])
                Em = cb_[:, 1552:1552 + 16 * NS].rearrange("p (h s) -> p h s", h=16)
                K.op('dve', lambda e, Es=Es, Em=Em, s_=s_: e.tensor_tensor(out=Em, in0=Es.unsqueeze(2).broadcast_to([128, 16, NS]), in1=idns[:, s_, :].unsqueeze(1).broadcast_to([128, 16, NS]), op=ALU.mult),
                     r=[cbk, 'idns'], w=[cbk])
                for hq in range(16):
                    kh = hq // 4
                    Rb = R6 if hq < 8 else R7
                    K.op('pe', lambda e, hq=hq, kh=kh, Em=Em, vb=vb, Rb=Rb, s_=s_: e.matmul(out=Rb[0][0:NS, (hq % 8) * 64:(hq % 8 + 1) * 64], lhsT=Em[:, hq, :], rhs=vb[:, kh * 64:(kh + 1) * 64],
                                                                                       start=(s_ == 0 and hq % 8 == 0), stop=(s_ == NS - 1 and hq % 8 == 7), skip_group_check=True),
                         r=[cbk], w=[Rb[1]], inc=(hq % 8 == 7))
                for hq in range(16):
                    K.op('pe', lambda e, hq=hq, Em=Em, s_=s_: e.matmul(out=R5[0][0:NS, hq:hq + 1], lhsT=Em[:, hq, :], rhs=ones[:, 0:1], start=(s_ == 0 and hq == 0),
                                                                   stop=(s_ == NS - 1 and hq == 15), skip_group_check=True), r=[cbk, 'ones'], w=[R5[1]], inc=(hq == 15))
            den = tiny[0:NS, 0:16]
            K.op('dve', lambda e: e.tensor_tensor(out=den, in0=R5[0][0:NS, 0:16], in1=snew, op=ALU.add), r=[R5[1], 'snew'], w=['mx'])
            K.op('dve', lambda e: e.tensor_tensor(out=den, in0=den, in1=esk[0:NS, :], op=ALU.add), r=['mx', 'esk'], w=['mx'])
            K.op('dve', lambda e: e.reciprocal(out=den, in_=den), r=['mx'], w=['mx'])
            num = prod[0:NS, :].rearrange("p (k g d) -> p k g d", k=4, g=4)
            K.op('dve', lambda e: e.tensor_tensor(out=num, in0=ksv[0:NS, 256:512].rearrange("p (k d) -> p k d", d=64).unsqueeze(2).broadcast_to([NS, 4, 4, 64]),
                                                  in1=snew.rearrange("p (k g) -> p k g", g=4).unsqueeze(3).broadcast_to([NS, 4, 4, 64]), op=ALU.mult), r=['ksv', 'snew', prk], w=[prk])
            for b2, Rb in enumerate((R6, R7)):
                K.op('dve', lambda e, b2=b2, Rb=Rb: e.tensor_tensor(out=prod[0:NS, b2 * 512:(b2 + 1) * 512], in0=Rb[0][0:NS, :], in1=prod[0:NS, b2 * 512:(b2 + 1) * 512], op=ALU.add),
                     r=[Rb[1], prk], w=[prk])
            K.op('dve', lambda e: e.tensor_tensor(out=junk[0:NS, :].rearrange("p (h d) -> p h d", d=64), in0=prod[0:NS, :].rearrange("p (h d) -> p h d", d=64),
                                                  in1=den.unsqueeze(2).broadcast_to([NS, 16, 64]), op=ALU.mult), r=[prk, 'mx'], w=['junk'])
            bko = bank()
            pbo = bko[0][:, :].bitcast(BF16)
            for c in range(8):
                K.op('pe', lambda e, c=c: e.transpose(out=pbo[:, c * NS:(c + 1) * NS], in_=junk[0:NS, c * 128:(c + 1) * 128], identity=ident[0:NS, 0:NS]),
                     r=['junk', 'ident'], w=[bko[1]], inc=(c == 7))
            K.op('dve', lambda e: e.tensor_copy(out=hT[:, :, T:TT], in_=pbo[:, 0:8 * NS].rearrange("p (a b) -> p a b", a=8)), r=[bko[1]], w=[('hT', NG)])

        wvs = [wload(swa_w_o, kh2 * 512, 4, 0, D) for kh2 in range(2)]
        for (t, c0, nrow) in toktiles():
            for half in range(2):
                p, pk = bank()
                for k in range(8):
                    wv, wk = wvs[k // 4]
                    K.op('pe', lambda e, p=p, k=k, wv=wv, c0=c0, nrow=nrow, half=half: e.matmul(out=p[0:nrow, :], lhsT=hT[:, k, c0:c0 + nrow], rhs=wv[:, k % 4, half * 512:(half + 1) * 512],
                                                                                             start=(k == 0), stop=(k == 7)), r=[wk, ('hT', min(t // TPG, NG))], w=[pk], inc=(k == 7))
                resid_add(t, nrow, half, p, pk)

    if stage in (1, 2):
        load_x(0) if False else None
        xattn(0)
    if stage == 2:
        ffn(0)
    XA['seg'] = 0
    if stage < 10:
        load_x(0)
        if stage == 3:
            retention(0)
        if stage == 4:
            conv_layer(0)
        if stage == 5:
            pool_layer(0)
        if stage == 6:
            swa_layer(0)
    if stage >= 10:
        for seg in range(NSEG):
            XA['seg'] = seg
            XA['smp'] = (seg == 0)
            load_x(seg)
            retention(seg); xattn(0); ffn(0)
            swa_layer(seg); xattn(1); ffn(1)
            conv_layer(seg); xattn(2); ffn(2)
            pool_layer(seg); xattn(3); ffn(3)
            final_norm(seg)

    if stage < 10:
        for t in range(NT):
            K.dma('sp', dbg[t * 128:(t + 1) * 128, :], x[:, t, :], r=[('x', t)])
        K.dma('sp', dbg[T:TT, :], x[0:NS, NT, :], r=[('x', NT)])

    K.finish(['sp'])
    with nc.Block() as block:
        @block.tensor
        def _(e):
            K.emit('pe', e)

        @block.scalar
        def _(e):
            K.emit('act', e)

        @block.vector
        def _(e):
            K.emit('dve', e)

        @block.gpsimd
        def _(e):
            K.emit('pool', e)

        @block.sync
        def _(e):
            K.emit('sp', e)
    return nc


def _tables(c, T, NS):
    f32 = np.float32
    NT = T // 128
    b, s = divmod(c, 4)
    NSEG = 4
    pos = np.arange(NSEG * T).astype(f32)
    tb = {}
    tb["c_ident"] = np.eye(128, dtype=f32)
    inv = (1.0 / (f32(10000.0) ** (np.arange(128, dtype=f32) / f32(128)))).astype(f32)
    ang = (pos[None, :] * inv[:, None]).astype(f32)
    tb["c_rcosT"] = np.cos(ang).astype(f32)
    tb["c_rsinT"] = np.sin(ang).astype(f32)
    angs = (f32(PAST) * inv).astype(f32)
    tb["c_rsT"] = np.stack([np.cos(angs), np.sin(angs)], 1).astype(f32)
    logg = np.log1p(-np.exp2(-5.0 - np.arange(4, dtype=f32))).astype(f32)
    l = np.arange(128, dtype=f32)
    dec = np.zeros((128, 4, 128), f32)
    for h in range(4):
        diff = l[None, :] - l[:, None]
        dec[:, h, :] = np.where(diff >= 0, np.exp(logg[h] * np.maximum(diff, 0)), 0.0)
    tb["c_dec"] = (dec.reshape(128, 512) / 16.0).astype(f32)
    gq = np.stack([np.exp(logg[h] * (l + 1.0)) for h in range(4)], 0).astype(f32)
    tb["c_gq"] = np.tile(gq.reshape(1, 512), (128, 1)).astype(f32)
    gk = np.zeros((128, 12), f32)
    gkA = np.zeros((128, 4, NT), f32)
    for h in range(4):
        gk[:, h] = np.exp(logg[h] * (127.0 - l)) / 16.0
        gk[:, 4 + h] = np.exp(logg[h] * 128.0)
        gk[:, 8 + h] = np.exp(logg[h])
        for t in range(NT):
            gkA[:, h, t] = np.exp(logg[h] * (127.0 - l + 128.0 * (NT - 1 - t))) / 16.0
    tb["c_gk"] = gk
    tb["c_gkA"] = gkA.reshape(128, 4 * NT)
    tb["c_id16"] = (np.eye(NS, dtype=f32) / 16.0).astype(f32)
    coef = np.zeros((8, 4), f32)
    for c2 in range(8):
        if c2 // 4 == b and c2 < c:
            coef[c2] = np.exp(logg * f32(T * (c - c2 - 1)))
    tb["c_coef"] = np.tile(coef.reshape(1, 32), (128, 1)).astype(f32)
    sel = np.zeros((8,), f32)
    if s > 0:
        sel[c - 1] = 1.0
    tb["c_sel"] = np.tile(sel[None, :], (128, 1)).astype(f32)
    inv8 = (1.0 / (f32(500000.0) ** (np.arange(8, dtype=f32) / f32(8)))).astype(f32)
    a8 = (pos[:, None] * inv8[None, :]).astype(f32)
    cs = np.concatenate([np.cos(a8), np.sin(a8)], 1).astype(f32)
    sw = np.zeros((128, NSEG * NT + 1, 16), f32)
    sw[:, :NSEG * NT, :] = cs.reshape(NSEG * NT, 128, 16).transpose(1, 0, 2)
    a8s = (f32(PAST) * inv8).astype(f32)
    sw[:, NSEG * NT, :] = np.concatenate([np.cos(a8s), np.sin(a8s)])[None, :]
    tb["c_swcs"] = sw
    kk = np.arange(128)[:, None]; ll = np.arange(128)[None, :]
    m_own = (ll >= kk).astype(f32); m_prev = (ll <= kk).astype(f32)
    m_first = np.zeros_like(m_prev)
    tb["c_mask"] = np.concatenate([m_own, m_prev, m_first], 1).astype(f32)
    pa = np.zeros((128, 12, 128), f32)
    for gi, w in enumerate((2, 4, 8, 16)):
        own = np.zeros((128, 128), f32); prev = np.zeros((128, 128), f32); first = np.zeros((128, 128), f32)
        for lq in range(128):
            for i in range(w):
                lp = lq - i
                if lp >= 0:
                    own[lp, lq] += 1.0 / w
                else:
                    prev[128 + lp, lq] += 1.0 / w
            cnt = min(lq + 1, w)
            for i in range(cnt):
                first[lq - i, lq] += 1.0 / cnt
        eye = np.eye(128, dtype=f32)
        pa[:, gi * 3 + 0, :] = own - eye
        pa[:, gi * 3 + 1, :] = prev
        pa[:, gi * 3 + 2, :] = first - eye
    tb["c_poolA"] = pa.reshape(128, 12 * 128)
    pbm = np.zeros((128, 2, 4, NS), f32)
    per = 8
    for sp_ in range(NS):
        hf, r = divmod(sp_, per)
        for j in range(15):
            for gi, w in enumerate((2, 4, 8, 16)):
                if j >= 15 - (w - 1):
                    pbm[r * 15 + j, hf, gi, sp_] = 1.0 / w
    tb["c_poolB"] = pbm.reshape(128, 2 * 4 * NS)
    tb["c_poolC"] = np.concatenate([np.eye(NS, dtype=f32) * (1.0 / w - 1.0) for w in (2, 4, 8, 16)], 1).astype(f32)
    tb["c_idns"] = np.tile(np.eye(NS, dtype=f32).reshape(1, NS * NS), (128, 1))
    er = np.zeros((NS, NS, 128), f32)
    for i in range(NS):
        er[i, i, :] = 1.0
    tb["c_erow"] = er.reshape(NS, NS * 128)
    return tb


def prep(inputs, T, NS):
    f32 = np.float32
    g = lambda k: np.asarray(inputs[k], dtype=f32)
    shared = {
        "norm_g": g("norm_g").reshape(12, D), "mem_norm_g": g("mem_norm_g"), "final_norm_g": g("final_norm_g").reshape(1, D),
        "ret_w_in": g("ret_w_in")[0], "ret_w_out": g("ret_w_out")[0], "swa_w_qkv": g("swa_w_qkv")[0], "swa_w_o": g("swa_w_o")[0],
        "swa_sinks": g("swa_sinks").reshape(1, 16), "conv_w_in": g("conv_w_in")[0], "conv_w": g("conv_w")[0], "conv_w_out": g("conv_w_out")[0],
        "pool_w": g("pool_w")[0].reshape(D, 256), "pool_scale": g("pool_scale").reshape(1, D),
        "xattn_w_q": g("xattn_w_q"), "xattn_w_kv": g("xattn_w_kv"), "xattn_w_o": g("xattn_w_o"),
        "ffn_w_in": g("ffn_w_in"), "ffn_w_out": g("ffn_w_out"),
    }
    xp = g("x_prompt"); xs = g("x_sample")
    maps = []
    for c in range(NCORES):
        b, s = divmod(c, 4)
        sl = slice(c * NS, (c + 1) * NS)
        m = dict(shared)
        m["xp"] = np.ascontiguousarray(xp[b])
        m["xs"] = np.ascontiguousarray(xs[sl, 0])
        m["sret"] = np.ascontiguousarray(g("state_ret")[0, sl])
        m["swk"] = np.ascontiguousarray(g("cache_swa_k")[0, sl].reshape(NS, 128, 256))
        m["swv"] = np.ascontiguousarray(g("cache_swa_v")[0, sl].reshape(NS, 128, 256))
        m["sconv"] = np.ascontiguousarray(g("state_conv")[0, sl])
        m["spool"] = np.ascontiguousarray(g("state_pool")[0, sl])
        m["cmk"] = np.ascontiguousarray(g("cache_mem_k")[:, sl].reshape(4, NS, NMEM, D))
        m["cmv"] = np.ascontiguousarray(g("cache_mem_v")[:, sl].reshape(4, NS, NMEM, D))
        m["memp"] = np.ascontiguousarray(g("mem_prompt")[b])
        m.update(_tables(c, T, NS))
        maps.append(m)
    return maps


_NC_CACHE = {}


def kernel(**inputs):
    T = inputs["x_prompt"].shape[1] // 4
    NS = inputs["x_sample"].shape[0] // NCORES
    key = (T, NS)
    if key not in _NC_CACHE:
        _NC_CACHE[key] = build(T, NS)
    nc = _NC_CACHE[key]
    maps = prep(inputs, T, NS)
    maps = [{k: v for k, v in m.items() if k in nc.used_inputs} for m in maps]
    res = run_bass_kernel_spmd(nc, maps, core_ids=list(range(NCORES))).results
    return assemble(res, T, NS)


def assemble(res, T, NS):
    B = 2
    cat = lambda name: np.concatenate([res[c][name] for c in range(NCORES)], 0)
    y_prompt = np.stack([res[c]["yp"] for c in (0, 4)]).reshape(B, 4 * T, D)
    y_sample = cat("ys").reshape(NCORES * NS, 1, D)
    last = [0, 4]
    ret_p = np.stack([res[c]["o_retp"] for c in last])[None]
    ret_s = cat("o_rets")[None]
    swk_p = np.stack([res[c]["o_swkp"].reshape(128, 4, 64) for c in last])[None]
    swv_p = np.stack([res[c]["o_swvp"].reshape(128, 4, 64) for c in last])[None]
    swk_s = cat("o_swks").reshape(1, NCORES * NS, 128, 4, 64)
    swv_s = cat("o_swvs").reshape(1, NCORES * NS, 128, 4, 64)
    conv_p = np.stack([res[c]["o_convp"] for c in last])[None]
    conv_s = cat("o_convs")[None]
    pool_p = np.stack([res[c]["o_poolp"] for c in last])[None]
    pool_s = cat("o_pools")[None]
    memk = np.stack([res[c]["o_memk"] for c in (0, 4)], 1).reshape(4, B, NMEM, 4, 256)
    memv = np.stack([res[c]["o_memv"] for c in (0, 4)], 1).reshape(4, B, NMEM, 4, 256)
    outs = (y_prompt, y_sample, ret_p, ret_s, swk_p, swv_p, swk_s, swv_s, conv_p, conv_s, pool_p, pool_s, memk, memv)
    return tuple(np.ascontiguousarray(o, dtype=np.float32) for o in outs)
```

```python
import contextlib
import os
import numpy as np
import concourse.bass as bass
import concourse.mybir as mybir
from concourse.bass_utils import run_bass_kernel_spmd

F32 = mybir.dt.float32
BF16 = mybir.dt.bfloat16
AF = mybir.ActivationFunctionType
ALU = mybir.AluOpType
AX = mybir.AxisListType

D = 1024
EPS = 1e-6
NCORES = 8
DFF = 2816
NMEM = 256
PAST = 8192


class Trk:
    def __init__(self, nc, es):
        self.nc = nc
        self.q = {e: [] for e in ('pe', 'act', 'dve', 'pool', 'sp')}
        self.cnt = {e: 0 for e in self.q}
        self.waited = {e: {} for e in self.q}
        self.sem = {e: es.enter_context(nc.semaphore('sem_' + e)) for e in ('pe', 'act', 'dve', 'pool')}
        self.ND = 8
        self.dsem = {e: [es.enter_context(nc.semaphore(f'dsem_{e}{i}')) for i in range(self.ND)]
                     for e in ('sp', 'pool', 'act')}
        self.dval = {e: [0] * self.ND for e in self.dsem}
        self.dnext = {e: 0 for e in self.dsem}
        self.ccsem = es.enter_context(nc.semaphore('ccsem'))
        self.ccval = 0
        self.lastw = {}
        self.rd = {}
        self.open = {e: False for e in self.q}
        self.alias = {}

    def _x(self, keys):
        out = []
        for k in keys:
            out.extend(self.alias.get(k, (k,)))
        return out

    def _semof(self, k):
        if k[0] == 'e':
            return self.sem[k[1]]
        if k[0] == 'c':
            return self.ccsem
        return self.dsem[k[1][0]][k[1][1]]

    def _deps(self, eng, reads, writes):
        deps = []
        for r in reads:
            w = self.lastw.get(r)
            if w is not None:
                deps.append((w, 'raw'))
            if isinstance(r, tuple) and r[0] == 'ps':
                for r2 in self.rd.get(r, ()):
                    if not (r2[0] == 'e' and r2[1] == eng):
                        deps.append((r2, 'rar'))
        for w_ in writes:
            w = self.lastw.get(w_)
            if w is not None:
                deps.append((w, 'waw'))
            for r in self.rd.get(w_, ()):
                deps.append((r, 'war'))
        waits = []
        for (ev, kind) in deps:
            k = ev[:2]
            val = ev[2]
            if ev[0] == 'e' and ev[1] == eng:
                if eng == 'pe' or kind != 'raw':
                    continue
            if self.waited[eng].get(k, 0) >= val:
                continue
            self.waited[eng][k] = val
            waits.append((k, val))
        return waits

    def _commit(self, ev, reads, writes):
        for w_ in writes:
            self.lastw[w_] = ev
            self.rd[w_] = set()
        for r in reads:
            self.rd.setdefault(r, set()).add(ev)

    def op(self, eng, fn, r=(), w=(), inc=True):
        r = self._x(r); w = self._x(w)
        waits = self._deps(eng, r, w)
        ev = ('e', eng, self.cnt[eng] + 1)
        if inc:
            self.cnt[eng] += 1
            self.open[eng] = False
            self.q[eng].append((waits, fn, (self.sem[eng], 1)))
        else:
            self.open[eng] = True
            self.q[eng].append((waits, fn, None))
        self._commit(ev, r, w)

    def dma(self, eng, out, in_, r=(), w=(), **kw):
        r = self._x(r); w = self._x(w)
        waits = self._deps(eng, r, w)
        i = self.dnext[eng]
        self.dnext[eng] = (i + 1) % self.ND
        k = ('d', (eng, i))
        if self.dval[eng][i] > 0 and self.waited[eng].get(k, 0) < self.dval[eng][i]:
            self.waited[eng][k] = self.dval[eng][i]
            waits.append((k, self.dval[eng][i]))
        self.dval[eng][i] += 16
        ev = ('d', (eng, i), self.dval[eng][i])
        self.q[eng].append((waits, lambda e: e.dma_start(out=out, in_=in_, **kw), (self.dsem[eng][i], 16)))
        self._commit(ev, r, w)

    def cc(self, ins, outs, r=(), w=()):
        r = self._x(r); w = self._x(w)
        waits = self._deps('pool', r, w)
        self.ccval += 1
        ev = ('c', 0, self.ccval)
        self.q['pool'].append((waits, lambda e: e.collective_compute(
            "AllGather", ALU.bypass, replica_groups=[list(range(NCORES))], ins=[ins], outs=[outs]),
            (self.ccsem, 1)))
        self._commit(ev, r, w)

    def finish(self, eng_list):
        for eng in eng_list:
            waits = []
            for e2 in ('pe', 'act', 'dve', 'pool'):
                if e2 != eng and self.cnt[e2] > self.waited[eng].get(('e', e2), 0):
                    waits.append((('e', e2), self.cnt[e2]))
            for e2 in self.dsem:
                for i in range(self.ND):
                    if self.dval[e2][i] > self.waited[eng].get(('d', (e2, i)), 0):
                        waits.append((('d', (e2, i)), self.dval[e2][i]))
            self.q[eng].append((waits, None, None))

    def emit(self, name, e):
        assert not self.open[name], name
        for waits, fn, inc in self.q[name]:
            for (k, val) in waits:
                e.wait_ge(self._semof(k), val)
            if fn is None:
                continue
            ins = fn(e)
            if inc is not None:
                ins.then_inc(inc[0], inc[1])


def build(T, NS, stage=99):
    nc = bass.Bass("TRN2", target_bir_lowering=False)
    es = contextlib.ExitStack()
    NT = T // 128
    GS = min(512, T)
    NG = T // GS
    TPG = GS // 128
    TT = T + NS
    K = Trk(nc, es)
    NSEG = 4

    used = set()
    nc.used_inputs = used

    class Lz:
        def __init__(self, name, shape):
            self.name = name; self.shp = list(shape); self._ap = None

        def ap(self):
            if self._ap is None:
                self._ap = nc.dram_tensor(self.name, self.shp, F32, kind="ExternalInput").ap()
                used.add(self.name)
            return self._ap

        def __getitem__(self, idx):
            return self.ap()[idx]

        def rearrange(self, *a, **k):
            return self.ap().rearrange(*a, **k)

    def din(name, shape, dt=F32):
        return Lz(name, shape)

    def dout(name, shape):
        return nc.dram_tensor(name, list(shape), F32, kind="ExternalOutput").ap()

    xp = din("xp", [NSEG * T, D]); xs = din("xs", [NS, D])
    sret = din("sret", [NS, 4, 256, 512])
    swk = din("swk", [NS, 128, 256]); swv = din("swv", [NS, 128, 256])
    sconv = din("sconv", [NS, 2, D]); spool = din("spool", [NS, 15, D])
    cmk = din("cmk", [4, NS, NMEM, D]); cmv = din("cmv", [4, NS, NMEM, D])
    memp = din("memp", [NMEM, D])
    norm_g = din("norm_g", [12, D]); mem_norm_g = din("mem_norm_g", [4, D]); final_norm_g = din("final_norm_g", [1, D])
    ret_w_in = din("ret_w_in", [D, 6 * D]); ret_w_out = din("ret_w_out", [2 * D, D])
    swa_w_qkv = din("swa_w_qkv", [D, 1536]); swa_w_o = din("swa_w_o", [D, D]); swa_sinks = din("swa_sinks", [1, 16])
    conv_w_in = din("conv_w_in", [D, 3 * D]); conv_w = din("conv_w", [3, D]); conv_w_out = din("conv_w_out", [D, D])
    pool_w = din("pool_w", [D, 256]); pool_scale = din("pool_scale", [1, D])
    xattn_w_q = din("xattn_w_q", [4, D, D]); xattn_w_kv = din("xattn_w_kv", [4, D, 2 * D]); xattn_w_o = din("xattn_w_o", [4, D, D])
    ffn_w_in = din("ffn_w_in", [4, D, 2 * DFF]); ffn_w_out = din("ffn_w_out", [4, DFF, D])
    c_ident = din("c_ident", [128, 128])
    c_rcosT = din("c_rcosT", [128, NSEG * T]); c_rsinT = din("c_rsinT", [128, NSEG * T])
    c_rsT = din("c_rsT", [128, 2])
    c_dec = din("c_dec", [128, 4 * 128])
    c_gq = din("c_gq", [128, 4 * 128])
    c_gk = din("c_gk", [128, 12])
    c_gkA = din("c_gkA", [128, 4 * (T // 128)])
    c_id16 = din("c_id16", [NS, NS])
    c_coef = din("c_coef", [128, 32])
    c_sel = din("c_sel", [128, 8])
    c_swcs = din("c_swcs", [128, NSEG * NT + 1, 16])
    c_mask = din("c_mask", [128, 3 * 128])
    c_poolA = din("c_poolA", [128, 12 * 128])
    c_poolB = din("c_poolB", [128, 2 * 4 * NS])
    c_poolC = din("c_poolC", [NS, 4 * NS])
    c_idns = din("c_idns", [128, NS * NS])
    c_erow = din("c_erow", [NS, NS * 128])

    yp = dout("yp", [NSEG * T, D]); ys = dout("ys", [NS, D])
    o_retp = dout("o_retp", [4, 256, 512]); o_rets = dout("o_rets", [NS, 4, 256, 512])
    o_swkp = dout("o_swkp", [128, 256]); o_swvp = dout("o_swvp", [128, 256])
    o_swks = dout("o_swks", [NS, 128, 256]); o_swvs = dout("o_swvs", [NS, 128, 256])
    o_convp = dout("o_convp", [2, D]); o_convs = dout("o_convs", [NS, 2, D])
    o_poolp = dout("o_poolp", [15, D]); o_pools = dout("o_pools", [NS, 15, D])
    o_memk = dout("o_memk", [4, NMEM, D]); o_memv = dout("o_memv", [4, NMEM, D])
    dbg = dout("dbg", [TT, D]) if stage < 10 else None
    dbg2 = dout("dbg2", [128, 8192]) if stage < 10 else None

    st_ret = nc.dram_tensor("st_ret", [4 * 256, 512], F32)
    st_h = nc.dram_tensor("st_h", [128, 512], F32)
    st_c = nc.dram_tensor("st_c", [128, 16], F32)
    st_p = nc.dram_tensor("st_p", [16, D], F32)
    cc_loc_ret = nc.dram_tensor("cc_loc_ret", [4 * 256, 512], F32)
    cc_gat_ret = nc.dram_tensor("cc_gat_ret", [NCORES * 4 * 256, 512], F32)
    cc_loc_h = nc.dram_tensor("cc_loc_h", [128, 512], F32)
    cc_gat_h = nc.dram_tensor("cc_gat_h", [NCORES * 128, 512], F32)
    cc_loc_c = nc.dram_tensor("cc_loc_c", [128, 16], F32)
    cc_gat_c = nc.dram_tensor("cc_gat_c", [NCORES * 128, 16], F32)
    cc_loc_p = nc.dram_tensor("cc_loc_p", [16, D], F32)
    cc_gat_p = nc.dram_tensor("cc_gat_p", [NCORES * 16, D], F32)

    def sb(name, shape, dt=F32):
        return nc.alloc_sbuf_tensor(name, list(shape), dt)

    x = sb("x", [128, NT + 1, D])
    hT = sb("hT", [128, 8, TT], BF16)
    BIGE = 18496
    big = sb("big", [128, BIGE], BF16)
    NW = 3
    wslot = [sb(f"w{i}", [128, 4096], BF16) for i in range(NW)]
    junk = sb("junk", [128, D], BF16)
    ident = sb("ident", [128, 128], BF16)
    ones = sb("ones", [128, 128], BF16)
    gcols = sb("gcols", [128, 13 * 8])
    mgcols = sb("mgcols", [128, 4 * 8])
    ss = sb("ss", [128, NT + 1]); rs = sb("rs", [128, NT + 1]); rstd = sb("rstd", [128, NT + 1])
    tabs = sb("tabs", [128, 2048])
    scr = [sb(f"scr{i}", [128, 2048], BF16) for i in range(4)]
    LE = 7680
    Lreg = sb("Lreg", [128, LE], BF16)
    lst = {'o': 0, 'keys': set()}

    def lreset():
        lst['o'] = 0
        lst['keys'] = set()

    def la(key, shape, dt=F32):
        n = 1
        for d_ in shape[1:]:
            n *= d_
        nb = n * (2 if dt == F32 else 1)
        o = lst['o']; o += o % 2
        assert o + nb <= LE, (key, o, nb)
        lst['o'] = o + nb
        v = Lreg[0:shape[0], o:o + nb]
        if dt == F32:
            v = v.bitcast(F32)
        blocks = tuple(('L', i) for i in range(o // 512, (o + nb - 1) // 512 + 1))
        if key in lst['keys']:
            K.alias[key] = tuple(sorted(set(K.alias[key] + blocks)))
        else:
            K.alias[key] = blocks
            lst['keys'].add(key)
        if len(shape) == 3:
            v = v.rearrange("p (a b) -> p a b", a=shape[1])
        elif len(shape) == 4:
            v = v.rearrange("p (a b c) -> p a b c", a=shape[1], b=shape[2])
        return v

    smal = sb("smal", [128, 1024])
    stg = smal[:, :].rearrange("p (a b) -> p a b", a=2)
    tiny = sb("tiny", [128, 32])
    dstage = sb("dstage", [128, 8192]) if os.environ.get("KDBG") else None
    ps = [nc.alloc_psum_tensor(f"ps{i}", [128, 512], F32) for i in range(8)]

    st = {'ps': 0, 'w': 0, 'scr': 0}
    XA = {}

    def bank():
        b = st['ps']; st['ps'] = (b + 1) % 5
        return ps[b], ('ps', b)

    def banks(n):
        return [bank() for _ in range(n)]

    def wload(src2d, r0, nrc, c0, ncols, eng='pool'):
        i = st['w']; st['w'] = (i + 1) % NW
        assert nrc * ncols <= 4096
        view = wslot[i][:, 0:nrc * ncols].rearrange("p (a b) -> p a b", a=nrc)
        src = src2d[r0:r0 + 128 * nrc, c0:c0 + ncols].rearrange("(a p) n -> p a n", p=128)
        K.dma(eng, view, src, w=[('w', i)])
        return view, ('w', i)

    def scratch():
        i = st['scr']; st['scr'] = (i + 1) % 4
        return scr[i], ('scr', i)

    dstate = {'c': 0}

    def dump(ap, keys, n):
        c0 = dstate['c']; dstate['c'] += n
        npart = ap.shape[0]
        if ap.dtype != F32:
            K.op('dve', lambda e: e.tensor_copy(out=dstage[0:npart, c0:c0 + n], in_=ap), r=keys, w=['dstage'])
            K.dma('sp', dbg2[0:npart, c0:c0 + n], dstage[0:npart, c0:c0 + n], r=['dstage'])
        else:
            K.dma('sp', dbg2[0:npart, c0:c0 + n], ap, r=keys)
        return c0

    K.dma('pool', ident[:], c_ident[:, :], w=['ident'])
    K.op('dve', lambda e: e.memset(ones[:], 1.0), w=['ones'])
    K.op('dve', lambda e: e.memset(x[:, NT, :], 0.0), w=[('x', NT)])
    K.dma('sp', gcols[:, 0:96].rearrange("p (n c) -> p n c", c=8),
          norm_g.rearrange("n (c p) -> p n c", p=128), w=['gcols'], allow_slow_non_contiguous=True)
    K.dma('sp', gcols[:, 96:104], final_norm_g.rearrange("n (c p) -> p (n c)", p=128), w=['gcols'], allow_slow_non_contiguous=True)
    K.dma('sp', mgcols[:].rearrange("p (n c) -> p n c", c=8),
          mem_norm_g.rearrange("n (c p) -> p n c", p=128), w=['mgcols'], allow_slow_non_contiguous=True)
    def load_x(seg):
        for t in range(NT):
            K.dma('sp', x[:, t, :], xp[seg * T + t * 128:seg * T + (t + 1) * 128, :], w=[('x', t)])
        K.dma('sp', x[0:NS, NT, :], xs[:, :], r=[], w=[('x', NT)])

    def sumsq_rstd(src_tiles, n, keyr):
        for i, (ap, key) in enumerate(src_tiles):
            K.op('act', lambda e, ap=ap, i=i: e.activation(out=junk[:], in_=ap, func=AF.Square,
                                                            accum_out=ss[:, i:i + 1]), r=[key], w=['junk', 'ss'])
        K.op('act', lambda e: e.activation(out=rs[:, 0:n], in_=ss[:, 0:n], func=AF.Sqrt, scale=1.0 / D, bias=epsc[:, 0:1]),
             r=['ss', 'epsc'], w=['rs'])
        K.op('dve', lambda e: e.reciprocal(out=rstd[:, 0:n], in_=rs[:, 0:n]), r=['rs'], w=[keyr])

    epsc = sb("epsc", [128, 1])
    K.op('dve', lambda e: e.memset(epsc[:], EPS), w=['epsc'])

    def norm_T(gi):
        sumsq_rstd([(x[:, t, :], ('x', t)) for t in range(NT + 1)], NT + 1, 'rstd')
        for n in range(NG + 1 if XA.get('smp', True) else NG):
            tiles = list(range(n * TPG, (n + 1) * TPG)) if n < NG else [NT]
            xs_ = [scratch(), scratch()]

            def xnv(ti, xs_=xs_):
                return xs_[ti // 2][0][:, (ti % 2) * D:(ti % 2 + 1) * D], xs_[ti // 2][1]
            for ti, t in enumerate(tiles):
                K.op('dve', lambda e, ti=ti, t=t, xnv=xnv: e.tensor_scalar(out=xnv(ti)[0], in0=x[:, t, :], scalar1=rstd[:, t:t + 1],
                                                                          scalar2=None, op0=ALU.mult),
                     r=[('x', t), 'rstd'], w=[xnv(ti)[1]])
            bk = banks(4)
            for c in range(8):
                pb = bk[c // 2][0][:, :].bitcast(BF16)
                for ti, t in enumerate(tiles):
                    last = (ti == len(tiles) - 1) and (c % 2 == 1)
                    if n < NG:
                        K.op('pe', lambda e, pb=pb, c=c, ti=ti, xnv=xnv: e.transpose(
                            out=pb[:, (c % 2) * 512 + ti * 128:(c % 2) * 512 + ti * 128 + 128],
                            in_=xnv(ti)[0][:, c * 128:(c + 1) * 128], identity=ident[:]),
                            r=[xnv(ti)[1], 'ident'], w=[bk[c // 2][1]], inc=last)
                    else:
                        K.op('pe', lambda e, pb=pb, c=c, xnv=xnv: e.transpose(
                            out=pb[:, (c % 2) * 512:(c % 2) * 512 + NS],
                            in_=xnv(0)[0][0:NS, c * 128:(c + 1) * 128], identity=ident[0:NS, 0:NS]),
                            r=[xnv(0)[1], 'ident'], w=[bk[c // 2][1]], inc=last)
            ncol = GS if n < NG else NS
            c0 = n * GS if n < NG else T
            for c in range(8):
                pb = bk[c // 2][0][:, :].bitcast(BF16)
                src = pb[:, (c % 2) * 512:(c % 2) * 512 + ncol]
                dst = hT[:, c, c0:c0 + ncol]
                gc = gcols[:, gi * 8 + c:gi * 8 + c + 1]
                if c % 2 == 0:
                    K.op('act', lambda e, src=src, dst=dst, gc=gc: e.activation(out=dst, in_=src, func=AF.Copy, scale=gc),
                         r=[bk[c // 2][1], 'gcols'], w=[('hT', n)])
                else:
                    K.op('dve', lambda e, src=src, dst=dst, gc=gc: e.tensor_scalar(out=dst, in0=src, scalar1=gc, scalar2=None,
                                                                                  op0=ALU.mult),
                         r=[bk[c // 2][1], 'gcols'], w=[('hT', n)])

    def tokgroups():
        return [(n, n * GS, GS) for n in range(NG)] + ([(NG, T, NS)] if XA.get('smp', True) else [])

    def toktiles():
        return [(t, t * 128, 128) for t in range(NT)] + ([(NT, T, NS)] if XA.get('smp', True) else [])

    def projB(wv, wkey, wc0, src, srckey, evac):
        for (n, c0, ncol) in tokgroups():
            p, pk = bank()
            for kc in range(8):
                K.op('pe', lambda e, p=p, kc=kc, c0=c0, ncol=ncol: e.matmul(
                    out=p[:, 0:ncol], lhsT=wv[:, kc, wc0:wc0 + 128], rhs=src[:, kc, c0:c0 + ncol],
                    start=(kc == 0), stop=(kc == 7)), r=[wkey, (srckey, n)], w=[pk], inc=(kc == 7))
            evac(n, c0, ncol, p, pk)

    def resid_add(t, nrow, half, p, pk):
        K.op('dve', lambda e: e.tensor_tensor(out=x[0:nrow, t, half * 512:(half + 1) * 512], in0=p[0:nrow, :],
                                              in1=x[0:nrow, t, half * 512:(half + 1) * 512], op=ALU.add),
             r=[pk, ('x', t)], w=[('x', t)])

    def projA_resid(lhs_fn, lhs_keys, nk, wv, wkey):
        for (t, c0, nrow) in toktiles():
            for half in range(2):
                p, pk = bank()
                for k in range(nk):
                    K.op('pe', lambda e, p=p, k=k, c0=c0, nrow=nrow, half=half: e.matmul(
                        out=p[0:nrow, :], lhsT=lhs_fn(k, c0, nrow), rhs=wv[:, k, half * 512:(half + 1) * 512],
                        start=(k == 0), stop=(k == nk - 1)), r=[wkey] + lhs_keys(t), w=[pk], inc=(k == nk - 1))
                resid_add(t, nrow, half, p, pk)

    def ffn(l):
        norm_T(l * 3 + 2)
        gT = [big[:, i * 4 * TT:(i + 1) * 4 * TT].rearrange("p (a b) -> p a b", a=4) for i in range(2)]
        nmac = (22 + 3) // 4
        for m in range(nmac):
            chunks = list(range(m * 4, min(22, m * 4 + 4)))
            gt = gT[m % 2]
            for jj, j in enumerate(chunks):
                if jj % 2 == 0:
                    i = st['w']; st['w'] = (i + 1) % NW
                    nb = min(2, len(chunks) - jj) * 128
                    wv = wslot[i][:, 0:8 * 2 * nb].rearrange("p (k a f) -> p k a f", k=8, a=2)
                    for ab in range(2):
                        src = ffn_w_in[l][:, ab * DFF + j * 128:ab * DFF + j * 128 + nb].rearrange("(k p) f -> p k f", p=128)
                        K.dma('pool', wv[:, :, ab, :], src, w=[('w', i)])
                    wkey = ('w', i)
                fo = (jj % 2) * 128
                for (n, c0, ncol) in tokgroups():
                    pa, pak = bank()
                    pbk_ = bank()
                    pb_, pbk = pbk_
                    for ab, pp, ppk in ((0, pa, pak), (1, pb_, pbk)):
                        for kc in range(8):
                            K.op('pe', lambda e, pp=pp, kc=kc, c0=c0, ncol=ncol, ab=ab, wv=wv, fo=fo: e.matmul(
                                out=pp[:, 0:ncol], lhsT=wv[:, kc, ab, fo:fo + 128], rhs=hT[:, kc, c0:c0 + ncol],
                                start=(kc == 0), stop=(kc == 7)), r=[wkey, ('hT', n)], w=[ppk], inc=(kc == 7))
                    sa, sak = scratch()
                    K.op('act', lambda e, sa=sa, pa=pa, ncol=ncol: e.activation(out=sa[:, 0:ncol], in_=pa[:, 0:ncol], func=AF.Silu),
                         r=[pak], w=[sak])
                    K.op('dve', lambda e, sa=sa, pb_=pb_, ncol=ncol, c0=c0, gt=gt, jj=jj: e.tensor_tensor(
                        out=gt[:, jj, c0:c0 + ncol], in0=pb_[:, 0:ncol], in1=sa[:, 0:ncol], op=ALU.mult),
                        r=[pbk, sak], w=[('gT', m % 2, n)])
            nk = len(chunks)
            wo, wok = [], []
            for h2 in range((nk + 3) // 4):
                v_, k_ = wload(ffn_w_out[l], m * 512, nk, 0, D) if False else (None, None)
            wv2, wk2 = wload(ffn_w_out[l], m * 512, nk, 0, D)
            projA_resid(lambda k, c0, nrow, gt=gt: gt[:, k, c0:c0 + nrow],
                        lambda t, m=m: [('gT', m % 2, min(t // TPG, NG))], nk, wv2, wk2)

    def mem_prep():
        memn = XA['memn']
        assert 8 * TT >= 4096
        mt = hT[:, :, :].rearrange("p a b -> p (a b)")[:, 0:4096].bitcast(F32).rearrange("p (a b) -> p a b", a=2)
        hk = [('hT', n) for n in range(NG + 1)]
        for i in range(2):
            K.dma('sp', mt[:, i, :], memp[i * 128:(i + 1) * 128, :], w=hk)
        sumsq_rstd([(mt[:, i, :], hk[0]) for i in range(2)], 2, 'rstd')
        for i in range(2):
            K.op('dve', lambda e, i=i: e.tensor_scalar(out=memn[:, i, :], in0=mt[:, i, :], scalar1=rstd[:, i:i + 1], scalar2=None,
                                                      op0=ALU.mult), r=hk + ['rstd'], w=['memn'])

    def mem_kv(l):
        memn = XA['memn']; KT = XA['KT']; Vb = XA['Vb']
        mT_, mTk = scratch()
        memT = mT_[:, :].rearrange("p (a b) -> p a b", a=8)
        bk = banks(2)
        for c in range(8):
            pb = bk[c // 4][0][:, :].bitcast(BF16)
            for i in range(2):
                K.op('pe', lambda e, pb=pb, c=c, i=i: e.transpose(out=pb[:, (c % 4) * 256 + i * 128:(c % 4) * 256 + i * 128 + 128],
                                                                  in_=memn[:, i, c * 128:(c + 1) * 128], identity=ident[:]),
                     r=['memn', 'ident'], w=[bk[c // 4][1]], inc=(i == 1 and c % 4 == 3))
        for c in range(8):
            pb = bk[c // 4][0][:, :].bitcast(BF16)
            K.op('act', lambda e, pb=pb, c=c: e.activation(out=memT[:, c, :], in_=pb[:, (c % 4) * 256:(c % 4) * 256 + 256],
                                                          func=AF.Copy, scale=mgcols[:, l * 8 + c:l * 8 + c + 1]),
                 r=[bk[c // 4][1], 'mgcols'], w=[mTk])
        KMK = int(os.environ.get("KMK", "99"))
        if KMK <= 1:
            return
        for kv in range(2):
            for hf in range(2):
                wv, wk = wload(xattn_w_kv[l], 0, 8, kv * D + hf * 512, 512)
                if KMK <= 2:
                    continue
                for i in range(2):
                    p, pk = bank()
                    for kc in range(8):
                        K.op('pe', lambda e, p=p, kc=kc, i=i, wv=wv: e.matmul(out=p[:, :], lhsT=memT[:, kc, i * 128:(i + 1) * 128],
                                                                            rhs=wv[:, kc, :], start=(kc == 0), stop=(kc == 7)),
                             r=[wk, mTk], w=[pk], inc=(kc == 7))
                    if KMK <= 3:
                        continue
                    K.op('act', lambda e, p=p, i=i: e.activation(out=stg[:, i, :], in_=p[:, :], func=AF.Copy), r=[pk], w=[('smal', i)])
                    if KMK <= 4:
                        continue
                    if kv == 1:
                        K.op('act', lambda e, p=p, i=i, hf=hf: e.activation(out=Vb[:, i, hf * 512:(hf + 1) * 512], in_=p[:, :], func=AF.Copy),
                             r=[pk], w=['Vb'])
                    dst = (o_memk if kv == 0 else o_memv)[l, i * 128:(i + 1) * 128, hf * 512:(hf + 1) * 512]
                    if KMK <= 5:
                        continue
                    if XA['seg'] == 0:
                        K.dma('sp', dst, stg[:, i, :], r=[('smal', i)])
                if kv == 0 and KMK > 6:
                    for cc_ in range(4):
                        p, pk = bank()
                        for kc in range(8):
                            K.op('pe', lambda e, p=p, kc=kc, cc_=cc_, wv=wv: e.matmul(out=p[:, 0:NMEM], lhsT=wv[:, kc, cc_ * 128:(cc_ + 1) * 128],
                                                                                    rhs=memT[:, kc, :], start=(kc == 0), stop=(kc == 7)),
                                 r=[wk, mTk], w=[pk], inc=(kc == 7))
                        K.op('dve', lambda e, p=p, cc_=cc_, hf=hf: e.tensor_copy(out=KT[:, hf * 4 + cc_, :], in_=p[:, 0:NMEM]),
                             r=[pk], w=['KT'])

    def xattn(l):
        SUB = int(os.environ.get("KSUB", "99"))
        lreset()
        XA['memn'] = la('memn', [128, 2, D], BF16); XA['KT'] = la('KT', [128, 8, NMEM], BF16); XA['Vb'] = la('Vb', [128, 2, D], BF16)
        KT = XA['KT']; Vb = XA['Vb']
        mem_prep()
        norm_T(l * 3 + 1)
        if SUB <= 1:
            return
        mem_kv(l)
        if SUB <= 2:
            return
        aT = big[:, 0:8 * TT].rearrange("p (a b) -> p a b", a=8)
        o0 = 8 * TT
        PT = big[:, o0:o0 + 1024].rearrange("p (a b) -> p a b", a=2)
        rden = smal[:, 0:512]
        for hf in range(2):
            wv, wk = wload(xattn_w_q[l], 0, 8, hf * 512, 512)
            for cc_ in range(4):
                c = hf * 4 + cc_

                def ev(n, c0, ncol, p, pk, c=c):
                    K.op('act', lambda e: e.activation(out=aT[:, c, c0:c0 + ncol], in_=p[:, 0:ncol], func=AF.Copy),
                         r=[pk], w=[('aT', c // 2, n)])
                projB(wv, wk, cc_ * 128, hT, 'hT', ev)
        if SUB <= 3:
            return
        for n in range(NG):
            c0 = n * GS
            for h in range(4):
                sb_ = banks(2)
                for mc in range(2):
                    for dc in range(2):
                        K.op('pe', lambda e, mc=mc, dc=dc, p=sb_[mc][0], h=h, c0=c0: e.matmul(
                            out=p[:, 0:GS], lhsT=KT[:, h * 2 + dc, mc * 128:(mc + 1) * 128], rhs=aT[:, h * 2 + dc, c0:c0 + GS],
                            start=(dc == 0), stop=(dc == 1)), r=['KT', ('aT', h, n)], w=[sb_[mc][1]], inc=(dc == 1))
                for mc in range(2):
                    K.op('act', lambda e, mc=mc, p=sb_[mc][0]: e.activation(out=PT[:, mc, 0:GS], in_=p[:, 0:GS], func=AF.Exp, scale=1.0 / 16.0),
                         r=[sb_[mc][1]], w=[('PT', mc)])
                dn, dnk = bank()
                for mc in range(2):
                    K.op('pe', lambda e, mc=mc, dn=dn: e.matmul(out=dn[:, 0:GS], lhsT=ones[:, :], rhs=PT[:, mc, 0:GS], start=(mc == 0), stop=(mc == 1)),
                         r=['ones', ('PT', mc)], w=[dnk], inc=(mc == 1))
                ob = banks(2)
                for dc in range(2):
                    for mc in range(2):
                        K.op('pe', lambda e, mc=mc, dc=dc, p=ob[dc][0], h=h: e.matmul(
                            out=p[:, 0:GS], lhsT=Vb[:, mc, h * 256 + dc * 128:h * 256 + dc * 128 + 128], rhs=PT[:, mc, 0:GS],
                            start=(mc == 0), stop=(mc == 1)), r=['Vb', ('PT', mc)], w=[ob[dc][1]], inc=(mc == 1))
                K.op('dve', lambda e, dn=dn: e.reciprocal(out=rden[:, 0:GS], in_=dn[:, 0:GS]), r=[dnk], w=[('smal', 0)])
                for dc in range(2):
                    K.op('dve', lambda e, dc=dc, p=ob[dc][0], h=h, c0=c0: e.tensor_tensor(out=aT[:, h * 2 + dc, c0:c0 + GS], in0=p[:, 0:GS], in1=rden[:, 0:GS],
                                                                               op=ALU.mult), r=[ob[dc][1], ('smal', 0)], w=[('aT', h, n)])
        if SUB <= 4:
            return
        if XA.get('smp', True):
            xattn_samples(l, aT)
        if SUB <= 5:
            return
        for hf in range(2):
            pass
        wvs = []
        for kh in range(2):
            wvs.append(wload(xattn_w_o[l], kh * 512, 4, 0, D))
        for (t, c0, nrow) in toktiles():
            for half in range(2):
                p, pk = bank()
                for k in range(8):
                    wv, wk = wvs[k // 4]
                    K.op('pe', lambda e, p=p, k=k, wv=wv, c0=c0, nrow=nrow, half=half: e.matmul(out=p[0:nrow, :], lhsT=aT[:, k, c0:c0 + nrow],
                                                                  rhs=wv[:, k % 4, half * 512:(half + 1) * 512], start=(k == 0), stop=(k == 7)),
                         r=[wk] + [('aT', hh, min(t // TPG, NG)) for hh in range(4)], w=[pk], inc=(k == 7))
                resid_add(t, nrow, half, p, pk)

    idns = sb("idns", [128, NS, NS], BF16)
    K.dma('pool', idns[:].rearrange("p a b -> p (a b)"), c_idns[:, :], w=['idns'])

    def xattn_samples(l, aT):
        qm = [la('qm0', [128, 8, NS], BF16), la('qm1', [128, 8, NS], BF16)]
        pm = [la('pm0', [128, 8, NS], BF16), la('pm1', [128, 8, NS], BF16)]
        PTs = la('PTs', [128, 8, NS], BF16)
        sc = [(ps[6], ('ps', 6)), (ps[7], ('ps', 7))]
        for s in range(NS):
            K.op('dve', lambda e, s=s: e.tensor_tensor(out=qm[s % 2], in0=aT[:, :, T:TT], in1=idns[:, s, :].unsqueeze(1).broadcast_to([128, 8, NS]), op=ALU.mult),
                 r=[('aT', hh, NG) for hh in range(4)] + ['idns'], w=['qm%d' % (s % 2)])
            kb, kbk = scratch()
            kbv = kb[:, :].rearrange("p (a b) -> p a b", a=2)
            K.dma('pool', kbv, cmk[l, s].rearrange("(a p) n -> p a n", p=128), w=[kbk])
            kt, ktk = scratch()
            ktv = kt[:, :].rearrange("p (a b) -> p a b", a=8)
            for g4 in range(2):
                bk = banks(2)
                for c4 in range(4):
                    c = g4 * 4 + c4
                    pb = bk[c4 // 2][0][:, :].bitcast(BF16)
                    for i in range(2):
                        K.op('pe', lambda e, pb=pb, c=c, c4=c4, i=i, kbv=kbv: e.transpose(
                            out=pb[:, (c4 % 2) * 256 + i * 128:(c4 % 2) * 256 + i * 128 + 128],
                            in_=kbv[:, i, c * 128:(c + 1) * 128], identity=ident[:]), r=[kbk, 'ident'], w=[bk[c4 // 2][1]],
                            inc=(i == 1 and c4 % 2 == 1))
                for b2 in range(2):
                    pb = bk[b2][0][:, :].bitcast(BF16)
                    eng = 'act' if b2 == 0 else 'dve'
                    dst = ktv[:, g4 * 4 + b2 * 2:g4 * 4 + b2 * 2 + 2, :]
                    if eng == 'act':
                        K.op('act', lambda e, pb=pb, dst=dst: e.activation(out=dst, in_=pb[:, 0:512].rearrange("p (a b) -> p a b", a=2), func=AF.Copy),
                             r=[bk[b2][1]], w=[ktk])
                    else:
                        K.op('dve', lambda e, pb=pb, dst=dst: e.tensor_copy(out=dst, in_=pb[:, 0:512].rearrange("p (a b) -> p a b", a=2)),
                             r=[bk[b2][1]], w=[ktk])
            for h in range(4):
                for dc in range(2):
                    first = (s == 0 and h % 2 == 0 and dc == 0)
                    lastm = (s == NS - 1 and dc == 1)
                    K.op('pe', lambda e, h=h, dc=dc, s=s, first=first, ktv=ktv: e.matmul(
                        out=sc[h // 2][0][0:NS, (h % 2) * 256:(h % 2) * 256 + 256], lhsT=qm[s % 2][:, h * 2 + dc, :], rhs=ktv[:, h * 2 + dc, :],
                        start=first, stop=(s == NS - 1 and dc == 1 and h % 2 == 1), skip_group_check=True),
                        r=['qm%d' % (s % 2), ktk], w=[sc[h // 2][1]], inc=True if (dc == 1) else False)
        if dstage is not None and l == 0:
            for b2 in range(2):
                K.op('act', lambda e, b2=b2: e.activation(out=tabs[0:NS, b2 * 512:(b2 + 1) * 512], in_=sc[b2][0][0:NS, :], func=AF.Copy), r=[sc[b2][1]], w=['tabsdbg'])
            dump(tabs[0:NS, 0:1024], ['tabsdbg'], 1024)
            dump(ktv[:, 0, :], [ktk], 256)
        mx = tiny[0:NS, 0:4]; nmx = tiny[0:NS, 4:8]; sm = tiny[0:NS, 8:12]; rsm = tiny[0:NS, 12:16]
        Pf = smal[0:NS, 0:1024]
        Pb = junk[0:NS, :]
        for h in range(4):
            src = sc[h // 2][0][0:NS, (h % 2) * 256:(h % 2) * 256 + 256]
            K.op('dve', lambda e, src=src, h=h: e.tensor_reduce(out=mx[:, h:h + 1], in_=src, axis=AX.X, op=ALU.max), r=[sc[h // 2][1]], w=['mx'])
        K.op('dve', lambda e: e.tensor_scalar(out=nmx, in0=mx, scalar1=-1.0 / 16.0, scalar2=None, op0=ALU.mult), r=['mx'], w=['nmx'])
        for h in range(4):
            src = sc[h // 2][0][0:NS, (h % 2) * 256:(h % 2) * 256 + 256]
            K.op('act', lambda e, src=src, h=h: e.activation(out=Pf[:, h * 256:(h + 1) * 256], in_=src, func=AF.Exp, scale=1.0 / 16.0,
                                                            bias=nmx[:, h:h + 1], accum_out=sm[:, h:h + 1]), r=[sc[h // 2][1], 'nmx'], w=[('smal', 0), ('smal', 1), 'sm'])
        K.op('dve', lambda e: e.reciprocal(out=rsm, in_=sm), r=['sm'], w=['rsm'])
        for h in range(4):
            K.op('dve', lambda e, h=h: e.tensor_scalar(out=Pb[:, h * 256:(h + 1) * 256], in0=Pf[:, h * 256:(h + 1) * 256], scalar1=rsm[:, h:h + 1],
                                                      scalar2=None, op0=ALU.mult), r=[('smal', 0), ('smal', 1), 'rsm'], w=['junk'])
        if dstage is not None and l == 0:
            dump(Pb, ['junk'], 1024)
            dump(aT[:, :, T:TT].rearrange("p a b -> p (a b)") if False else aT[:, 0, T:TT], [('aT', 0, NG)], NS)
        bkp = bank()
        pbp = bkp[0][:, :].bitcast(BF16)
        for j in range(8):
            K.op('pe', lambda e, j=j: e.transpose(out=pbp[:, j * NS:(j + 1) * NS], in_=Pb[:, j * 128:(j + 1) * 128], identity=ident[0:NS, 0:NS]),
                 r=['junk', 'ident'], w=[bkp[1]], inc=(j == 7))
        K.op('dve', lambda e: e.tensor_copy(out=PTs[:, :, :], in_=pbp[:, 0:8 * NS].rearrange("p (a b) -> p a b", a=8)), r=[bkp[1]], w=['PTs'])
        ob = [(ps[6], ('ps', 6)), (ps[7], ('ps', 7))]
        for s in range(NS):
            K.op('dve', lambda e, s=s: e.tensor_tensor(out=pm[s % 2], in0=PTs[:, :, :], in1=idns[:, s, :].unsqueeze(1).broadcast_to([128, 8, NS]), op=ALU.mult),
                 r=['PTs', 'idns'], w=['pm%d' % (s % 2)])
            vb, vbk = scratch()
            vbv = vb[:, :].rearrange("p (a b) -> p a b", a=2)
            K.dma('pool', vbv, cmv[l, s].rearrange("(a p) n -> p a n", p=128), w=[vbk])
            for h in range(4):
                for mc in range(2):
                    first = (s == 0 and h % 2 == 0 and mc == 0)
                    K.op('pe', lambda e, h=h, mc=mc, s=s, first=first, vbv=vbv: e.matmul(
                        out=ob[h // 2][0][0:NS, (h % 2) * 256:(h % 2) * 256 + 256], lhsT=pm[s % 2][:, h * 2 + mc, :], rhs=vbv[:, mc, h * 256:(h + 1) * 256],
                        start=first, stop=(s == NS - 1 and mc == 1 and h % 2 == 1), skip_group_check=True),
                        r=['pm%d' % (s % 2), vbk], w=[ob[h // 2][1]], inc=(mc == 1))
        osb = junk[0:NS, :]
        for b2 in range(2):
            K.op('act', lambda e, b2=b2: e.activation(out=osb[:, b2 * 512:(b2 + 1) * 512], in_=ob[b2][0][0:NS, :], func=AF.Copy), r=[ob[b2][1]], w=['junk'])
        if dstage is not None and l == 0:
            dump(osb, ['junk'], 1024)
        bko = bank()
        pbo = bko[0][:, :].bitcast(BF16)
        for c in range(8):
            K.op('pe', lambda e, c=c: e.transpose(out=pbo[:, c * NS:(c + 1) * NS], in_=osb[:, c * 128:(c + 1) * 128], identity=ident[0:NS, 0:NS]),
                 r=['junk', 'ident'], w=[bko[1]], inc=(c == 7))
        K.op('dve', lambda e: e.tensor_copy(out=aT[:, :, T:TT], in_=pbo[:, 0:8 * NS].rearrange("p (a b) -> p a b", a=8)),
             r=[bko[1]], w=[('aT', hh, NG) for hh in range(4)])

    def retention(seg):
        norm_T(0)
        tb16 = tabs[:, :].bitcast(BF16)
        cosT = tb16[:, 0:T]; sinT = tb16[:, T:2 * T]
        K.dma('pool', cosT, c_rcosT[:, seg * T:(seg + 1) * T], w=['tabs'])
        K.dma('pool', sinT, c_rsinT[:, seg * T:(seg + 1) * T], w=['tabs'])
        lreset()
        dec = la('rtab', [128, 512], BF16); gq = la('rtab', [128, 512], BF16)
        gk = la('rtab', [128, 12]); gkA = la('rtab', [128, 4 * NT]); coef = la('rtab', [128, 32])
        rsT = la('rtab', [128, 2]); id16 = la('rtab', [NS, NS])
        K.dma('pool', dec[:], c_dec[:, :], w=['rtab']); K.dma('pool', gq[:], c_gq[:, :], w=['rtab'])
        K.dma('sp', gk[:], c_gk[:, :], w=['rtab']); K.dma('sp', gkA[:], c_gkA[:, :], w=['rtab'])
        K.dma('sp', coef[:], c_coef[:, :], w=['rtab']); K.dma('sp', rsT[:], c_rsT[:, :], w=['rtab'])
        K.dma('sp', id16[:], c_id16[:, :], w=['rtab'])
        Sf = la('Sf', [128, 2, 512]); Sbf = la('Sbf', [128, 2, 512], BF16)
        Ss = la('Ss', [128, 2, 512]); Ssb = la('Ssb', [128, 2, 512], BF16)
        stat = la('stat', [128, 16])
        for k_ in ('stat2', 'stat3', 'stat4'):
            K.alias[k_] = K.alias['stat']
        qT = big[:, 0:2 * TT].rearrange("p (a b) -> p a b", a=2)
        kT = big[:, 2 * TT:4 * TT].rearrange("p (a b) -> p a b", a=2)
        vt = big[:, 4 * TT:4 * TT + (NT + 1) * 512].rearrange("p (a b) -> p a b", b=512)
        o_ = 4 * TT + (NT + 1) * 512
        kdec = big[:, o_:o_ + 256].rearrange("p (a b) -> p a b", a=1); o_ += 256
        ATb = big[:, o_:o_ + 256].rearrange("p (a b) -> p a b", a=2); o_ += 256
        qdt = big[:, o_:o_ + 512].rearrange("p (i a b) -> p i a b", i=2, a=2); o_ += 512
        yT = big[:, o_:o_ + 512].rearrange("p (i a b) -> p i a b", i=1, a=4); o_ += 512
        assert o_ <= BIGE, o_
        a_ = smal[:, 0:512]; b_ = smal[:, 512:1024]

        def proj_rope(wv, wkey, wc0, dst, dkey):
            for (n, c0, ncol) in tokgroups():
                pp = banks(2)
                for dc in range(2):
                    for kc in range(8):
                        K.op('pe', lambda e, p=pp[dc][0], kc=kc, c0=c0, ncol=ncol, dc=dc: e.matmul(
                            out=p[:, 0:ncol], lhsT=wv[:, kc, wc0 + dc * 128:wc0 + dc * 128 + 128], rhs=hT[:, kc, c0:c0 + ncol],
                            start=(kc == 0), stop=(kc == 7)), r=[wkey, ('hT', n)], w=[pp[dc][1]], inc=(kc == 7))
                p1, k1 = pp[0]; p2, k2 = pp[1]
                for half in range(2):
                    pa, ka, pb_, kb = (p1, k1, p2, k2) if half == 0 else (p2, k2, p1, k1)
                    if n < NG:
                        K.op('dve', lambda e, pa=pa, c0=c0, ncol=ncol: e.tensor_tensor(out=a_[:, 0:ncol], in0=pa[:, 0:ncol], in1=cosT[:, c0:c0 + ncol], op=ALU.mult),
                             r=[ka, 'tabs'], w=[('smal', 0)])
                        K.op('dve', lambda e, pb_=pb_, c0=c0, ncol=ncol: e.tensor_tensor(out=b_[:, 0:ncol], in0=pb_[:, 0:ncol], in1=sinT[:, c0:c0 + ncol], op=ALU.mult),
                             r=[kb, 'tabs'], w=[('smal', 1)])
                    else:
                        K.op('dve', lambda e, pa=pa, ncol=ncol: e.tensor_scalar(out=a_[:, 0:ncol], in0=pa[:, 0:ncol], scalar1=rsT[:, 0:1], scalar2=None, op0=ALU.mult),
                             r=[ka, 'rtab'], w=[('smal', 0)])
                        K.op('dve', lambda e, pb_=pb_, ncol=ncol: e.tensor_scalar(out=b_[:, 0:ncol], in0=pb_[:, 0:ncol], scalar1=rsT[:, 1:2], scalar2=None, op0=ALU.mult),
                             r=[kb, 'rtab'], w=[('smal', 1)])
                    K.op('dve', lambda e, half=half, c0=c0, ncol=ncol: e.tensor_tensor(out=dst[:, half, c0:c0 + ncol], in0=a_[:, 0:ncol], in1=b_[:, 0:ncol],
                                                                                    op=(ALU.subtract if half == 0 else ALU.add)),
                         r=[('smal', 0), ('smal', 1)], w=[(dkey, n)])

        def proj_v(wv, wkey):
            for (t, c0, nrow) in toktiles():
                p, pk = bank()
                for kc in range(8):
                    K.op('pe', lambda e, p=p, kc=kc, c0=c0, nrow=nrow: e.matmul(out=p[0:nrow, :], lhsT=hT[:, kc, c0:c0 + nrow], rhs=wv[:, kc, :],
                                                                               start=(kc == 0), stop=(kc == 7)),
                         r=[wkey, ('hT', min(t // TPG, NG))], w=[pk], inc=(kc == 7))
                K.op('act', lambda e, p=p, t=t, nrow=nrow: e.activation(out=vt[0:nrow, t, :], in_=p[0:nrow, :], func=AF.Copy), r=[pk], w=[('vt', t)])

        def kdec_tile(h, t, scal, slot):
            bk, bkk = bank()
            pb = bk[:, :].bitcast(BF16)
            for dc in range(2):
                K.op('pe', lambda e, dc=dc, t=t: e.transpose(out=pb[:, dc * 128:(dc + 1) * 128], in_=kT[:, dc, t * 128:(t + 1) * 128], identity=ident[:]),
                     r=[('kT', t // TPG), 'ident'], w=[bkk], inc=(dc == 1))
            K.op('dve', lambda e, slot=slot: e.tensor_scalar(out=kdec[:, slot, :], in0=pb[:, 0:256], scalar1=scal, scalar2=None, op0=ALU.mult),
                 r=[bkk, 'rtab'], w=[('kdec', slot)])

        RB = [(ps[6], ('ps', 6)), (ps[7], ('ps', 7))]
        def phaseB(h):
            if seg == 0:
                K.op('dve', lambda e: e.memset(Sf[:, :, :], 0.0), w=['Sf'])
            else:
                K.dma('sp', Sf[:, :, :], st_ret[h * 256:(h + 1) * 256, :].rearrange("(a p) n -> p a n", p=128), r=[('st_ret', h)], w=['Sf'])
            K.op('act', lambda e: e.activation(out=Sbf[:, :, :], in_=Sf[:, :, :], func=AF.Copy), r=['Sf'], w=['Sbf'])
            wq_, wqk = wload(ret_w_in, 0, 8, h * 256, 256)
            proj_rope(wq_, wqk, 0, qT, 'qT')
            wk_, wkk = wload(ret_w_in, 0, 8, D + h * 256, 256)
            proj_rope(wk_, wkk, 0, kT, 'kT')
            wv_, wvk = wload(ret_w_in, 0, 8, 2 * D + h * 512, 512)
            proj_v(wv_, wvk)

            def gnorm(p, pk, nrow, t):
                K.op('dve', lambda e: e.bn_stats(out=stat[0:nrow, 0:6], in_=p[0:nrow, :]), r=[pk], w=['stat'])
                K.op('dve', lambda e: e.bn_aggr(out=stat[0:nrow, 8:10], in_=stat[0:nrow, 0:6]), r=['stat'], w=['stat2'])
                K.op('act', lambda e: e.activation(out=stat[0:nrow, 10:11], in_=stat[0:nrow, 9:10], func=AF.Sqrt, bias=epsc[0:nrow, 0:1]), r=['stat2', 'epsc'], w=['stat3'])
                K.op('dve', lambda e: e.reciprocal(out=stat[0:nrow, 11:12], in_=stat[0:nrow, 10:11]), r=['stat3'], w=['stat4'])
                K.op('dve', lambda e: e.tensor_scalar(out=vt[0:nrow, t, :], in0=p[0:nrow, :], scalar1=stat[0:nrow, 8:9], scalar2=stat[0:nrow, 11:12],
                                                      op0=ALU.subtract, op1=ALU.mult), r=[pk, 'stat2', 'stat4'], w=[('vt', t)])

            for t in range(NT):
                sl = t % 2
                tc0 = t * 128
                kdec_tile(h, t, gk[:, h:h + 1], 0)
                pi, pik = bank()
                for dc in range(2):
                    K.op('pe', lambda e, dc=dc, tc0=tc0, pi=pi: e.matmul(out=pi[:, 0:128], lhsT=kT[:, dc, tc0:tc0 + 128], rhs=qT[:, dc, tc0:tc0 + 128],
                                                                        start=(dc == 0), stop=(dc == 1)),
                         r=[('kT', t // TPG), ('qT', t // TPG)], w=[pik], inc=(dc == 1))
                K.op('dve', lambda e, sl=sl, pi=pi, h=h: e.tensor_tensor(out=ATb[:, sl, :], in0=pi[:, 0:128], in1=dec[:, h * 128:(h + 1) * 128], op=ALU.mult),
                     r=[pik, 'rtab'], w=[('AT', sl)])
                K.op('dve', lambda e, sl=sl, tc0=tc0, h=h: e.tensor_tensor(out=qdt[:, sl, :, :], in0=qT[:, :, tc0:tc0 + 128],
                                                                          in1=gq[:, h * 128:(h + 1) * 128].unsqueeze(1).broadcast_to([128, 2, 128]), op=ALU.mult),
                     r=[('qT', t // TPG), 'rtab'], w=[('qdt', sl)])
                po, pok = bank()
                K.op('pe', lambda e, sl=sl, t=t, po=po: e.matmul(out=po[:, :], lhsT=ATb[:, sl, :], rhs=vt[:, t, :], start=True, stop=False),
                     r=[('AT', sl), ('vt', t)], w=[pok], inc=False)
                for dc in range(2):
                    K.op('pe', lambda e, sl=sl, dc=dc, po=po: e.matmul(out=po[:, :], lhsT=qdt[:, sl, dc, :], rhs=Sbf[:, dc, :], start=False, stop=(dc == 1)),
                         r=[('qdt', sl), 'Sbf'], w=[pok], inc=(dc == 1))
                pkv = banks(2)
                for dc in range(2):
                    K.op('pe', lambda e, sl=sl, dc=dc, t=t, p=pkv[dc][0]: e.matmul(out=p[:, :], lhsT=kdec[:, 0, dc * 128:(dc + 1) * 128], rhs=vt[:, t, :],
                                                                                  start=True, stop=True),
                         r=[('kdec', 0), ('vt', t)], w=[pkv[dc][1]], inc=True)
                for dc in range(2):
                    K.op('dve', lambda e, dc=dc, p=pkv[dc][0], h=h: e.scalar_tensor_tensor(out=Sf[:, dc, :], in0=Sf[:, dc, :], scalar=gk[:, 4 + h:5 + h], in1=p[:, :],
                                                                                          op0=ALU.mult, op1=ALU.add), r=['Sf', 'rtab', pkv[dc][1]], w=['Sf'])
                K.op('act', lambda e: e.activation(out=Sbf[:, :, :], in_=Sf[:, :, :], func=AF.Copy), r=['Sf'], w=['Sbf'])
                gnorm(po, pok, 128, t)
            if seg == NSEG - 1:
                K.dma('sp', o_retp[h].rearrange("(a p) n -> p a n", p=128), Sf[:, :, :], r=['Sf'])
            else:
                K.dma('sp', st_ret[h * 256:(h + 1) * 256, :].rearrange("(a p) n -> p a n", p=128), Sf[:, :, :], r=['Sf'], w=[('st_ret', h)])

            if XA.get('smp', True):
                bks, bksk = bank()
                pbs = bks[:, :].bitcast(BF16)
                for dc in range(2):
                    K.op('pe', lambda e, dc=dc: e.transpose(out=pbs[0:NS, dc * 128:(dc + 1) * 128], in_=kT[:, dc, T:TT], identity=ident[:]),
                         r=[('kT', NG), 'ident'], w=[bksk], inc=(dc == 1))
                kst = junk[0:NS, 0:256]
                K.op('dve', lambda e: e.tensor_copy(out=kst, in_=pbs[0:NS, 0:256]), r=[bksk], w=['junk'])
                qds_, qdsk = scratch()
                qds = qds_[:, 0:2 * NS * NS].rearrange("p (c a b) -> p c a b", c=2, a=NS)
                K.op('dve', lambda e: e.tensor_tensor(out=qds[:, :, :, :].rearrange("p c a b -> p (c a) b") if False else qds[:, 0, :, :],
                                                      in0=qT[:, 0, T:TT].unsqueeze(1).broadcast_to([128, NS, NS]), in1=idns[:, :, :], op=ALU.mult),
                     r=[('qT', NG), 'idns'], w=[qdsk])
                K.op('dve', lambda e: e.tensor_tensor(out=qds[:, 1, :, :], in0=qT[:, 1, T:TT].unsqueeze(1).broadcast_to([128, NS, NS]), in1=idns[:, :, :], op=ALU.mult),
                     r=[('qT', NG), 'idns'], w=[qdsk])
                kms = junk[0:NS, 256:512]
                for s_ in range(NS):
                    K.dma('sp', Ss[:, :, :], sret[s_, h].rearrange("(a p) n -> p a n", p=128), w=['Ss'])
                    K.op('dve', lambda e, s_=s_: e.tensor_scalar(out=kms, in0=kst, scalar1=id16[:, s_:s_ + 1], scalar2=None, op0=ALU.mult),
                         r=['junk', 'rtab'], w=['kms'])
                    pkv = banks(2)
                    for dc in range(2):
                        K.op('pe', lambda e, dc=dc, p=pkv[dc][0]: e.matmul(out=p[:, :], lhsT=kms[:, dc * 128:(dc + 1) * 128], rhs=vt[0:NS, NT, :], start=True, stop=True),
                             r=['kms', ('vt', NT)], w=[pkv[dc][1]], inc=True)
                    for dc in range(2):
                        K.op('dve', lambda e, dc=dc, p=pkv[dc][0], h=h: e.scalar_tensor_tensor(out=Ss[:, dc, :], in0=Ss[:, dc, :], scalar=gk[:, 8 + h:9 + h], in1=p[:, :],
                                                                                              op0=ALU.mult, op1=ALU.add), r=['Ss', 'rtab', pkv[dc][1]], w=['Ss'])
                    if seg == 0:
                        K.dma('sp', o_rets[s_, h].rearrange("(a p) n -> p a n", p=128), Ss[:, :, :], r=['Ss'])
                    K.op('act', lambda e: e.activation(out=Ssb[:, :, :], in_=Ss[:, :, :], func=AF.Copy), r=['Ss'], w=['Ssb'])
                    for dc in range(2):
                        K.op('pe', lambda e, dc=dc, s_=s_: e.matmul(out=RB[0][0][0:NS, :], lhsT=qds[:, dc, s_, :], rhs=Ssb[:, dc, :],
                                                                    start=(s_ == 0 and dc == 0), stop=(s_ == NS - 1 and dc == 1), skip_group_check=True),
                             r=[qdsk, 'Ssb'], w=[RB[0][1]], inc=(dc == 1))
                gnorm(RB[0][0], RB[0][1], NS, NT)

            wg_, wgk = wload(ret_w_in, 0, 8, 4 * D + h * 512, 512)
            wo_, wok = wload(ret_w_out, h * 512, 4, 0, D)
            for (t, c0, nrow) in toktiles():
                p, pk = bank()
                for kc in range(8):
                    K.op('pe', lambda e, p=p, kc=kc, c0=c0, nrow=nrow: e.matmul(out=p[0:nrow, :], lhsT=hT[:, kc, c0:c0 + nrow], rhs=wg_[:, kc, :],
                                                                               start=(kc == 0), stop=(kc == 7)),
                         r=[wgk, ('hT', min(t // TPG, NG))], w=[pk], inc=(kc == 7))
                sg, sgk = scratch()
                K.op('act', lambda e, p=p, nrow=nrow, sg=sg: e.activation(out=sg[0:nrow, 0:512], in_=p[0:nrow, :], func=AF.Silu), r=[pk], w=[sgk])
                K.op('dve', lambda e, nrow=nrow, sg=sg, t=t: e.tensor_tensor(out=vt[0:nrow, t, :], in0=vt[0:nrow, t, :], in1=sg[0:nrow, 0:512], op=ALU.mult),
                     r=[sgk, ('vt', t)], w=[('vt', t)])
                bk, bkk = bank()
                pb = bk[:, :].bitcast(BF16)
                for j in range(4):
                    K.op('pe', lambda e, j=j, t=t, nrow=nrow: e.transpose(out=pb[:, j * 128:j * 128 + nrow], in_=vt[0:nrow, t, j * 128:(j + 1) * 128],
                                                                        identity=ident[0:nrow, 0:nrow]), r=[('vt', t), 'ident'], w=[bkk], inc=(j == 3))
                ys = 0
                K.op('act', lambda e, ys=ys, nrow=nrow: e.activation(out=yT[:, ys, :, 0:nrow], in_=pb[:, 0:512].rearrange("p (a b) -> p a b", a=4)[:, :, 0:nrow],
                                                                    func=AF.Copy), r=[bkk], w=[('yT', ys)])
                for half in range(2):
                    p2, p2k = bank()
                    for j in range(4):
                        K.op('pe', lambda e, p2=p2, j=j, ys=ys, nrow=nrow, half=half: e.matmul(out=p2[0:nrow, :], lhsT=yT[:, ys, j, 0:nrow],
                                                                                            rhs=wo_[:, j, half * 512:(half + 1) * 512], start=(j == 0), stop=(j == 3)),
                             r=[wok, ('yT', ys)], w=[p2k], inc=(j == 3))
                    resid_add(t, nrow, half, p2, p2k)

        for h in range(4):
            phaseB(h)

    def conv_layer(seg):
        norm_T(6)
        lreset()
        wc = la('cvtab', [128, 24]); selc = la('cvtab', [128, 8])
        for j in range(3):
            K.dma('sp', wc[:, j * 8:(j + 1) * 8], conv_w[j:j + 1, :].rearrange("o (c p) -> p (o c)", p=128), w=['cvtab'], allow_slow_non_contiguous=True)
        bg01 = la('bg01', [128, 8, 2]); cl = la('cl', [128, 8, 2]); bufs = la('cvbufs', [128, 8, 2, NS]); cins = la('cins', [128, 8, NS])
        halo = la('halo', [128, 16]); hst = la('hst', [128, 16]); afix = la('afix', [128, 8, 2]); ufix = la('ufix', [128, 8, 2], BF16)
        uT = big[:, 0:8 * TT].rearrange("p (a b) -> p a b", a=8)
        o_ = 8 * TT + (8 * TT) % 2
        cinb = big[:, o_:o_ + 2 * (GS + 2)].bitcast(F32)
        a_ = smal[:, 0:512]; b_ = smal[:, 512:1024]
        if seg == 0:
            K.op('dve', lambda e: e.memset(halo[:, :], 0.0), w=['halo'])
        else:
            K.dma('sp', halo[:, :], st_c[:, :], r=['st_c'], w=['halo'])
        hv = halo[:, :].rearrange("p (c j) -> p c j", j=2)
        for c in range(8 if XA.get('smp', True) else 0):
            for j in range(2):
                K.dma('sp', bufs[:, c, j, :], sconv[:, j, c * 128:(c + 1) * 128].rearrange("s p -> p s"), w=['cvbufs'], allow_slow_non_contiguous=True)

        def chunk(c):
            i = st['w']; st['w'] = (i + 1) % NW
            wv = wslot[i][:, 0:8 * 3 * 128].rearrange("p (k a f) -> p k a f", k=8, a=3)
            for a3 in range(3):
                K.dma('pool', wv[:, :, a3, :], conv_w_in[:, a3 * D + c * 128:a3 * D + (c + 1) * 128].rearrange("(k p) f -> p k f", p=128), w=[('w', i)])
            wkey = ('w', i)
            K.op('dve', lambda e: e.tensor_copy(out=cinb[:, 0:2], in_=hv[:, c, :]), r=['halo'], w=['cinb'])
            for (n, c0, ncol) in tokgroups():
                pp = banks(3)
                for a3 in range(3):
                    for kc in range(8):
                        K.op('pe', lambda e, p=pp[a3][0], kc=kc, a3=a3, c0=c0, ncol=ncol: e.matmul(
                            out=p[:, 0:ncol], lhsT=wv[:, kc, a3, :], rhs=hT[:, kc, c0:c0 + ncol], start=(kc == 0), stop=(kc == 7)),
                            r=[wkey, ('hT', n)], w=[pp[a3][1]], inc=(kc == 7))
                (pbg, kbg), (pcg, kcg), (pz, kz) = pp
                K.op('act', lambda e, pz=pz, ncol=ncol: e.activation(out=a_[:, 0:ncol], in_=pz[:, 0:ncol], func=AF.Copy), r=[kz], w=[('smal', 0)])
                if n < NG:
                    K.op('dve', lambda e, pcg=pcg, c0=c0, ncol=ncol: e.tensor_tensor(out=cinb[:, 2:2 + ncol], in0=pcg[:, 0:ncol], in1=a_[:, 0:ncol], op=ALU.mult),
                         r=[kcg, ('smal', 0)], w=['cinb'])
                    if n == 0:
                        K.op('act', lambda e, pbg=pbg: e.activation(out=bg01[:, c, :], in_=pbg[:, 0:2], func=AF.Copy), r=[kbg], w=['bg01'])
                    K.op('dve', lambda e, c0=c0, ncol=ncol: e.tensor_scalar(out=b_[:, 0:ncol], in0=cinb[:, 2:2 + ncol], scalar1=wc[:, 16 + c:17 + c], scalar2=None, op0=ALU.mult),
                         r=['cinb', 'cvtab'], w=[('smal', 1)])
                    for j in (1, 0):
                        K.op('dve', lambda e, c0=c0, ncol=ncol, j=j: e.scalar_tensor_tensor(out=b_[:, 0:ncol], in0=cinb[:, j:j + ncol], scalar=wc[:, j * 8 + c:j * 8 + c + 1],
                                                                                      in1=b_[:, 0:ncol], op0=ALU.mult, op1=ALU.add), r=['cinb', 'cvtab', ('smal', 1)], w=[('smal', 1)])
                else:
                    K.op('dve', lambda e, pcg=pcg: e.tensor_tensor(out=cins[:, c, :], in0=pcg[:, 0:NS], in1=a_[:, 0:NS], op=ALU.mult), r=[kcg, ('smal', 0)], w=['cins'])
                    K.op('dve', lambda e: e.tensor_scalar(out=b_[:, 0:NS], in0=cins[:, c, :], scalar1=wc[:, 16 + c:17 + c], scalar2=None, op0=ALU.mult),
                         r=['cins', 'cvtab'], w=[('smal', 1)])
                    for j in (1, 0):
                        K.op('dve', lambda e, j=j: e.scalar_tensor_tensor(out=b_[:, 0:NS], in0=bufs[:, c, j, :], scalar=wc[:, j * 8 + c:j * 8 + c + 1], in1=b_[:, 0:NS],
                                                                        op0=ALU.mult, op1=ALU.add), r=['cvbufs', 'cvtab', ('smal', 1)], w=[('smal', 1)])
                K.op('dve', lambda e, pbg=pbg, c0=c0, ncol=ncol: e.tensor_tensor(out=uT[:, c, c0:c0 + ncol], in0=pbg[:, 0:ncol], in1=b_[:, 0:ncol], op=ALU.mult),
                     r=[kbg, ('smal', 1)], w=[('aT', c // 2, n)])
                if n < NG:
                    if n == NG - 1:
                        K.op('dve', lambda e: e.tensor_copy(out=cl[:, c, :], in_=cinb[:, GS:GS + 2]), r=['cinb'], w=['cl'])
                    K.op('dve', lambda e: e.tensor_copy(out=cinb[:, 0:2], in_=cinb[:, GS:GS + 2]), r=['cinb'], w=['cinb'])

        for c in range(8):
            chunk(c)
        if seg == NSEG - 1:
            for c in range(8):
                K.dma('sp', o_convp[:, c * 128:(c + 1) * 128].rearrange("j p -> p j"), cl[:, c, :], r=['cl'], allow_slow_non_contiguous=True)
        else:
            K.dma('sp', st_c[:, :], cl[:, :, :].rearrange("p c j -> p (c j)"), r=['cl'], w=['st_c'])
        if seg == 0:
            K.dma('sp', o_convs[:, 0, :], sconv[:, 1, :])
            for c in range(8):
                K.dma('sp', o_convs[:, 1, c * 128:(c + 1) * 128].rearrange("s p -> p s"), cins[:, c, :], r=['cins'], allow_slow_non_contiguous=True)
        wvs = [wload(conv_w_out, kh * 512, 4, 0, D) for kh in range(2)]

        def outp(rows_fn, nrow, t, keys):
            for half in range(2):
                p, pk = bank()
                for k in range(8):
                    wv, wk = wvs[k // 4]
                    K.op('pe', lambda e, p=p, k=k, wv=wv, half=half: e.matmul(out=p[0:nrow, :], lhsT=rows_fn(k), rhs=wv[:, k % 4, half * 512:(half + 1) * 512],
                                                                             start=(k == 0), stop=(k == 7)), r=[wk] + keys, w=[pk], inc=(k == 7))
                resid_add(t, nrow, half, p, pk)
        for (t, c0, nrow) in toktiles():
            outp(lambda k, c0=c0, nrow=nrow: uT[:, k, c0:c0 + nrow], nrow, t, [('aT', hh, min(t // TPG, NG)) for hh in range(4)])

    def pool_layer(seg):
        sumsq_rstd([(x[:, t, :], ('x', t)) for t in range(NT + 1)], NT + 1, 'rstd')
        Am = tabs[:, 0:1536].rearrange("p (a b) -> p a b", a=12)
        K.dma('sp', tabs[:, 0:1536], c_poolA[:, :], w=['tabs'])
        lreset()
        pB = la('pltab', [128, 8 * NS]); pC = la('pltab', [NS, 4 * NS]); selc = la('pltab', [128, 8])
        K.dma('sp', pB[:], c_poolB[:, :], w=['pltab']); K.dma('sp', pC[:], c_poolC[:, :], w=['pltab']); pass
        dT = big[:, 0:8 * TT].rearrange("p (a b) -> p a b", a=8)
        f32slots = [scr[i][:, :].bitcast(F32) for i in range(4)]
        gb, gbk = f32slots[3], ('scr', 3)
        K.dma('sp', gb, norm_g[9:10, :].partition_broadcast(128).rearrange("p o n -> p (o n)"), w=[gbk])
        wg_, wgk = wload(pool_w, 0, 8, 0, 256)
        scb = junk[:, :]
        K.dma('pool', scb, pool_scale[0:1, :].partition_broadcast(128).rearrange("p o n -> p (o n)"), w=['junk'])
        for r8 in range(8):
            K.op('dve', lambda e, r8=r8: e.tensor_tensor(out=wg_[:, r8, :], in0=wg_[:, r8, :], in1=scb[:, (r8 // 2) * 256:(r8 // 2 + 1) * 256], op=ALU.mult),
                 r=[wgk, 'junk'], w=[wgk])

        def xnf(t, slot):
            K.op('dve', lambda e: e.tensor_scalar(out=f32slots[slot][:, :], in0=x[:, t, :], scalar1=rstd[:, t:t + 1], scalar2=None, op0=ALU.mult),
                 r=[('x', t), 'rstd'], w=[('scr', slot)])

        def pooled_tile(t, cur, prv, first):
            bk = banks(2)
            for c in range(8):
                gi = c // 2
                p = bk[c // 4][0]
                col = (c % 4) * 128
                K.op('pe', lambda e, p=p, c=c, gi=gi, col=col: e.matmul(out=p[:, col:col + 128], lhsT=f32slots[cur][:, c * 128:(c + 1) * 128],
                                                                       rhs=Am[:, gi * 3 + (2 if first else 0), :], start=True, stop=False),
                     r=[('scr', cur), 'tabs'], w=[bk[c // 4][1]], inc=False)
                K.op('pe', lambda e, p=p, c=c, gi=gi, col=col: e.matmul(out=p[:, col:col + 128], lhsT=f32slots[prv][:, c * 128:(c + 1) * 128],
                                                                       rhs=Am[:, gi * 3 + 1, :], start=False, stop=True),
                     r=[('scr', prv), 'tabs'], w=[bk[c // 4][1]], inc=True)
            for c in range(8):
                p = bk[c // 4][0]
                col = (c % 4) * 128
                K.op('act', lambda e, p=p, c=c, col=col: e.activation(out=dT[:, c, t * 128:(t + 1) * 128], in_=p[:, col:col + 128], func=AF.Copy,
                                                                     scale=gcols[:, 72 + c:73 + c]), r=[bk[c // 4][1], 'gcols'], w=[('aT', c // 2, t // TPG)])

        if XA.get('smp', True):
            K.op('dve', lambda e: e.tensor_scalar(out=f32slots[0][:, :], in0=x[:, NT, :], scalar1=rstd[:, NT:NT + 1], scalar2=None, op0=ALU.mult),
                 r=[('x', NT), 'rstd'], w=[('scr', 0)])
            K.op('dve', lambda e: e.tensor_tensor(out=f32slots[0][0:NS, :], in0=f32slots[0][0:NS, :], in1=gb[0:NS, :], op=ALU.mult), r=[('scr', 0), gbk], w=[('scr', 0)])
            if seg == 0:
                K.dma('sp', o_pools[:, 14, :], f32slots[0][0:NS, :], r=[('scr', 0)])
                K.dma('sp', o_pools[:, 0:14, :], spool[:, 1:15, :])
            nh = (NS + 7) // 8
            bk = banks(2)
            for hf in range(nh):
                ns_h = min(8, NS - hf * 8)
                K.dma('sp', f32slots[1 + hf][0:ns_h * 15, :], spool[hf * 8:hf * 8 + ns_h].rearrange("s j n -> (s j) n"), w=[('scr', 1 + hf)])
            for c in range(8):
                gi = c // 2
                p = bk[c // 4][0]
                col = (c % 4) * NS
                for hf in range(nh):
                    ns_h = min(8, NS - hf * 8)
                    K.op('pe', lambda e, p=p, c=c, gi=gi, col=col, hf=hf, ns_h=ns_h: e.matmul(
                        out=p[:, col:col + NS], lhsT=f32slots[1 + hf][0:ns_h * 15, c * 128:(c + 1) * 128], rhs=pB[0:ns_h * 15, (hf * 4 + gi) * NS:(hf * 4 + gi + 1) * NS],
                        start=(hf == 0), stop=False), r=[('scr', 1 + hf), 'pltab'], w=[bk[c // 4][1]], inc=False)
                K.op('pe', lambda e, p=p, c=c, gi=gi, col=col: e.matmul(out=p[:, col:col + NS], lhsT=f32slots[0][0:NS, c * 128:(c + 1) * 128], rhs=pC[:, gi * NS:(gi + 1) * NS],
                                                                       start=False, stop=True), r=[('scr', 0), 'pltab'], w=[bk[c // 4][1]], inc=True)
            for c in range(8):
                p = bk[c // 4][0]
                col = (c % 4) * NS
                K.op('act', lambda e, p=p, c=c, col=col: e.activation(out=dT[:, c, T:TT], in_=p[:, col:col + NS], func=AF.Copy), r=[bk[c // 4][1]], w=[('aT', c // 2, NG)])
        K.op('dve', lambda e: e.memset(f32slots[2][:, :], 0.0), w=[('scr', 2)])
        if seg > 0:
            K.dma('sp', f32slots[2][112:128, :], st_p[:, :], r=['st_p'], w=[('scr', 2)])
        for t in range(NT):
            xnf(t, t % 2)
            pooled_tile(t, t % 2, 2 if t == 0 else (t - 1) % 2, (t == 0 and seg == 0))
        last = (NT - 1) % 2
        if seg == NSEG - 1:
            K.op('dve', lambda e: e.tensor_tensor(out=f32slots[2][:, :], in0=f32slots[last][:, :], in1=gb, op=ALU.mult), r=[('scr', last), gbk], w=[('scr', 2)])
            K.dma('sp', o_poolp[:, :], f32slots[2][113:128, :], r=[('scr', 2)])
        else:
            K.dma('sp', st_p[:, :], f32slots[last][112:128, :], r=[('scr', last)], w=['st_p'])
        for (t, c0, nrow) in toktiles():
            for half in range(2):
                p, pk = bank()
                for g2 in range(2):
                    g = half * 2 + g2
                    for kc in range(2):
                        K.op('pe', lambda e, p=p, g=g, g2=g2, kc=kc, c0=c0, nrow=nrow: e.matmul(out=p[0:nrow, g2 * 256:(g2 + 1) * 256], lhsT=dT[:, g * 2 + kc, c0:c0 + nrow],
                                                                                         rhs=wg_[:, g * 2 + kc, :], start=(g2 == 0 and kc == 0), stop=(g2 == 1 and kc == 1),
                                                                                         skip_group_check=True),
                             r=[wgk] + [('aT', hh, min(t // TPG, NG)) for hh in range(4)], w=[pk], inc=(g2 == 1 and kc == 1))
                resid_add(t, nrow, half, p, pk)

    def final_norm(seg):
        sumsq_rstd([(x[:, t, :], ('x', t)) for t in range(NT + 1)], NT + 1, 'rstd')
        gb_, gbk = scratch()
        gb = gb_[:, :].bitcast(F32)
        K.dma('sp', gb, final_norm_g[0:1, :].partition_broadcast(128).rearrange("p o n -> p (o n)"), w=[gbk])
        for (t, c0, nrow) in toktiles():
            K.op('dve', lambda e, t=t, nrow=nrow: e.scalar_tensor_tensor(out=x[0:nrow, t, :], in0=x[0:nrow, t, :], scalar=rstd[0:nrow, t:t + 1], in1=gb[0:nrow, :],
                                                                       op0=ALU.mult, op1=ALU.mult), r=[('x', t), 'rstd', gbk], w=[('x', t)])
            dst = yp[seg * T + t * 128:seg * T + (t + 1) * 128, :] if t < NT else ys[:, :]
            if t < NT or seg == 0:
                K.dma('sp', dst, x[0:nrow, t, :], r=[('x', t)])

    def swa_layer(seg):
        norm_T(3)
        lreset()
        swcs = la('swtab', [128, NT + 1, 16]); msk = la('swtab', [128, 3, 128], BF16); selc = la('swtab', [128, 8])
        esk = la('esk', [128, 16]); ksv = la('ksv', [128, 512]); ktn = la('ktn', [128, 4, NS], BF16)
        swqT = la('swqT', [128, 2, 8, 128], BF16)
        K.op('dve', lambda e: e.memset(swqT[:, :, :, :], 0.0), w=['swqT'])
        K.dma('sp', swcs[:, 0:NT, :], c_swcs[:, seg * NT:(seg + 1) * NT, :], w=['swtab'])
        K.dma('sp', swcs[:, NT, :], c_swcs[:, NSEG * NT, :], w=['swtab'])
        K.dma('pool', msk[:, :, :].rearrange("p a b -> p (a b)"), c_mask[:, :], w=['swtab'])
        K.dma('sp', esk[:], swa_sinks[0:1, :].partition_broadcast(128).rearrange("p o n -> p (o n)"), w=['esk'])
        K.op('act', lambda e: e.activation(out=esk[:], in_=esk[:], func=AF.Exp), r=['esk'], w=['esk'])
        KW = T + 128
        KTd = big[:, 0:4 * KW].rearrange("p (a b) -> p a b", a=4)
        Vd = big[:, 4 * KW:4 * KW + (NT + 1) * 512].rearrange("p (t k u d) -> p t k u d", k=4, u=2, d=64)
        assert 4 * KW + (NT + 1) * 512 <= BIGE
        R5 = (ps[5], ('ps', 5)); R6 = (ps[6], ('ps', 6)); R7 = (ps[7], ('ps', 7))
        tmpv = smal[:, 512:1024]

        def rope_tm(buf, H, nrow, idx, bkey):
            b3 = buf.rearrange("p (h d) -> p h d", d=64)
            x1 = b3[0:nrow, :, 0:8]; x2 = b3[0:nrow, :, 8:16]
            cs = swcs[0:nrow, idx, 0:8].unsqueeze(1).broadcast_to([nrow, H, 8])
            sn = swcs[0:nrow, idx, 8:16].unsqueeze(1).broadcast_to([nrow, H, 8])
            tv = [tmpv[0:nrow, i * 128:i * 128 + H * 8].rearrange("p (h d) -> p h d", d=8) for i in range(4)]
            K.op('dve', lambda e: e.tensor_tensor(out=tv[0], in0=x1, in1=cs, op=ALU.mult), r=[bkey, 'swtab'], w=[('smal', 1)])
            K.op('dve', lambda e: e.tensor_tensor(out=tv[1], in0=x2, in1=sn, op=ALU.mult), r=[bkey, 'swtab'], w=[('smal', 1)])
            K.op('dve', lambda e: e.tensor_tensor(out=tv[2], in0=x2, in1=cs, op=ALU.mult), r=[bkey, 'swtab'], w=[('smal', 1)])
            K.op('dve', lambda e: e.tensor_tensor(out=tv[3], in0=x1, in1=sn, op=ALU.mult), r=[bkey, 'swtab'], w=[('smal', 1)])
            K.op('dve', lambda e: e.tensor_tensor(out=x1, in0=tv[0], in1=tv[1], op=ALU.subtract), r=[('smal', 1)], w=[bkey])
            K.op('dve', lambda e: e.tensor_tensor(out=x2, in0=tv[2], in1=tv[3], op=ALU.add), r=[('smal', 1)], w=[bkey])

        def kv_store(kvf, kvkey, nrow, slot, ktdst, ktkey='KTd'):
            k4 = kvf[0:nrow, 0:256].rearrange("p (k d) -> p k d", d=64)
            v4 = kvf[0:nrow, 256:512].rearrange("p (k d) -> p k d", d=64)
            kd = junk[0:nrow, 0:512].rearrange("p (k u d) -> p k u d", k=4, u=2)
            K.op('dve', lambda e: e.tensor_copy(out=kd, in_=k4.unsqueeze(2).broadcast_to([nrow, 4, 2, 64])), r=[kvkey], w=['junk'])
            if slot is not None:
                K.op('dve', lambda e: e.tensor_copy(out=Vd[0:nrow, slot, :, :, :], in_=v4.unsqueeze(2).broadcast_to([nrow, 4, 2, 64])), r=[kvkey], w=[('Vd', slot)])
            bk, bkk = bank()
            pb = bk[:, :].bitcast(BF16)
            for kh in range(4):
                K.op('pe', lambda e, kh=kh: e.transpose(out=pb[:, kh * 128:kh * 128 + nrow], in_=junk[0:nrow, kh * 128:(kh + 1) * 128], identity=ident[0:nrow, 0:nrow]),
                     r=['junk', 'ident'], w=[bkk], inc=(kh == 3))
            K.op('act', lambda e: e.activation(out=ktdst, in_=pb[:, 0:512].rearrange("p (a b) -> p a b", a=4)[:, :, 0:nrow], func=AF.Copy), r=[bkk], w=[ktkey])

        kvf = smal[:, 0:512]
        if seg == 0:
            K.op('dve', lambda e: e.memset(kvf, 0.0), w=[('smal', 0)])
        else:
            K.dma('sp', kvf, st_h[:, :], r=['st_h'], w=[('smal', 0)])
        kv_store(kvf, ('smal', 0), 128, 0, KTd[:, :, 0:128])
        wkv, wkvk = wload(swa_w_qkv, 0, 8, 1024, 512)
        for (t, c0, nrow) in toktiles():
            p, pk = bank()
            for kc in range(8):
                K.op('pe', lambda e, p=p, kc=kc, c0=c0, nrow=nrow: e.matmul(out=p[0:nrow, :], lhsT=hT[:, kc, c0:c0 + nrow], rhs=wkv[:, kc, :], start=(kc == 0), stop=(kc == 7)),
                     r=[wkvk, ('hT', min(t // TPG, NG))], w=[pk], inc=(kc == 7))
            K.op('act', lambda e, p=p, nrow=nrow: e.activation(out=kvf[0:nrow, :], in_=p[0:nrow, :], func=AF.Copy), r=[pk], w=[('smal', 0)])
            rope_tm(kvf[:, 0:256], 4, nrow, t, ('smal', 0))
            if t < NT:
                kv_store(kvf, ('smal', 0), 128, t + 1, KTd[:, :, 128 + t * 128:128 + (t + 1) * 128])
                if t == NT - 1:
                    if seg == NSEG - 1:
                        K.dma('sp', o_swkp[:, :], kvf[:, 0:256], r=[('smal', 0)])
                        K.dma('sp', o_swvp[:, :], kvf[:, 256:512], r=[('smal', 0)])
                    else:
                        K.dma('sp', st_h[:, :], kvf[:, :], r=[('smal', 0)], w=['st_h'])
            else:
                K.op('dve', lambda e: e.tensor_copy(out=ksv[0:NS, :], in_=kvf[0:NS, :]), r=[('smal', 0)], w=['ksv'])
                kv_store(kvf, ('smal', 0), NS, None, ktn[:, :, :], 'ktn')
                if seg == 0:
                    K.dma('sp', o_swks[:, 127, :], ksv[0:NS, 0:256], r=['ksv'])
                    K.dma('sp', o_swvs[:, 127, :], ksv[0:NS, 256:512], r=['ksv'])
                    K.dma('sp', o_swks[:, 0:127, :], swk[:, 1:128, :])
                    K.dma('sp', o_swvs[:, 0:127, :], swv[:, 1:128, :])

        wq = [wload(swa_w_qkv, 0, 8, hf * 512, 512) for hf in range(2)]

        def q_block(t, c0, nrow):
            pp = banks(2)
            for hf in range(2):
                for kc in range(8):
                    K.op('pe', lambda e, p=pp[hf][0], kc=kc, hf=hf: e.matmul(out=p[0:nrow, :], lhsT=hT[:, kc, c0:c0 + nrow], rhs=wq[hf][0][:, kc, :], start=(kc == 0), stop=(kc == 7)),
                         r=[wq[hf][1], ('hT', min(t // TPG, NG))], w=[pp[hf][1]], inc=(kc == 7))
            qf_, qfk = scratch()
            qf = qf_[:, :].bitcast(F32)
            for hf in range(2):
                K.op('act', lambda e, hf=hf: e.activation(out=qf[0:nrow, hf * 512:(hf + 1) * 512], in_=pp[hf][0][0:nrow, :], func=AF.Copy), r=[pp[hf][1]], w=[qfk])
            rope_tm(qf, 16, nrow, t, qfk)
            K.op('dve', lambda e: e.tensor_copy(out=junk[0:nrow, :], in_=qf[0:nrow, :]), r=[qfk], w=['junk'])
            bk, bkk = bank()
            pb = bk[:, :].bitcast(BF16)
            for c in range(8):
                K.op('pe', lambda e, c=c: e.transpose(out=pb[:, c * 128:c * 128 + nrow], in_=junk[0:nrow, c * 128:(c + 1) * 128], identity=ident[0:nrow, 0:nrow]),
                     r=['junk', 'ident'], w=[bkk], inc=(c == 7))
            qtk = 'swqT'
            qTt = swqT[:, :, :, :]
            K.op('act', lambda e: e.activation(out=swqT[0:64, 0, :, 0:nrow], in_=pb[0:64, :].rearrange("p (a b) -> p a b", a=8)[:, :, 0:nrow], func=AF.Copy), r=[bkk], w=[qtk])
            K.op('act', lambda e: e.activation(out=swqT[64:128, 1, :, 0:nrow], in_=pb[64:128, :].rearrange("p (a b) -> p a b", a=8)[:, :, 0:nrow], func=AF.Copy), r=[bkk], w=[qtk])
            return qTt, qtk, qf, qfk

        def attend_block(t, qTt, qtk):
            def per_kh(kh):
                (po, pok), (pp_, ppk) = banks(2)
                for g in range(4):
                    ch = kh * 2 + g // 2; base = (g % 2) * 64
                    K.op('pe', lambda e, g=g, ch=ch, base=base, po=po: e.matmul(out=po[:, g * 128:(g + 1) * 128], lhsT=KTd[:, kh, 128 + t * 128:256 + t * 128],
                                                                               rhs=qTt[:, g % 2, ch, :], start=True, stop=True), r=['KTd', qtk], w=[pok], inc=(g == 3))
                for g in range(4):
                    ch = kh * 2 + g // 2; base = (g % 2) * 64
                    K.op('pe', lambda e, g=g, ch=ch, base=base, pp_=pp_: e.matmul(out=pp_[:, g * 128:(g + 1) * 128], lhsT=KTd[:, kh, t * 128:128 + t * 128],
                                                                                 rhs=qTt[:, g % 2, ch, :], start=True, stop=True), r=['KTd', qtk], w=[ppk], inc=(g == 3))
                E_, Ek = scratch()
                Eo = E_[:, 0:512]; Ep = E_[:, 512:1024]
                K.op('act', lambda e, po=po: e.activation(out=Eo, in_=po[:, :], func=AF.Exp, scale=0.125), r=[pok], w=[Ek])
                K.op('act', lambda e, pp_=pp_: e.activation(out=Ep, in_=pp_[:, :], func=AF.Exp, scale=0.125), r=[ppk], w=[Ek])
                mi = 2 if (t == 0 and seg == 0) else 1
                K.op('dve', lambda e: e.tensor_tensor(out=Eo.rearrange("p (g l) -> p g l", g=4), in0=Eo.rearrange("p (g l) -> p g l", g=4),
                                                      in1=msk[:, 0, :].unsqueeze(1).broadcast_to([128, 4, 128]), op=ALU.mult), r=[Ek, 'swtab'], w=[Ek])
                K.op('dve', lambda e: e.tensor_tensor(out=Ep.rearrange("p (g l) -> p g l", g=4), in0=Ep.rearrange("p (g l) -> p g l", g=4),
                                                      in1=msk[:, mi, :].unsqueeze(1).broadcast_to([128, 4, 128]), op=ALU.mult), r=[Ek, 'swtab'], w=[Ek])
                (pd_, pdk), (pv_, pvk) = banks(2)
                K.op('pe', lambda e, pd_=pd_: e.matmul(out=pd_[:, :], lhsT=ones[:, :], rhs=Eo, start=True, stop=False), r=['ones', Ek], w=[pdk], inc=False)
                K.op('pe', lambda e, pd_=pd_: e.matmul(out=pd_[:, :], lhsT=ones[:, :], rhs=Ep, start=False, stop=True), r=['ones', Ek], w=[pdk], inc=True)
                K.op('pe', lambda e, pv_=pv_: e.matmul(out=pv_[:, :], lhsT=Vd[:, t + 1, kh, :, :].rearrange("p u d -> p (u d)"), rhs=Eo, start=True, stop=False),
                     r=[('Vd', t + 1), Ek], w=[pvk], inc=False)
                K.op('pe', lambda e, pv_=pv_: e.matmul(out=pv_[:, :], lhsT=Vd[:, t, kh, :, :].rearrange("p u d -> p (u d)"), rhs=Ep, start=False, stop=True),
                     r=[('Vd', t), Ek], w=[pvk], inc=True)
                rd = smal[:, 0:512]
                K.op('dve', lambda e, pd_=pd_: e.tensor_tensor(out=rd.rearrange("p (g l) -> p g l", g=4), in0=pd_[:, :].rearrange("p (g l) -> p g l", g=4),
                                                              in1=esk[:, kh * 4:(kh + 1) * 4].unsqueeze(2).broadcast_to([128, 4, 128]), op=ALU.add), r=[pdk, 'esk'], w=[('smal', 0)])
                K.op('dve', lambda e: e.reciprocal(out=rd, in_=rd), r=[('smal', 0)], w=[('smal', 0)])
                for g in range(4):
                    ch = kh * 2 + g // 2; base = (g % 2) * 64
                    K.op('dve', lambda e, g=g, ch=ch, base=base, pv_=pv_: e.tensor_tensor(out=hT[base:base + 64, ch, t * 128:(t + 1) * 128], in0=pv_[base:base + 64, g * 128:(g + 1) * 128],
                                                                                         in1=rd[base:base + 64, g * 128:(g + 1) * 128], op=ALU.mult),
                         r=[pvk, ('smal', 0)], w=[('hT', t // TPG)])
            for kh in range(4):
                per_kh(kh)

        def do_block(t):
            qTt, qtk, _, _ = q_block(t, t * 128, 128)
            attend_block(t, qTt, qtk)

        for t in range(NT):
            do_block(t)

        if XA.get('smp', True):
            qTs, qsk, qf, qfk = q_block(NT, T, NS)
            prod_, prk = scratch()
            prod = prod_[:, :].bitcast(F32)
            K.op('dve', lambda e: e.tensor_tensor(out=prod[0:NS, :].rearrange("p (k g d) -> p k g d", k=4, g=4), in0=qf[0:NS, :].rearrange("p (k g d) -> p k g d", k=4, g=4),
                                                  in1=ksv[0:NS, 0:256].rearrange("p (k d) -> p k d", d=64).unsqueeze(2).broadcast_to([NS, 4, 4, 64]), op=ALU.mult),
                 r=[qfk, 'ksv'], w=[prk])
            snew = tiny[0:NS, 16:32]
            K.op('dve', lambda e: e.tensor_reduce(out=snew, in_=prod[0:NS, :].rearrange("p (h d) -> p h d", d=64), axis=AX.X, op=ALU.add), r=[prk], w=['snew'])
            K.op('act', lambda e: e.activation(out=snew, in_=snew, func=AF.Exp, scale=0.125), r=['snew'], w=['snew'])
            for s_ in range(NS):
                cb_, cbk = scratch()
                kb = cb_[:, 0:256]; vb = cb_[:, 256:512]
                K.dma('pool', kb, swk[s_], w=[cbk]); K.dma('pool', vb, swv[s_], w=[cbk])
                kd = cb_[:, 512:1024].rearrange("p (k u d) -> p k u d", k=4, u=2)
                K.op('dve', lambda e, kb=kb, kd=kd: e.tensor_copy(out=kd, in_=kb.rearrange("p (k d) -> p k d", d=64).unsqueeze(2).broadcast_to([128, 4, 2, 64])), r=[cbk], w=[cbk])
                bk, bkk = bank()
                pb = bk[:, :].bitcast(BF16)
                for kh in range(4):
                    K.op('pe', lambda e, kh=kh, cb_=cb_, pb=pb: e.transpose(out=pb[:, kh * 128:(kh + 1) * 128], in_=cb_[:, 512 + kh * 128:512 + (kh + 1) * 128], identity=ident[:]),
                         r=[cbk, 'ident'], w=[bkk], inc=(kh == 3))
                KTs = cb_[:, 1024:1536].rearrange("p (a b) -> p a b", a=4)
                K.op('act', lambda e, KTs=KTs, pb=pb: e.activation(out=KTs, in_=pb[:, 0:512].rearrange("p (a b) -> p a b", a=4), func=AF.Copy), r=[bkk], w=[cbk])
                sc_, sck = bank()
                for hq in range(16):
                    kh = hq // 4; g = hq % 4; ch = kh * 2 + g // 2; base = (g % 2) * 64
                    K.op('pe', lambda e, hq=hq, kh=kh, ch=ch, base=base, KTs=KTs, sc_=sc_, s_=s_: e.matmul(out=sc_[:, hq:hq + 1], lhsT=KTs[:, kh, :], rhs=qTs[:, (hq % 4) % 2, ch, s_:s_ + 1],
                                                                                                    start=True, stop=True), r=[cbk, qsk], w=[sck], inc=(hq == 15))
                Es = cb_[:, 1536:1552]
                K.op('act', lambda e, Es=Es, sc_=sc_: e.activation(out=Es, in_=sc_[:, 0:16], func=AF.Exp, scale=0.125), r=[sck], w=[cbk])
                Em = cb_[:, 1552:1552 + 16 * NS].rearrange("p (h s) -> p h s", h=16)
                K.op('dve', lambda e, Es=Es, Em=Em, s_=s_: e.tensor_tensor(out=Em, in0=Es.unsqueeze(2).broadcast_to([128, 16, NS]), in1=idns[:, s_, :].unsqueeze(1).broadcast_to([128, 16, NS]), op=ALU.mult),
                     r=[cbk, 'idns'], w=[cbk])
                for hq in range(16):
                    kh = hq // 4
                    Rb = R6 if hq < 8 else R7
                    K.op('pe', lambda e, hq=hq, kh=kh, Em=Em, vb=vb, Rb=Rb, s_=s_: e.matmul(out=Rb[0][0:NS, (hq % 8) * 64:(hq % 8 + 1) * 64], lhsT=Em[:, hq, :], rhs=vb[:, kh * 64:(kh + 1) * 64],
                                                                                       start=(s_ == 0 and hq % 8 == 0), stop=(s_ == NS - 1 and hq % 8 == 7), skip_group_check=True),
                         r=[cbk], w=[Rb[1]], inc=(hq % 8 == 7))
                for hq in range(16):
                    K.op('pe', lambda e, hq=hq, Em=Em, s_=s_: e.matmul(out=R5[0][0:NS, hq:hq + 1], lhsT=Em[:, hq, :], rhs=ones[:, 0:1], start=(s_ == 0 and hq == 0),
                                                                   stop=(s_ == NS - 1 and hq == 15), skip_group_check=True), r=[cbk, 'ones'], w=[R5[1]], inc=(hq == 15))
            den = tiny[0:NS, 0:16]
            K.op('dve', lambda e: e.tensor_tensor(out=den, in0=R5[0][0:NS, 0:16], in1=snew, op=ALU.add), r=[R5[1], 'snew'], w=['mx'])
            K.op('dve', lambda e: e.tensor_tensor(out=den, in0=den, in1=esk[0:NS, :], op=ALU.add), r=['mx', 'esk'], w=['mx'])
            K.op('dve', lambda e: e.reciprocal(out=den, in_=den), r=['mx'], w=['mx'])
            num = prod[0:NS, :].rearrange("p (k g d) -> p k g d", k=4, g=4)
            K.op('dve', lambda e: e.tensor_tensor(out=num, in0=ksv[0:NS, 256:512].rearrange("p (k d) -> p k d", d=64).unsqueeze(2).broadcast_to([NS, 4, 4, 64]),
                                                  in1=snew.rearrange("p (k g) -> p k g", g=4).unsqueeze(3).broadcast_to([NS, 4, 4, 64]), op=ALU.mult), r=['ksv', 'snew', prk], w=[prk])
            for b2, Rb in enumerate((R6, R7)):
                K.op('dve', lambda e, b2=b2, Rb=Rb: e.tensor_tensor(out=prod[0:NS, b2 * 512:(b2 + 1) * 512], in0=Rb[0][0:NS, :], in1=prod[0:NS, b2 * 512:(b2 + 1) * 512], op=ALU.add),
                     r=[Rb[1], prk], w=[prk])
            K.op('dve', lambda e: e.tensor_tensor(out=junk[0:NS, :].rearrange("p (h d) -> p h d", d=64), in0=prod[0:NS, :].rearrange("p (h d) -> p h d", d=64),
                                                  in1=den.unsqueeze(2).broadcast_to([NS, 16, 64]), op=ALU.mult), r=[prk, 'mx'], w=['junk'])
            bko = bank()
            pbo = bko[0][:, :].bitcast(BF16)
            for c in range(8):
                K.op('pe', lambda e, c=c: e.transpose(out=pbo[:, c * NS:(c + 1) * NS], in_=junk[0:NS, c * 128:(c + 1) * 128], identity=ident[0:NS, 0:NS]),
                     r=['junk', 'ident'], w=[bko[1]], inc=(c == 7))
            K.op('dve', lambda e: e.tensor_copy(out=hT[:, :, T:TT], in_=pbo[:, 0:8 * NS].rearrange("p (a b) -> p a b", a=8)), r=[bko[1]], w=[('hT', NG)])

        wvs = [wload(swa_w_o, kh2 * 512, 4, 0, D) for kh2 in range(2)]
        for (t, c0, nrow) in toktiles():
            for half in range(2):
                p, pk = bank()
                for k in range(8):
                    wv, wk = wvs[k // 4]
                    K.op('pe', lambda e, p=p, k=k, wv=wv, c0=c0, nrow=nrow, half=half: e.matmul(out=p[0:nrow, :], lhsT=hT[:, k, c0:c0 + nrow], rhs=wv[:, k % 4, half * 512:(half + 1) * 512],
                                                                                             start=(k == 0), stop=(k == 7)), r=[wk, ('hT', min(t // TPG, NG))], w=[pk], inc=(k == 7))
                resid_add(t, nrow, half, p, pk)

    if stage in (1, 2):
        load_x(0) if False else None
        xattn(0)
    if stage == 2:
        ffn(0)
    XA['seg'] = 0
    if stage < 10:
        load_x(0)
        if stage == 3:
            retention(0)
        if stage == 4:
            conv_layer(0)
        if stage == 5:
            pool_layer(0)
        if stage == 6:
            swa_layer(0)
    if stage >= 10:
        for seg in range(NSEG):
            XA['seg'] = seg
            XA['smp'] = (seg == 0)
            load_x(seg)
            retention(seg); xattn(0); ffn(0)
            swa_layer(seg); xattn(1); ffn(1)
            conv_layer(seg); xattn(2); ffn(2)
            pool_layer(seg); xattn(3); ffn(3)
            final_norm(seg)

    if stage < 10:
        for t in range(NT):
            K.dma('sp', dbg[t * 128:(t + 1) * 128, :], x[:, t, :], r=[('x', t)])
        K.dma('sp', dbg[T:TT, :], x[0:NS, NT, :], r=[('x', NT)])

    K.finish(['sp'])
    with nc.Block() as block:
        @block.tensor
        def _(e):
            K.emit('pe', e)

        @block.scalar
        def _(e):
            K.emit('act', e)

        @block.vector
        def _(e):
            K.emit('dve', e)

        @block.gpsimd
        def _(e):
            K.emit('pool', e)

        @block.sync
        def _(e):
            K.emit('sp', e)
    return nc


def _tables(c, T, NS):
    f32 = np.float32
    NT = T // 128
    b, s = divmod(c, 4)
    NSEG = 4
    pos = np.arange(NSEG * T).astype(f32)
    tb = {}
    tb["c_ident"] = np.eye(128, dtype=f32)
    inv = (1.0 / (f32(10000.0) ** (np.arange(128, dtype=f32) / f32(128)))).astype(f32)
    ang = (pos[None, :] * inv[:, None]).astype(f32)
    tb["c_rcosT"] = np.cos(ang).astype(f32)
    tb["c_rsinT"] = np.sin(ang).astype(f32)
    angs = (f32(PAST) * inv).astype(f32)
    tb["c_rsT"] = np.stack([np.cos(angs), np.sin(angs)], 1).astype(f32)
    logg = np.log1p(-np.exp2(-5.0 - np.arange(4, dtype=f32))).astype(f32)
    l = np.arange(128, dtype=f32)
    dec = np.zeros((128, 4, 128), f32)
    for h in range(4):
        diff = l[None, :] - l[:, None]
        dec[:, h, :] = np.where(diff >= 0, np.exp(logg[h] * np.maximum(diff, 0)), 0.0)
    tb["c_dec"] = (dec.reshape(128, 512) / 16.0).astype(f32)
    gq = np.stack([np.exp(logg[h] * (l + 1.0)) for h in range(4)], 0).astype(f32)
    tb["c_gq"] = np.tile(gq.reshape(1, 512), (128, 1)).astype(f32)
    gk = np.zeros((128, 12), f32)
    gkA = np.zeros((128, 4, NT), f32)
    for h in range(4):
        gk[:, h] = np.exp(logg[h] * (127.0 - l)) / 16.0
        gk[:, 4 + h] = np.exp(logg[h] * 128.0)
        gk[:, 8 + h] = np.exp(logg[h])
        for t in range(NT):
            gkA[:, h, t] = np.exp(logg[h] * (127.0 - l + 128.0 * (NT - 1 - t))) / 16.0
    tb["c_gk"] = gk
    tb["c_gkA"] = gkA.reshape(128, 4 * NT)
    tb["c_id16"] = (np.eye(NS, dtype=f32) / 16.0).astype(f32)
    coef = np.zeros((8, 4), f32)
    for c2 in range(8):
        if c2 // 4 == b and c2 < c:
            coef[c2] = np.exp(logg * f32(T * (c - c2 - 1)))
    tb["c_coef"] = np.tile(coef.reshape(1, 32), (128, 1)).astype(f32)
    sel = np.zeros((8,), f32)
    if s > 0:
        sel[c - 1] = 1.0
    tb["c_sel"] = np.tile(sel[None, :], (128, 1)).astype(f32)
    inv8 = (1.0 / (f32(500000.0) ** (np.arange(8, dtype=f32) / f32(8)))).astype(f32)
    a8 = (pos[:, None] * inv8[None, :]).astype(f32)
    cs = np.concatenate([np.cos(a8), np.sin(a8)], 1).astype(f32)
    sw = np.zeros((128, NSEG * NT + 1, 16), f32)
    sw[:, :NSEG * NT, :] = cs.reshape(NSEG * NT, 128, 16).transpose(1, 0, 2)
    a8s = (f32(PAST) * inv8).astype(f32)
    sw[:, NSEG * NT, :] = np.concatenate([np.cos(a8s), np.sin(a8s)])[None, :]
    tb["c_swcs"] = sw
    kk = np.arange(128)[:, None]; ll = np.arange(128)[None, :]
    m_own = (ll >= kk).astype(f32); m_prev = (ll <= kk).astype(f32)
    m_first = np.zeros_like(m_prev)
    tb["c_mask"] = np.concatenate([m_own, m_prev, m_first], 1).astype(f32)
    pa = np.zeros((128, 12, 128), f32)
    for gi, w in enumerate((2, 4, 8, 16)):
        own = np.zeros((128, 128), f32); prev = np.zeros((128, 128), f32); first = np.zeros((128, 128), f32)
        for lq in range(128):
            for i in range(w):
                lp = lq - i
                if lp >= 0:
                    own[lp, lq] += 1.0 / w
                else:
                    prev[128 + lp, lq] += 1.0 / w
            cnt = min(lq + 1, w)
            for i in range(cnt):
                first[lq - i, lq] += 1.0 / cnt
        eye = np.eye(128, dtype=f32)
        pa[:, gi * 3 + 0, :] = own - eye
        pa[:, gi * 3 + 1, :] = prev
        pa[:, gi * 3 + 2, :] = first - eye
    tb["c_poolA"] = pa.reshape(128, 12 * 128)
    pbm = np.zeros((128, 2, 4, NS), f32)
    per = 8
    for sp_ in range(NS):
        hf, r = divmod(sp_, per)
        for j in range(15):
            for gi, w in enumerate((2, 4, 8, 16)):
                if j >= 15 - (w - 1):
                    pbm[r * 15 + j, hf, gi, sp_] = 1.0 / w
    tb["c_poolB"] = pbm.reshape(128, 2 * 4 * NS)
    tb["c_poolC"] = np.concatenate([np.eye(NS, dtype=f32) * (1.0 / w - 1.0) for w in (2, 4, 8, 16)], 1).astype(f32)
    tb["c_idns"] = np.tile(np.eye(NS, dtype=f32).reshape(1, NS * NS), (128, 1))
    er = np.zeros((NS, NS, 128), f32)
    for i in range(NS):
        er[i, i, :] = 1.0
    tb["c_erow"] = er.reshape(NS, NS * 128)
    return tb


def prep(inputs, T, NS):
    f32 = np.float32
    g = lambda k: np.asarray(inputs[k], dtype=f32)
    shared = {
        "norm_g": g("norm_g").reshape(12, D), "mem_norm_g": g("mem_norm_g"), "final_norm_g": g("final_norm_g").reshape(1, D),
        "ret_w_in": g("ret_w_in")[0], "ret_w_out": g("ret_w_out")[0], "swa_w_qkv": g("swa_w_qkv")[0], "swa_w_o": g("swa_w_o")[0],
        "swa_sinks": g("swa_sinks").reshape(1, 16), "conv_w_in": g("conv_w_in")[0], "conv_w": g("conv_w")[0], "conv_w_out": g("conv_w_out")[0],
        "pool_w": g("pool_w")[0].reshape(D, 256), "pool_scale": g("pool_scale").reshape(1, D),
        "xattn_w_q": g("xattn_w_q"), "xattn_w_kv": g("xattn_w_kv"), "xattn_w_o": g("xattn_w_o"),
        "ffn_w_in": g("ffn_w_in"), "ffn_w_out": g("ffn_w_out"),
    }
    xp = g("x_prompt"); xs = g("x_sample")
    maps = []
    for c in range(NCORES):
        b, s = divmod(c, 4)
        sl = slice(c * NS, (c + 1) * NS)
        m = dict(shared)
        m["xp"] = np.ascontiguousarray(xp[b])
        m["xs"] = np.ascontiguousarray(xs[sl, 0])
        m["sret"] = np.ascontiguousarray(g("state_ret")[0, sl])
        m["swk"] = np.ascontiguousarray(g("cache_swa_k")[0, sl].reshape(NS, 128, 256))
        m["swv"] = np.ascontiguousarray(g("cache_swa_v")[0, sl].reshape(NS, 128, 256))
        m["sconv"] = np.ascontiguousarray(g("state_conv")[0, sl])
        m["spool"] = np.ascontiguousarray(g("state_pool")[0, sl])
        m["cmk"] = np.ascontiguousarray(g("cache_mem_k")[:, sl].reshape(4, NS, NMEM, D))
        m["cmv"] = np.ascontiguousarray(g("cache_mem_v")[:, sl].reshape(4, NS, NMEM, D))
        m["memp"] = np.ascontiguousarray(g("mem_prompt")[b])
        m.update(_tables(c, T, NS))
        maps.append(m)
    return maps


_NC_CACHE = {}


def kernel(**inputs):
    T = inputs["x_prompt"].shape[1] // 4
    NS = inputs["x_sample"].shape[0] // NCORES
    key = (T, NS)
    if key not in _NC_CACHE:
        _NC_CACHE[key] = build(T, NS)
    nc = _NC_CACHE[key]
    maps = prep(inputs, T, NS)
    maps = [{k: v for k, v in m.items() if k in nc.used_inputs} for m in maps]
    res = run_bass_kernel_spmd(nc, maps, core_ids=list(range(NCORES))).results
    return assemble(res, T, NS)


def assemble(res, T, NS):
    B = 2
    cat = lambda name: np.concatenate([res[c][name] for c in range(NCORES)], 0)
    y_prompt = np.stack([res[c]["yp"] for c in (0, 4)]).reshape(B, 4 * T, D)
    y_sample = cat("ys").reshape(NCORES * NS, 1, D)
    last = [0, 4]
    ret_p = np.stack([res[c]["o_retp"] for c in last])[None]
    ret_s = cat("o_rets")[None]
    swk_p = np.stack([res[c]["o_swkp"].reshape(128, 4, 64) for c in last])[None]
    swv_p = np.stack([res[c]["o_swvp"].reshape(128, 4, 64) for c in last])[None]
    swk_s = cat("o_swks").reshape(1, NCORES * NS, 128, 4, 64)
    swv_s = cat("o_swvs").reshape(1, NCORES * NS, 128, 4, 64)
    conv_p = np.stack([res[c]["o_convp"] for c in last])[None]
    conv_s = cat("o_convs")[None]
    pool_p = np.stack([res[c]["o_poolp"] for c in last])[None]
    pool_s = cat("o_pools")[None]
    memk = np.stack([res[c]["o_memk"] for c in (0, 4)], 1).reshape(4, B, NMEM, 4, 256)
    memv = np.stack([res[c]["o_memv"] for c in (0, 4)], 1).reshape(4, B, NMEM, 4, 256)
    outs = (y_prompt, y_sample, ret_p, ret_s, swk_p, swv_p, swk_s, swv_s, conv_p, conv_s, pool_p, pool_s, memk, memv)
    return tuple(np.ascontiguousarray(o, dtype=np.float32) for o in outs)
```
